# Optimizing a Trainium2 kernel written in Bass

```python
import math
import jax, jax.numpy as jnp
from jax import lax
import numpy as np

D_MODEL = 2048
BATCH = 4
SEQ = 4096
DEPTH = 1

NORM_EPS = 1e-6
SSM_EXPAND = 2
SSM_D_INNER = SSM_EXPAND * D_MODEL
SSM_HEAD_DIM = 64
SSM_N_HEADS = SSM_D_INNER // SSM_HEAD_DIM
SSM_N_GROUPS = 8
SSM_HEADS_PER_GROUP = SSM_N_HEADS // SSM_N_GROUPS
SSM_D_STATE = 128
SSM_CONV = 4
SSM_CHUNK = 64
SSM_XBC_DIM = SSM_D_INNER + 2 * SSM_N_GROUPS * SSM_D_STATE
GDN_HEAD_DIM = 128
GDN_N_QK_HEADS = D_MODEL // GDN_HEAD_DIM
GDN_N_V_HEADS = 2 * GDN_N_QK_HEADS
GDN_V_PER_QK = GDN_N_V_HEADS // GDN_N_QK_HEADS
GDN_QK_DIM = GDN_N_QK_HEADS * GDN_HEAD_DIM
GDN_V_DIM = GDN_N_V_HEADS * GDN_HEAD_DIM
GDN_QKV_DIM = 2 * GDN_QK_DIM + GDN_V_DIM
GDN_CONV = 4
GDN_CHUNK = 64
FFN_HIDDEN = ((8 * D_MODEL + 3 * 256 - 1) // (3 * 256)) * 256
DT_MIN = 1e-3
DT_MAX = 1e-1
IN_SIZES = (SSM_D_INNER, SSM_XBC_DIM, SSM_N_HEADS, GDN_QKV_DIM, GDN_V_DIM, GDN_N_V_HEADS, GDN_N_V_HEADS, D_MODEL, D_MODEL)
IN_DIM = sum(IN_SIZES)
IN_SPLITS = [int(v) for v in np.cumsum(IN_SIZES)[:-1]]

kernel_name = 'hybrid_ssd_gdn_adaln_block'


def rms_norm(x, w, eps=NORM_EPS):
    xf = x.astype(jnp.float32)
    y = xf * lax.rsqrt(jnp.mean(xf * xf, axis=-1, keepdims=True) + eps)
    return (y * w.astype(jnp.float32)).astype(x.dtype)


def l2_normalize(x, eps=1e-6):
    return x * lax.rsqrt(jnp.sum(x * x, axis=-1, keepdims=True) + eps)


def modulate(h, shift, scale):
    return h * (1.0 + scale[:, None, :]) + shift[:, None, :]


def causal_depthwise_conv(x, w, b=None):
    k, ch = w.shape
    y = lax.conv_general_dilated(x, w[:, None, :].astype(x.dtype), window_strides=(1,), padding=[(k - 1, 0)], dimension_numbers=('NWC', 'WIO', 'NWC'), feature_group_count=ch)
    if b is not None:
        y = y + b.astype(x.dtype)
    return y


def mamba2_ssd_mixer(z, xbc, dt_raw, conv_w, conv_b, dt_bias, a_log, d_skip, norm_w):
    f32 = jnp.float32
    bsz, seq, _ = z.shape
    L, G, J, P, N = SSM_CHUNK, SSM_N_GROUPS, SSM_HEADS_PER_GROUP, SSM_HEAD_DIM, SSM_D_STATE
    nc = seq // L
    xbc = jax.nn.silu(causal_depthwise_conv(xbc, conv_w, conv_b))
    xs, b_in, c_in = jnp.split(xbc, [SSM_D_INNER, SSM_D_INNER + G * N], axis=-1)
    x = xs.astype(f32).reshape(bsz, nc, L, G, J, P)
    bm = b_in.astype(f32).reshape(bsz, nc, L, G, N)
    cm = c_in.astype(f32).reshape(bsz, nc, L, G, N)
    dt = jax.nn.softplus(dt_raw.astype(f32) + dt_bias.astype(f32)).reshape(bsz, nc, L, G, J)
    a = -jnp.exp(a_log.astype(f32)).reshape(G, J)
    xdt = x * dt[..., None]
    a_cum = jnp.cumsum(dt * a, axis=2).transpose(0, 1, 3, 4, 2)
    causal = jnp.tril(jnp.ones((L, L), dtype=bool))
    seg = a_cum[..., :, None] - a_cum[..., None, :]
    decay = jnp.where(causal, jnp.exp(jnp.where(causal, seg, 0.0)), 0.0)
    cb = jnp.einsum('bclgn,bcsgn->bcgls', cm, bm)
    y_diag = jnp.einsum('bcgjls,bcsgjp->bclgjp', cb[:, :, :, None] * decay, xdt)
    to_end = jnp.exp(a_cum[..., -1:] - a_cum)
    xw = xdt * to_end.transpose(0, 1, 4, 2, 3)[..., None]

    def step(state, inp):
        c_k, b_k, xw_k, acum_k = inp
        y_off = jnp.einsum('blgn,bgjpn->blgjp', c_k, state) * jnp.exp(acum_k).transpose(0, 3, 1, 2)[..., None]
        state = state * jnp.exp(acum_k[..., -1])[..., None, None] + jnp.einsum('blgn,blgjp->bgjpn', b_k, xw_k)
        return state, y_off

    state0 = jnp.zeros((bsz, G, J, P, N), f32)
    _, y_off = lax.scan(step, state0, (cm.swapaxes(0, 1), bm.swapaxes(0, 1), xw.swapaxes(0, 1), a_cum.swapaxes(0, 1)))
    y = y_diag + y_off.swapaxes(0, 1) + d_skip.astype(f32).reshape(G, J, 1) * x
    y = y.reshape(bsz, seq, SSM_D_INNER) * jax.nn.silu(z.astype(f32))
    y = rms_norm(y.reshape(bsz, seq, G, SSM_D_INNER // G), norm_w.reshape(G, SSM_D_INNER // G))
    return y.reshape(bsz, seq, SSM_D_INNER).astype(z.dtype)


def gated_deltanet_mixer(qkv, z, beta_raw, a_raw, conv_w, a_log, dt_bias, norm_w):
    f32 = jnp.float32
    bsz, seq, _ = qkv.shape
    L, H, Dk, Dv = GDN_CHUNK, GDN_N_V_HEADS, GDN_HEAD_DIM, GDN_HEAD_DIM
    nc = seq // L
    qkv = jax.nn.silu(causal_depthwise_conv(qkv, conv_w))
    q, k, v = jnp.split(qkv.astype(f32), [GDN_QK_DIM, 2 * GDN_QK_DIM], axis=-1)
    q = l2_normalize(q.reshape(bsz, seq, GDN_N_QK_HEADS, Dk)) * (Dk ** -0.5)
    k = l2_normalize(k.reshape(bsz, seq, GDN_N_QK_HEADS, Dk))
    q = jnp.repeat(q, GDN_V_PER_QK, axis=2)
    k = jnp.repeat(k, GDN_V_PER_QK, axis=2)
    v = v.reshape(bsz, seq, H, Dv)
    beta = jax.nn.sigmoid(beta_raw.astype(f32))
    g = -jnp.exp(a_log.astype(f32)) * jax.nn.softplus(a_raw.astype(f32) + dt_bias.astype(f32))

    def chunks(t):
        return t.reshape(bsz, nc, L, H, -1).transpose(0, 3, 1, 2, 4)

    qc, kc, vc = chunks(q), chunks(k), chunks(v)
    bc = beta.reshape(bsz, nc, L, H).transpose(0, 3, 1, 2)
    gam = jnp.cumsum(g.reshape(bsz, nc, L, H).transpose(0, 3, 1, 2), axis=-1)
    incl = jnp.tril(jnp.ones((L, L), dtype=bool))
    strict = jnp.tril(jnp.ones((L, L), dtype=bool), k=-1)
    seg = gam[..., :, None] - gam[..., None, :]
    dec = jnp.where(incl, jnp.exp(jnp.where(incl, seg, 0.0)), 0.0)
    a_mat = jnp.where(strict, jnp.einsum('bhcld,bhcsd->bhcls', kc, kc) * dec * bc[..., None], 0.0)
    rhs = jnp.concatenate([vc * bc[..., None], kc * (bc * jnp.exp(gam))[..., None]], axis=-1)
    sol = lax.linalg.triangular_solve(a_mat + jnp.eye(L, dtype=f32), rhs, left_side=True, lower=True, unit_diagonal=True)
    u, w = sol[..., :Dv], sol[..., Dv:]
    qk = jnp.einsum('bhcld,bhcsd->bhcls', qc, kc) * dec
    q_dec = qc * jnp.exp(gam)[..., None]
    k_tail = kc * jnp.exp(gam[..., -1:] - gam)[..., None]
    g_last = jnp.exp(gam[..., -1])

    def step(state, inp):
        u_k, w_k, qk_k, qd_k, kt_k, gl_k = inp
        v_new = u_k - jnp.einsum('bhld,bhdv->bhlv', w_k, state)
        o = jnp.einsum('bhld,bhdv->bhlv', qd_k, state) + jnp.einsum('bhls,bhsv->bhlv', qk_k, v_new)
        state = state * gl_k[..., None, None] + jnp.einsum('bhld,bhlv->bhdv', kt_k, v_new)
        return state, o

    xs = tuple(jnp.moveaxis(t, 2, 0) for t in (u, w, qk, q_dec, k_tail, g_last))
    _, o = lax.scan(step, jnp.zeros((bsz, H, Dk, Dv), f32), xs)
    o = o.transpose(1, 0, 3, 2, 4).reshape(bsz, seq, H, Dv)
    o = rms_norm(o, norm_w) * jax.nn.silu(z.astype(f32).reshape(bsz, seq, H, Dv))
    return o.reshape(bsz, seq, GDN_V_DIM).astype(qkv.dtype)


def hybrid_mixer(h, w_in, ssm_conv_w, ssm_conv_b, ssm_dt_bias, ssm_a_log, ssm_d_skip, ssm_norm_w, gdn_conv_w, gdn_a_log, gdn_dt_bias, gdn_norm_w, w_ssm_proj, w_gdn_proj, w_o):
    ssm_z, ssm_xbc, ssm_dt, gdn_qkv, gdn_z, gdn_beta, gdn_a, gate_ssm, gate_gdn = jnp.split(h @ w_in, IN_SPLITS, axis=-1)
    y_ssm = mamba2_ssd_mixer(ssm_z, ssm_xbc, ssm_dt, ssm_conv_w, ssm_conv_b, ssm_dt_bias, ssm_a_log, ssm_d_skip, ssm_norm_w) @ w_ssm_proj
    y_gdn = gated_deltanet_mixer(gdn_qkv, gdn_z, gdn_beta, gdn_a, gdn_conv_w, gdn_a_log, gdn_dt_bias, gdn_norm_w) @ w_gdn_proj
    merged = jax.nn.sigmoid(gate_ssm) * y_ssm + jax.nn.sigmoid(gate_gdn) * y_gdn
    return merged @ w_o


def swiglu(h, w_gate_up, w_down):
    gate, up = jnp.split(h @ w_gate_up, 2, axis=-1)
    return (jax.nn.silu(gate) * up) @ w_down


def _normal(key, shape, scale):
    return jax.random.normal(key, shape, jnp.float32) * scale


def _dt_bias_init(key, shape):
    u = jax.random.uniform(key, shape, jnp.float32)
    dt = jnp.exp(u * (math.log(DT_MAX) - math.log(DT_MIN)) + math.log(DT_MIN))
    return dt + jnp.log(-jnp.expm1(-dt))


def setup_inputs(seed: int = 0) -> dict:
    key = jax.random.key(seed)
    ks = jax.random.split(key, 24)
    L = DEPTH
    return {
        'x': _normal(ks[0], (BATCH, SEQ, D_MODEL), 1.0),
        'c': _normal(ks[1], (BATCH, D_MODEL), 1.0),
        'w_ada': _normal(ks[2], (L, D_MODEL, 6 * D_MODEL), 0.5 * D_MODEL ** -0.5),
        'b_ada': _normal(ks[3], (L, 6 * D_MODEL), 0.01),
        'norm_mix_w': 1.0 + _normal(ks[4], (L, D_MODEL), 0.02),
        'w_in': _normal(ks[5], (L, D_MODEL, IN_DIM), D_MODEL ** -0.5),
        'ssm_conv_w': _normal(ks[6], (L, SSM_CONV, SSM_XBC_DIM), SSM_CONV ** -0.5),
        'ssm_conv_b': _normal(ks[7], (L, SSM_XBC_DIM), 0.01),
        'ssm_dt_bias': _dt_bias_init(ks[8], (L, SSM_N_HEADS)),
        'ssm_a_log': jnp.log(jax.random.uniform(ks[9], (L, SSM_N_HEADS), jnp.float32, 1.0, 16.0)),
        'ssm_d_skip': 1.0 + _normal(ks[10], (L, SSM_N_HEADS), 0.02),
        'ssm_norm_w': 1.0 + _normal(ks[11], (L, SSM_D_INNER), 0.02),
        'gdn_conv_w': _normal(ks[12], (L, GDN_CONV, GDN_QKV_DIM), GDN_CONV ** -0.5),
        'gdn_a_log': jnp.log(jax.random.uniform(ks[13], (L, GDN_N_V_HEADS), jnp.float32, 1.0, 16.0)),
        'gdn_dt_bias': _dt_bias_init(ks[14], (L, GDN_N_V_HEADS)),
        'gdn_norm_w': 1.0 + _normal(ks[15], (L, GDN_HEAD_DIM), 0.02),
        'w_ssm_proj': _normal(ks[16], (L, SSM_D_INNER, D_MODEL), SSM_D_INNER ** -0.5),
        'w_gdn_proj': _normal(ks[17], (L, GDN_V_DIM, D_MODEL), GDN_V_DIM ** -0.5),
        'w_o': _normal(ks[18], (L, D_MODEL, D_MODEL), D_MODEL ** -0.5),
        'norm_ffn_w': 1.0 + _normal(ks[19], (L, D_MODEL), 0.02),
        'w_gate_up': _normal(ks[20], (L, D_MODEL, 2 * FFN_HIDDEN), D_MODEL ** -0.5),
        'w_down': _normal(ks[21], (L, FFN_HIDDEN, D_MODEL), FFN_HIDDEN ** -0.5),
        'final_norm_w': 1.0 + _normal(ks[22], (D_MODEL,), 0.02),
    }


def reference(x, c, w_ada, b_ada, norm_mix_w, w_in, ssm_conv_w, ssm_conv_b, ssm_dt_bias, ssm_a_log, ssm_d_skip, ssm_norm_w, gdn_conv_w, gdn_a_log, gdn_dt_bias, gdn_norm_w, w_ssm_proj, w_gdn_proj, w_o, norm_ffn_w, w_gate_up, w_down, final_norm_w):
    c_act = jax.nn.silu(c)
    for layer in range(DEPTH):
        shift_m, scale_m, gate_m, shift_f, scale_f, gate_f = jnp.split(c_act @ w_ada[layer] + b_ada[layer], 6, axis=-1)
        h = modulate(rms_norm(x, norm_mix_w[layer]), shift_m, scale_m)
        mix = hybrid_mixer(h, w_in[layer], ssm_conv_w[layer], ssm_conv_b[layer], ssm_dt_bias[layer], ssm_a_log[layer], ssm_d_skip[layer], ssm_norm_w[layer], gdn_conv_w[layer], gdn_a_log[layer], gdn_dt_bias[layer], gdn_norm_w[layer], w_ssm_proj[layer], w_gdn_proj[layer], w_o[layer])
        x = x + gate_m[:, None, :] * mix
        h = modulate(rms_norm(x, norm_ffn_w[layer]), shift_f, scale_f)
        x = x + gate_f[:, None, :] * swiglu(h, w_gate_up[layer], w_down[layer])
    return rms_norm(x, final_norm_w)
```

```python
from contextlib import ExitStack
import numpy as np
import os as _os
import concourse.bass as bass
import concourse.mybir as mybir
from concourse.bass_utils import run_bass_kernel_spmd

F32 = mybir.dt.float32
BF16 = mybir.dt.bfloat16
AF = mybir.ActivationFunctionType
ALU = mybir.AluOpType

D = 2048
SEQ = 4096
TP = 2048
TQ = 2048
TT = TP + TQ
IN_DIM = 26752
FFN = 5632
NEG = -30000.0
OFF_Z, OFF_XBC, OFF_DT, OFF_QKV, OFF_GZ, OFF_BETA, OFF_A, OFF_GS, OFF_GG = (
    0, 4096, 10240, 10304, 18496, 22592, 22624, 22656, 24704)
ENGS = ("pe", "act", "dve", "pool", "sp")


class Buf:
    ALL = []

    def __init__(self, ap, const=False, excl=False):
        self.ap = ap
        self.excl = excl
        self.w = None
        self.r = {}
        self.const = const
        Buf.ALL.append(self)

    def __getitem__(self, k):
        return self.ap[k]


class Slot:
    def __init__(self, sem):
        self.sem = sem
        self.count = 0


class Prog:
    def __init__(self, nc, es, tag):
        self.nc = nc
        self.tag = tag
        self.streams = {e: [] for e in ENGS}
        self.waited = {e: {} for e in ENGS}
        self.sems = {e: es.enter_context(nc.semaphore(f"{tag}_s_{e}")) for e in ENGS}
        self.done = es.enter_context(nc.semaphore(f"{tag}_done"))
        self.es = es
        self.nslot = 0
        self.dma_toks = []
        for b in Buf.ALL:
            b.w = None
            b.r = {}

    def slot(self):
        self.nslot += 1
        return Slot(self.es.enter_context(self.nc.semaphore(f"{self.tag}_d{self.nslot}")))

    def _waits(self, eng, deps):
        out = []
        for d in deps:
            if d is None:
                continue
            if d[0] == "e":
                if d[1] == eng and eng == "pe":
                    continue
                key, val = ("e", d[1]), d[2]
            else:
                key, val = ("d", id(d[1])), d[2]
            if self.waited[eng].get(key, -1) >= val:
                continue
            self.waited[eng][key] = val
            out.append(d)
        return out

    def _collect(self, reads, writes, deps):
        al = list(deps)
        for b in reads:
            al.append(b.w)
        for b in writes:
            al.append(b.w)
            al.extend(b.r.values())
        return al

    def _update(self, tok, reads, writes):
        for b in reads:
            if not b.const:
                b.r[(tok[0], tok[1] if tok[0] == "e" else id(tok[1]))] = tok
        for b in writes:
            b.w = tok
            b.r = {}

    def op(self, eng, fn, reads=(), writes=(), deps=()):
        writes = list(writes) + [b for b in reads if b.excl]
        reads = [b for b in reads if not b.excl]
        waits = self._waits(eng, self._collect(reads, writes, deps))
        idx = len(self.streams[eng])
        self.streams[eng].append(["op", fn, waits, False, None])
        tok = ("e", eng, idx)
        self._update(tok, reads, writes)
        return tok

    def dma(self, q, out, in_, slot, reads=(), writes=(), deps=()):
        waits = self._waits(q, self._collect(reads, writes, deps))
        slot.count += 16
        self.streams[q].append(["dma", (out, in_), waits, False, slot])
        tok = ("d", slot, slot.count)
        self._update(tok, reads, writes)
        self.dma_toks.append(tok)
        return tok

    def replay(self):
        nc = self.nc
        for e in ENGS:
            for rec in self.streams[e]:
                for w in rec[2]:
                    if w[0] == "e":
                        self.streams[w[1]][w[2]][3] = True
        last = {}
        for e in ENGS:
            ops = [i for i, r in enumerate(self.streams[e]) if r[0] == "op"]
            if ops:
                self.streams[e][ops[-1]][3] = True
                last[e] = ops[-1]
        counts = {}
        for e in ENGS:
            c = 0
            cl = []
            for rec in self.streams[e]:
                if rec[0] == "op" and rec[3]:
                    c += 1
                cl.append(c)
            counts[e] = cl
        final_d = {}
        for t in self.dma_toks:
            final_d[id(t[1])] = (t[1], max(final_d.get(id(t[1]), (None, 0))[1], t[2]))

        def run(eng_name, eng):
            for rec in self.streams[eng_name]:
                for w in rec[2]:
                    if w[0] == "e":
                        eng.wait_ge(self.sems[w[1]], counts[w[1]][w[2]])
                    else:
                        eng.wait_ge(w[1].sem, w[2])
                if rec[0] == "op":
                    ins = rec[1](eng)
                    if rec[3]:
                        ins.then_inc(self.sems[eng_name], 1)
                else:
                    o, i = rec[1]
                    eng.dma_start(out=o, in_=i).then_inc(rec[4].sem, 16)
            if eng_name == "sp":
                for e2, li in last.items():
                    eng.wait_ge(self.sems[e2], counts[e2][li])
                for sl, v in final_d.values():
                    eng.wait_ge(sl.sem, v)
                eng.sem_inc(self.done, 1)
            else:
                eng.wait_ge(self.done, 1)

        with nc.Block() as block:
            block.sync(lambda e: run("sp", e))
            block.tensor(lambda e: run("pe", e))
            block.scalar(lambda e: run("act", e))
            block.vector(lambda e: run("dve", e))
            block.gpsimd(lambda e: run("pool", e))


_UID = [0]


def sb(nc, es, name, shape, dt, const=False):
    _UID[0] += 1
    return Buf(es.enter_context(nc.sbuf_tensor(f"s{_UID[0]}_{name}", list(shape), dt)), const=const)


def ps(nc, es, name, shape, dt=F32):
    _UID[0] += 1
    return Buf(es.enter_context(nc.psum_tensor(f"p{_UID[0]}_{name}", list(shape), dt)), excl=True)


def build_program(dbg=False, upto=99, p1only=False, mini=False):
    nc = bass.Bass("TRN2", target_bir_lowering=False)
    I = {}

    def din(name, shape, dt=F32):
        if mini and name in ("w_ada", "w_in", "w_ssm", "w_gdn", "w_o", "w_gu", "w_dn"):
            shape = [128, 128]
        I[name] = nc.dram_tensor(name, list(shape), dt, kind="ExternalInput").ap()
        return I[name]

    xin = din("xin", [TT, D])
    flag_d = din("flag", [128, 1])
    c_col_d = din("c_col", [128, 16])
    w_ada = din("w_ada", [D, 6 * D])
    b_ada_col_d = din("b_ada_col", [128, 96])
    nmw_d = din("nmw", [128, 16])
    nfw_d = din("nfw", [128, 16])
    fnw_row_d = din("fnw_row", [128, D])
    w_in = din("w_in", [D, IN_DIM])
    cw_d = din("cw", [128, 112, 4])
    cb_d = din("cb", [128, 112])
    smallbias_d = din("smallbias", [128, 1])
    alog_d = din("alog", [128, 1])
    dskip_d = din("dskip_row", [128, 64])
    ssmnw_d = din("ssmnw_col", [128, 32])
    gdnnw_d = din("gdnnw_col", [128, 1])
    w_ssm = din("w_ssm", [4096, D])
    w_gdn = din("w_gdn", [4096, D])
    w_o = din("w_o", [D, D])
    w_gu = din("w_gu", [D, 2 * FFN])
    w_dn = din("w_dn", [FFN, D])
    ident_d = din("ident", [128, 128])
    tri_d = din("tri", [128, 128])
    ones_d = din("ones", [128, 128])
    maskT_d = din("maskT", [128, 128])
    maskLs_d = din("maskLs", [128, 128])
    y_out = nc.dram_tensor("y", [TQ, D], F32, kind="ExternalOutput").ap()

    cs = nc.dram_tensor("cs", [14336, TT], BF16).ap()
    zs = nc.dram_tensor("zs", [12288, TQ], BF16).ap()
    smtok = nc.dram_tensor("smtok", [TT, 256], F32).ap()
    so = nc.dram_tensor("so", [4096, TQ], BF16).ap()
    go = nc.dram_tensor("go", [4096, TQ], BF16).ap()
    dbg_outs = {}

    with ExitStack() as top:
        ident = sb(nc, top, "ident", [128, 128], F32, const=True)
        identb = sb(nc, top, "identb", [128, 128], BF16, const=True)
        tri = sb(nc, top, "tri", [128, 128], F32, const=True)
        ones = sb(nc, top, "ones", [128, 128], F32, const=True)
        maskT = sb(nc, top, "maskT", [128, 128], F32, const=True)
        maskLs = sb(nc, top, "maskLs", [128, 128], F32, const=True)
        flag = sb(nc, top, "flagt", [128, 1], F32, const=True)
        ada = sb(nc, top, "ada", [128, 96], F32, const=True)
        A1 = sb(nc, top, "A1", [128, 16], F32, const=True)
        A2 = sb(nc, top, "A2", [128, 16], F32, const=True)
        fnw_row = sb(nc, top, "fnw_row", [128, D], F32, const=True)
        cw = sb(nc, top, "cw", [128, 112, 4], F32, const=True)
        cbias = sb(nc, top, "cbias", [128, 112], F32, const=True)
        smallbias = sb(nc, top, "smallbias", [128, 1], F32, const=True)
        negA = sb(nc, top, "negA", [128, 1], F32, const=True)
        dskip = sb(nc, top, "dskip", [128, 64], F32, const=True)
        ssmnw = sb(nc, top, "ssmnw", [128, 32], F32, const=True)
        gdnnw = sb(nc, top, "gdnnw", [128, 1], F32, const=True)
        hist = sb(nc, top, "hist", [128, 112, 3], F32)

        with ExitStack() as es:
            P = Prog(nc, es, "p0")
            ld = P.slot()
            for buf, src in ((ident, ident_d), (tri, tri_d), (ones, ones_d), (maskT, maskT_d), (maskLs, maskLs_d),
                             (flag, flag_d), (fnw_row, fnw_row_d), (cw, cw_d), (cbias, cb_d), (smallbias, smallbias_d),
                             (dskip, dskip_d), (ssmnw, ssmnw_d), (gdnnw, gdnnw_d)):
                P.dma("sp", buf.ap[:], src, ld, writes=[buf])
            ccol = sb(nc, es, "ccol", [128, 16], F32)
            cact = sb(nc, es, "cact", [128, 16], F32)
            bada = sb(nc, es, "bada", [128, 96], F32)
            nmw = sb(nc, es, "nmw", [128, 16], F32)
            nfw = sb(nc, es, "nfw", [128, 16], F32)
            alog = sb(nc, es, "alog", [128, 1], F32)
            tmp16 = sb(nc, es, "tmp16", [128, 16], F32)
            for buf, src in ((ccol, c_col_d), (bada, b_ada_col_d), (nmw, nmw_d), (nfw, nfw_d), (alog, alog_d)):
                P.dma("sp", buf.ap[:], src, ld, writes=[buf])
            P.op("act", lambda e: e.activation(out=cact.ap[:], in_=ccol.ap[:], func=AF.Silu), reads=[ccol], writes=[cact])
            P.op("act", lambda e: e.activation(out=negA.ap[:], in_=alog.ap[:], func=AF.Exp), reads=[alog], writes=[negA])
            P.op("dve", lambda e: e.tensor_scalar(out=negA.ap[:], in0=negA.ap[:], scalar1=-1.0, scalar2=None, op0=ALU.mult),
                 reads=[negA], writes=[negA])
            P.op("dve", lambda e: e.tensor_copy(out=identb.ap[:], in_=ident.ap[:]), reads=[ident], writes=[identb])
            P.op("dve", lambda e: e.memset(hist.ap[:], 0.0), writes=[hist])
            wa = [sb(nc, es, f"wa{i}", [128, 16, 128], F32) for i in range(3)]
            wsl = [P.slot() for _ in range(3)]
            pada = ps(nc, es, "pada", [128, 96])
            wav = w_ada.rearrange("(kc p) f -> p kc f", p=128)
            if mini:
                P.op("pe", lambda e: e.matmul(pada.ap[:, 0:96], lhsT=ident.ap[:], rhs=ones.ap[:, 0:96], start=True, stop=True),
                     reads=[ident, ones], writes=[pada])
            for ft in range(0 if mini else 96):
                wb = wa[ft % 3]
                P.dma("sp", wb.ap[:, 0:8, :], wav[:, 0:8, ft * 128:(ft + 1) * 128], wsl[ft % 3], writes=[wb])
                P.dma("sp", wb.ap[:, 8:16, :], wav[:, 8:16, ft * 128:(ft + 1) * 128], wsl[ft % 3], writes=[])
                wb.w = ("d", wsl[ft % 3], wsl[ft % 3].count)
                for kc in range(16):
                    P.op("pe", lambda e, wb=wb, kc=kc, ft=ft: e.matmul(pada.ap[:, ft:ft + 1], lhsT=wb.ap[:, kc, :],
                                                                      rhs=cact.ap[:, kc:kc + 1], start=(kc == 0), stop=(kc == 15)),
                         reads=[wb, cact], writes=[pada])
            P.op("dve", lambda e: e.tensor_tensor(out=ada.ap[:], in0=pada.ap[:], in1=bada.ap[:], op=ALU.add),
                 reads=[pada, bada], writes=[ada])
            for (Ax, nw, c0) in ((A1, nmw, 16), (A2, nfw, 64)):
                P.op("dve", lambda e, c0=c0: e.tensor_scalar(out=tmp16.ap[:], in0=ada.ap[:, c0:c0 + 16], scalar1=1.0, scalar2=None,
                                                            op0=ALU.add), reads=[ada], writes=[tmp16])
                P.op("dve", lambda e, Ax=Ax, nw=nw: e.tensor_tensor(out=Ax.ap[:], in0=tmp16.ap[:], in1=nw.ap[:], op=ALU.mult),
                     reads=[tmp16, nw], writes=[Ax])
            P.replay()

        for pas in range(2):
            if upto < 1 + pas:
                continue
            t0 = pas * TP
            NT = 16
            import os as _os
            NT1 = int(_os.environ.get('NT1', '16'))
            CUT = int(_os.environ.get('CUT', '99'))
            with ExitStack() as es:
                P = Prog(nc, es, f"p2{pas}")
                hT = sb(nc, es, "hT", [128, 16, 2048], BF16)
                xt = [sb(nc, es, f"xt{i}", [128, D], F32) for i in range(2)]
                xsl = [P.slot() for _ in range(2)]
                sq = sb(nc, es, "sqjunk", [128, D], F32)
                st1 = [sb(nc, es, f"st1_{i}", [128, 4], F32) for i in range(2)]
                ptr = [ps(nc, es, f"ptr{i}", [128, 4, 128]) for i in range(4)]
                p1_last = {}
                for ti in range(NT1):
                    xb = xt[ti % 2]
                    s1 = st1[ti % 2]
                    P.dma("sp", xb.ap[:, 0:1024], xin[t0 + ti * 128:t0 + (ti + 1) * 128, 0:1024], xsl[ti % 2], writes=[xb])
                    P.dma("sp", xb.ap[:, 1024:2048], xin[t0 + ti * 128:t0 + (ti + 1) * 128, 1024:2048], xsl[ti % 2])
                    xb.w = ("d", xsl[ti % 2], xsl[ti % 2].count)
                    P.op("act", lambda e, xb=xb, s1=s1: e.activation(out=sq.ap[:], in_=xb.ap[:], func=AF.Square, accum_out=s1.ap[:, 0:1]),
                         reads=[xb], writes=[sq, s1])
                    if CUT < 1:
                        continue
                    P.op("act", lambda e, s1=s1: e.activation(out=s1.ap[:, 1:2], in_=s1.ap[:, 0:1], func=AF.Ln, scale=1.0 / D, bias=1e-6),
                         reads=[s1], writes=[s1])
                    P.op("act", lambda e, s1=s1: e.activation(out=s1.ap[:, 2:3], in_=s1.ap[:, 1:2], func=AF.Exp, scale=-0.5),
                         reads=[s1], writes=[s1])
                    if CUT < 2:
                        continue
                    P.op("dve", lambda e, xb=xb, s1=s1: e.tensor_scalar(out=xb.ap[:], in0=xb.ap[:], scalar1=s1.ap[:, 2:3], scalar2=None,
                                                                       op0=ALU.mult), reads=[xb, s1], writes=[xb])
                    for q in range(4):
                        if CUT < 3:
                            continue
                        pb = ptr[q]
                        for j in range(4):
                            kc = q * 4 + j
                            P.op("pe", lambda e, pb=pb, j=j, kc=kc, xb=xb: e.transpose(pb.ap[:, j, :], xb.ap[:, kc * 128:(kc + 1) * 128], ident.ap[:]),
                                 reads=[xb, ident], writes=[pb])
                        for j in range(4):
                            if CUT < 4:
                                continue
                            kc = q * 4 + j
                            eng = "act" if (q % 2 == 0) else "dve"
                            if eng == "act":
                                P.op("act", lambda e, pb=pb, j=j, kc=kc, ti=ti: e.activation(
                                    out=hT.ap[:, kc, ti * 128:(ti + 1) * 128], in_=pb.ap[:, j, :], func=AF.Identity,
                                    scale=A1.ap[:, kc:kc + 1], bias=ada.ap[:, kc:kc + 1]), reads=[pb, A1, ada])
                                p1_last["act"] = P.streams["act"] and ("e", "act", len(P.streams["act"]) - 1)
                            else:
                                P.op("dve", lambda e, pb=pb, j=j, kc=kc, ti=ti: e.tensor_scalar(
                                    out=hT.ap[:, kc, ti * 128:(ti + 1) * 128], in0=pb.ap[:, j, :], scalar1=A1.ap[:, kc:kc + 1],
                                    scalar2=ada.ap[:, kc:kc + 1], op0=ALU.mult, op1=ALU.add), reads=[pb, A1, ada])
                                p1_last["dve"] = ("e", "dve", len(P.streams["dve"]) - 1)
                tiles = []
                for t in range(48):
                    kd = "hist" if (pas == 0 and t >= 40) else "conv"
                    tiles.append((kd, t, [(OFF_XBC + t * 128, 128, 0)]))
                for t in range(64):
                    kd = "hist" if (pas == 0 and t < 16) else "conv"
                    tiles.append((kd, 48 + t, [(OFF_QKV + t * 128, 128, 0)]))
                tiles.append(("small", 0, [(OFF_BETA, 64, 0), (OFF_DT, 64, 64)]))
                if p1only:
                    tiles = []
                if pas == 1:
                    for t in range(32):
                        tiles.append(("silu", t, [(OFF_Z + t * 128, 128, 0)]))
                    for t in range(32):
                        tiles.append(("silu", 32 + t, [(OFF_GZ + t * 128, 128, 0)]))
                    for t in range(32):
                        tiles.append(("sig", 64 + t, [(OFF_GS + t * 128, 128, 0)]))
                wst = [sb(nc, es, f"wst{i}", [128, 16, 128], F32) for i in range(3)]
                wsl = [P.slot() for _ in range(3)]
                wbf = [sb(nc, es, f"wbf{i}", [128, 16, 128], BF16) for i in range(2)]
                pmm = [ps(nc, es, f"pmm{i}", [128, 512]) for i in range(4)]
                cbuf = [sb(nc, es, f"cbuf{i}", [128, 3 + 2048], F32) for i in range(2)]
                acc = [sb(nc, es, f"acc{i}", [128, 2048], F32) for i in range(2)]
                ost = [sb(nc, es, f"ost{i}", [128, 2048], BF16) for i in range(2)]
                osl = [P.slot() for _ in range(2)]
                sm1 = sb(nc, es, "sm1", [128, 2048], F32)
                sm2 = sb(nc, es, "sm2", [128, 2048], F32)
                sm3 = sb(nc, es, "sm3", [128, 2048], F32)
                smt = [sb(nc, es, f"smt{i}", [128, 256], F32) for i in range(2)]
                smsl = [P.slot() for _ in range(2)]
                winv = w_in.rearrange("(kc p) f -> p kc f", p=128)
                nconv = 0
                for tix, (kind, tidx, segs) in enumerate(tiles):
                    ws = wst[tix % 3]
                    first = True
                    for (c0, ncol, f0) in segs:
                        for half in range(2):
                            P.dma("sp", ws.ap[:, half * 8:(half + 1) * 8, f0:f0 + ncol], winv[:, half * 8:(half + 1) * 8, c0:c0 + ncol],
                                  wsl[tix % 3], writes=[ws] if first else [])
                            first = False
                    ws.w = ("d", wsl[tix % 3], wsl[tix % 3].count)
                    wb = wbf[tix % 2]
                    P.op("pool", lambda e, wb=wb, ws=ws: e.tensor_copy(out=wb.ap[:], in_=ws.ap[:]), reads=[ws], writes=[wb])
                    if kind in ("conv", "hist"):
                        cbf = cbuf[nconv % 2]
                        ac = acc[nconv % 2]
                        nconv += 1
                        if pas == 0:
                            P.op("pool", lambda e, cbf=cbf: e.memset(cbf.ap[:, 0:3], 0.0), writes=[cbf])
                        else:
                            P.op("pool", lambda e, cbf=cbf, tidx=tidx: e.tensor_scalar(out=cbf.ap[:, 0:3], in0=hist.ap[:, tidx, :], scalar1=flag.ap[:, 0:1],
                                                                                    scalar2=None, op0=ALU.mult), reads=[hist], writes=[cbf])
                    for blk in range(4):
                        if kind == "hist" and blk < 3:
                            continue
                        pb = pmm[(tix * 4 + blk) % 4]
                        for kc in range(16):
                            P.op("pe", lambda e, pb=pb, wb=wb, kc=kc, blk=blk: e.matmul(pb.ap[:], lhsT=wb.ap[:, kc, :], rhs=hT.ap[:, kc, blk * 512:(blk + 1) * 512],
                                                                                  start=(kc == 0), stop=(kc == 15)), reads=[wb], writes=[pb], deps=list(p1_last.values()))
                        sl_ = slice(blk * 512, (blk + 1) * 512)
                        if kind in ("conv", "hist"):
                            P.op("act", lambda e, pb=pb, cbf=cbf, blk=blk: e.activation(out=cbf.ap[:, 3 + blk * 512:3 + (blk + 1) * 512], in_=pb.ap[:], func=AF.Copy),
                                 reads=[pb], writes=[cbf])
                        elif kind == "small":
                            P.op("act", lambda e, pb=pb, sl_=sl_: e.activation(out=sm1.ap[:, sl_], in_=pb.ap[:], func=AF.Identity, bias=smallbias.ap[:, 0:1]),
                                 reads=[pb, smallbias], writes=[sm1])
                        else:
                            o_ = ost[tix % 2]
                            fn_ = AF.Silu if kind == "silu" else AF.Sigmoid
                            P.op("act", lambda e, pb=pb, o_=o_, sl_=sl_, fn_=fn_: e.activation(out=o_.ap[:, sl_], in_=pb.ap[:], func=fn_),
                                 reads=[pb], writes=[o_])
                    if kind == "hist":
                        P.op("pool", lambda e, cbf=cbf, tidx=tidx: e.tensor_copy(out=hist.ap[:, tidx, :], in_=cbf.ap[:, 2048:2051]), reads=[cbf], writes=[hist])
                    elif kind == "conv":
                        P.op("dve", lambda e, cbf=cbf, ac=ac, tidx=tidx: e.tensor_scalar(out=ac.ap[:], in0=cbf.ap[:, 0:2048], scalar1=cw.ap[:, tidx, 0:1],
                                                                                    scalar2=cbias.ap[:, tidx:tidx + 1], op0=ALU.mult, op1=ALU.add),
                             reads=[cbf, cw, cbias], writes=[ac])
                        for k in range(1, 4):
                            P.op("dve", lambda e, cbf=cbf, ac=ac, tidx=tidx, k=k: e.scalar_tensor_tensor(out=ac.ap[:], in0=cbf.ap[:, k:k + 2048], scalar=cw.ap[:, tidx, k:k + 1],
                                                                                                 in1=ac.ap[:], op0=ALU.mult, op1=ALU.add),
                                 reads=[cbf, cw, ac], writes=[ac])
                        if pas == 0:
                            P.op("pool", lambda e, cbf=cbf, tidx=tidx: e.tensor_copy(out=hist.ap[:, tidx, :], in_=cbf.ap[:, 2048:2051]), reads=[cbf], writes=[hist])
                        o_ = ost[tix % 2]
                        P.op("act", lambda e, ac=ac, o_=o_: e.activation(out=o_.ap[:], in_=ac.ap[:], func=AF.Silu), reads=[ac], writes=[o_])
                        P.dma("act", cs[tidx * 128:(tidx + 1) * 128, t0:t0 + 2048], o_.ap[:], osl[tix % 2], reads=[o_])
                    elif kind == "small":
                        P.op("act", lambda e: e.activation(out=sm2.ap[:], in_=sm1.ap[:], func=AF.Exp), reads=[sm1], writes=[sm2])
                        P.op("act", lambda e: e.activation(out=sm2.ap[:], in_=sm2.ap[:], func=AF.Ln, bias=1.0), reads=[sm2], writes=[sm2])
                        P.op("act", lambda e: e.activation(out=sm1.ap[0:32, :], in_=sm1.ap[0:32, :], func=AF.Sigmoid), reads=[sm1, sm2], writes=[sm1])
                        P.op("dve", lambda e: e.tensor_scalar(out=sm3.ap[:], in0=sm2.ap[:], scalar1=negA.ap[:, 0:1], scalar2=None, op0=ALU.mult),
                             reads=[sm2, negA], writes=[sm3])
                        P.op("dve", lambda e: e.tensor_copy(out=sm1.ap[32:64, :], in_=sm3.ap[32:64, :]), reads=[sm3, sm1], writes=[sm1])
                        P.op("dve", lambda e: e.tensor_copy(out=sm1.ap[64:128, :], in_=sm2.ap[64:128, :]), reads=[sm2, sm1], writes=[sm1])
                        for ti in range(NT):
                            pb = ptr[ti % 4]
                            stt = smt[ti % 2]
                            P.op("pe", lambda e, pb=pb, ti=ti: e.transpose(pb.ap[:, 0, :], sm1.ap[:, ti * 128:(ti + 1) * 128], ident.ap[:]), reads=[sm1], writes=[pb])
                            P.op("pe", lambda e, pb=pb, ti=ti: e.transpose(pb.ap[:, 1, :], sm3.ap[:, ti * 128:(ti + 1) * 128], ident.ap[:]), reads=[sm3], writes=[pb])
                            P.op("dve", lambda e, pb=pb, stt=stt: e.tensor_copy(out=stt.ap[:], in_=pb.ap[:, 0:2, :].rearrange("p a b -> p (a b)")), reads=[pb], writes=[stt])
                            P.dma("act", smtok[t0 + ti * 128:t0 + (ti + 1) * 128, :], stt.ap[:], smsl[ti % 2], reads=[stt])
                    else:
                        P.dma("act", zs[tidx * 128:(tidx + 1) * 128, :], o_.ap[:], osl[tix % 2], reads=[o_])
                P.replay()

        if upto >= 3:
          with ExitStack() as es:
            P = Prog(nc, es, "p3")
            KL = 6
            NCH = TT // 128
            C0 = TP // 128
            c_lo = int(_os.environ.get("MIX_CLO", "0"))
            c_hi = int(_os.environ.get("MIX_CHI", str(NCH)))
            do_gdn = int(_os.environ.get("MIX_GDN", "1"))
            do_ssd = int(_os.environ.get("MIX_SSD", "1"))

            def T(name, shape, dt=F32):
                return sb(nc, es, name, shape, dt)

            def mm(ob, oap, lb, lap, rb, rap, start=True, stop=True):
                return P.op("pe", lambda e: e.matmul(oap, lhsT=lap, rhs=rap, start=start, stop=stop), reads=[lb, rb], writes=[ob])

            def tp(ob, oap, ib, iap):
                return P.op("pe", lambda e: e.transpose(oap, iap, ident.ap[:]), reads=[ib, ident], writes=[ob])

            def act(ob, oap, ib, iap, func, rd=(), wr=(), **kw):
                return P.op("act", lambda e: e.activation(out=oap, in_=iap, func=func, **kw), reads=[ib] + list(rd), writes=[ob] + list(wr))

            def cp(eng, ob, oap, ib, iap):
                return P.op(eng, lambda e: e.tensor_copy(out=oap, in_=iap), reads=[ib], writes=[ob])

            def tt(eng, ob, oap, ab, aap, bb, bap, op):
                return P.op(eng, lambda e: e.tensor_tensor(out=oap, in0=aap, in1=bap, op=op), reads=[ab, bb], writes=[ob])

            def tsc(eng, ob, oap, ab, aap, s1, s2, op0, op1=None, rd=()):
                if op1 is None:
                    return P.op(eng, lambda e: e.tensor_scalar(out=oap, in0=aap, scalar1=s1, scalar2=None, op0=op0), reads=[ab] + list(rd), writes=[ob])
                return P.op(eng, lambda e: e.tensor_scalar(out=oap, in0=aap, scalar1=s1, scalar2=s2, op0=op0, op1=op1), reads=[ab] + list(rd), writes=[ob])

            def stt(eng, ob, oap, ab, aap, sc, cb_, cap, op0, op1, rd=()):
                return P.op(eng, lambda e: e.scalar_tensor_tensor(out=oap, in0=aap, scalar=sc, in1=cap, op0=op0, op1=op1),
                            reads=[ab, cb_] + list(rd), writes=[ob])

            csl_s = T("csl_s", [128, 48, 128], BF16)
            csl_g = T("csl_g", [128, 64, 128], BF16)
            csls = [P.slot() for _ in range(2)]
            zsl = [T(f"zsl{i}", [128, 64, 128], BF16) for i in range(1)] * 2
            zsls = [P.slot() for _ in range(2)]
            smb = [T(f"smb{i}", [128, 256]) for i in range(2)]
            smsl = [P.slot() for _ in range(2)]
            Sg_t = T("Sg", [128, 32, 128])
            Ss_t = T("Ss", [128, 64, 64])
            Sg = [Buf(Sg_t.ap[:, h, :]) for h in range(32)]
            Ss = [Buf(Ss_t.ap[:, h, :]) for h in range(64)]
            gcol, ngcol, glast, ktl, eglast = [T(n, [128, 96]) for n in ("gcol", "ngcol", "glast", "ktl", "eglast")]
            bg, nbeta = T("bg", [128, 32]), T("nbeta", [128, 32])
            gost = [T(f"gost{i}", [128, 32, 128], BF16) for i in range(1)] * 2
            sost = [T(f"sost{i}", [128, 32, 128], BF16) for i in range(1)] * 2
            gosl = [P.slot() for _ in range(2)]
            sosl = [P.slot() for _ in range(2)]
            pS = ps(nc, es, "pS", [128, 512])
            pP = ps(nc, es, "pP", [128, 4, 128])
            lanes = []
            for i in range(KL):
                L = {"pA": ps(nc, es, f"pA{i}", [128, 4, 128])}
                for n in ("t1", "Ma", "Mb", "Ba", "Bb", "X", "PT", "qdT", "bkg", "bv", "nwT", "vnew", "kt", "on"):
                    L[n] = T(f"L{i}_{n}", [128, 128])
                L["decs"], L["decT"], L["egrr"] = L["Mb"], L["Bb"], L["nwT"]
                L["st"] = T(f"L{i}_st", [128, 4])
                L["junk"] = L["on"]
                lanes.append(L)
            prep = []
            for i in range(3):
                Pp = {n: T(f"pp{i}_{n}", [128, 128]) for n in ("kTf", "ktok", "kTn", "qtok", "qTn", "vTf0", "vTf1", "vtok0", "vtok1", "zf0", "zf1", "junk")}
                Pp["qTf"] = Pp["kTf"]
                Pp["BTf"], Pp["Btok"], Pp["CTf"], Pp["KQg"] = Pp["kTn"], Pp["ktok"], Pp["qTn"], Pp["qtok"]
                Pp["st"] = T(f"pp{i}_st", [128, 8])
                prep.append(Pp)
            xTf = [T(f"xTf{i}", [128, 128]) for i in range(2)]
            xtk = [T(f"xtk{i}", [128, 128]) for i in range(4)]
            yg = T("yg", [128, 512])
            yz = T("yz", [128, 512])
            zf4 = T("zf4", [128, 4, 128])
            gst = T("gst", [128, 4])
            csv = cs.rearrange("(t p) k -> p t k", p=128)
            zsv = zs.rearrange("(t p) k -> p t k", p=128)
            sov = so.rearrange("(t p) k -> p t k", p=128)
            gov = go.rearrange("(t p) k -> p t k", p=128)

            P.op("dve", lambda e: e.memset(Sg_t.ap[:], 0.0), writes=Sg)
            P.op("pool", lambda e: e.memset(Ss_t.ap[:], 0.0), writes=Ss)

            def run_lanes(gens):
                gens = list(gens)
                while gens:
                    batch, gens = gens[:KL], gens[KL:]
                    live = [g(lanes[i]) for i, g in enumerate(batch)]
                    while live:
                        for g in list(live):
                            try:
                                next(g)
                            except StopIteration:
                                live.remove(g)

            def neumann(L, own):
                pA = pB = L["pA"]
                tp(pA, pA.ap[:, 3, :], L["Ma"], L["Ma"].ap[:])
                act(L["Ba"], L["Ba"].ap[:], pA, pA.ap[:, 3, :], AF.Copy)
                tt("dve", L["X"], L["X"].ap[:], pA, pA.ap[:, 3, :], ident, ident.ap[:], ALU.add)
                yield
                M, B, Mn, Bn = L["Ma"], L["Ba"], L["Mb"], L["Bb"]
                for j in range(1, 7):
                    mm(pB, pB.ap[:, 0, :], B, B.ap[:], M, M.ap[:])
                    if j < 6:
                        mm(pB, pB.ap[:, 1, :], M, M.ap[:], B, B.ap[:])
                    act(Mn, Mn.ap[:], pB, pB.ap[:, 0, :], AF.Copy)
                    if j < 6:
                        cp("dve", Bn, Bn.ap[:], pB, pB.ap[:, 1, :])
                    yield
                    mm(pB, pB.ap[:, 2, :], Mn, Mn.ap[:], L["X"], L["X"].ap[:])
                    tt("dve", L["X"], L["X"].ap[:], pB, pB.ap[:, 2, :], L["X"], L["X"].ap[:], ALU.add)
                    yield
                    M, B, Mn, Bn = Mn, Bn, M, B

            def gdn_unit(c, h, Pp, e01, smc, zslab):
                own = c >= C0
                col = h

                def gen(L):
                    pA = pB = L["pA"]
                    S = Sg[h]
                    vtok = Pp[f"vtok{e01}"]
                    mm(pA, pA.ap[:, 0, :], smc, smc.ap[:, 32 + h:33 + h].to_broadcast([128, 128]), tri, tri.ap[:])
                    mm(pA, pA.ap[:, 1, :], Pp["kTn"], Pp["kTn"].ap[:], Pp["kTn"], Pp["kTn"].ap[:])
                    if own:
                        mm(pA, pA.ap[:, 2, :], Pp["kTn"], Pp["kTn"].ap[:], Pp["qTn"], Pp["qTn"].ap[:])
                    yield
                    stt("dve", L["t1"], L["t1"].ap[:], pA, pA.ap[:, 0, :], -1.0, maskLs, maskLs.ap[:], ALU.mult, ALU.add)
                    act(L["decs"], L["decs"].ap[:], L["t1"], L["t1"].ap[:], AF.Exp, rd=[gcol], bias=gcol.ap[:, col:col + 1])
                    if own:
                        tt("dve", L["junk"], L["junk"].ap[:], pA, pA.ap[:, 0, :], maskT, maskT.ap[:], ALU.add)
                        act(L["decT"], L["decT"].ap[:], L["junk"], L["junk"].ap[:], AF.Exp, rd=[ngcol], bias=ngcol.ap[:, col:col + 1])
                        act(L["egrr"], L["egrr"].ap[:], pA, pA.ap[:, 0, :], AF.Exp)
                    yield
                    stt("dve", L["Ma"], L["Ma"].ap[:], pA, pA.ap[:, 1, :], nbeta.ap[:, h:h + 1], L["decs"], L["decs"].ap[:], ALU.mult, ALU.mult, rd=[nbeta])
                    if own:
                        tt("dve", L["PT"], L["PT"].ap[:], pA, pA.ap[:, 2, :], L["decT"], L["decT"].ap[:], ALU.mult)
                        tt("pool", L["qdT"], L["qdT"].ap[:], Pp["qTn"], Pp["qTn"].ap[:], L["egrr"], L["egrr"].ap[:], ALU.mult)
                    tsc("pool", L["bkg"], L["bkg"].ap[:], Pp["ktok"], Pp["ktok"].ap[:], bg.ap[:, h:h + 1], None, ALU.mult, rd=[bg])
                    tsc("pool", L["bv"], L["bv"].ap[:], vtok, vtok.ap[:], smc.ap[:, h:h + 1], None, ALU.mult, rd=[smc])
                    tsc("pool", L["kt"], L["kt"].ap[:], Pp["ktok"], Pp["ktok"].ap[:], ktl.ap[:, col:col + 1], None, ALU.mult, rd=[ktl])
                    yield
                    yield from neumann(L, own)
                    mm(pA, pA.ap[:, 0, :], L["bkg"], L["bkg"].ap[:], L["X"], L["X"].ap[:])
                    P.op("act", lambda e: e.mul(out=L["nwT"].ap[:], in_=pA.ap[:, 0, :], mul=-1.0), reads=[pA], writes=[L["nwT"]])
                    yield
                    mm(pA, pA.ap[:, 1, :], L["X"], L["X"].ap[:], L["bv"], L["bv"].ap[:], start=True, stop=False)
                    mm(pA, pA.ap[:, 1, :], L["nwT"], L["nwT"].ap[:], S, S.ap, start=False, stop=True)
                    act(L["vnew"], L["vnew"].ap[:], pA, pA.ap[:, 1, :], AF.Copy)
                    yield
                    if own:
                        mm(pB, pB.ap[:, 3, :], L["qdT"], L["qdT"].ap[:], S, S.ap, start=True, stop=False)
                        mm(pB, pB.ap[:, 3, :], L["PT"], L["PT"].ap[:], L["vnew"], L["vnew"].ap[:], start=False, stop=True)
                    mm(pA, pA.ap[:, 2, :], L["kt"], L["kt"].ap[:], L["vnew"], L["vnew"].ap[:])
                    stt("dve", S, S.ap, S, S.ap, eglast.ap[:, col:col + 1], pA, pA.ap[:, 2, :], ALU.mult, ALU.add, rd=[eglast])
                    yield
                    if own:
                        zf = Pp[f"zf{e01}"]
                        st = L["st"]
                        act(L["junk"], L["junk"].ap[:], pB, pB.ap[:, 3, :], AF.Square, wr=[st], accum_out=st.ap[:, 0:1])
                        P.op("act", lambda e: e.activation(out=st.ap[:, 1:2], in_=st.ap[:, 0:1], func=AF.Ln, scale=1.0 / 128, bias=1e-6), reads=[L["junk"]], writes=[st])
                        P.op("act", lambda e: e.activation(out=st.ap[:, 2:3], in_=st.ap[:, 1:2], func=AF.Exp, scale=-0.5), reads=[st], writes=[st])
                        tp(pA, pA.ap[:, 0, :], zf, zf.ap[:])
                        tsc("dve", L["on"], L["on"].ap[:], pB, pB.ap[:, 3, :], st.ap[:, 2:3], None, ALU.mult, rd=[st])
                        yield
                        tt("dve", L["on"], L["on"].ap[:], pA, pA.ap[:, 0, :], L["on"], L["on"].ap[:], ALU.mult)
                        tp(pB, pB.ap[:, 1, :], L["on"], L["on"].ap[:])
                        g_ = gost[c % 2]
                        act(g_, g_.ap[:, h, :], pB, pB.ap[:, 1, :], AF.Copy)
                        yield
                return gen

            def ssd_unit(c, g, j, Pp, xt_, smc):
                own = c >= C0
                hs = 8 * g + j
                col = 32 + hs

                def gen(L):
                    pA = pB = L["pA"]
                    S = Ss[hs]
                    v = L["bv"]
                    tsc("pool", v, v.ap[:, 0:64], xt_, xt_.ap[:, (hs % 2) * 64:(hs % 2) * 64 + 64], smc.ap[:, 64 + hs:65 + hs], None, ALU.mult, rd=[smc])
                    tsc("pool", L["kt"], L["kt"].ap[:], Pp["Btok"], Pp["Btok"].ap[:], ktl.ap[:, col:col + 1], None, ALU.mult, rd=[ktl])
                    if own:
                        mm(pA, pA.ap[:, 0, :], smc, smc.ap[:, 192 + hs:193 + hs].to_broadcast([128, 128]), tri, tri.ap[:])
                        yield
                        tt("dve", L["junk"], L["junk"].ap[:], pA, pA.ap[:, 0, :], maskT, maskT.ap[:], ALU.add)
                        act(L["decT"], L["decT"].ap[:], L["junk"], L["junk"].ap[:], AF.Exp, rd=[ngcol], bias=ngcol.ap[:, col:col + 1])
                        act(L["egrr"], L["egrr"].ap[:], pA, pA.ap[:, 0, :], AF.Exp)
                        yield
                        tt("dve", L["PT"], L["PT"].ap[:], Pp["KQg"], Pp["KQg"].ap[:], L["decT"], L["decT"].ap[:], ALU.mult)
                        tt("pool", L["qdT"], L["qdT"].ap[:], Pp["CTf"], Pp["CTf"].ap[:], L["egrr"], L["egrr"].ap[:], ALU.mult)
                        yield
                        mm(pB, pB.ap[:, 3, 0:64], L["qdT"], L["qdT"].ap[:], S, S.ap, start=True, stop=False)
                        mm(pB, pB.ap[:, 3, 0:64], L["PT"], L["PT"].ap[:], v, v.ap[:, 0:64], start=False, stop=True)
                    mm(pA, pA.ap[:, 2, 0:64], L["kt"], L["kt"].ap[:], v, v.ap[:, 0:64])
                    stt("dve", S, S.ap, S, S.ap, eglast.ap[:, col:col + 1], pA, pA.ap[:, 2, 0:64], ALU.mult, ALU.add, rd=[eglast])
                    yield
                    if own:
                        stt("dve", yg, yg.ap[:, j * 64:(j + 1) * 64], xt_, xt_.ap[:, (hs % 2) * 64:(hs % 2) * 64 + 64], dskip.ap[:, hs:hs + 1],
                            pB, pB.ap[:, 3, 0:64], ALU.mult, ALU.add, rd=[dskip])
                        yield
                return gen

            for c in range(c_lo, c_hi):
                own = c >= C0
                smc = smb[c % 2]
                tsl = slice(c * 128, (c + 1) * 128)
                for q4 in range(4):
                    P.dma("sp", csl_g.ap[:, q4 * 16:(q4 + 1) * 16, :], csv[:, 48 + q4 * 16:48 + (q4 + 1) * 16, tsl], csls[0], writes=[csl_g] if q4 == 0 else [])
                csl_g.w = ("d", csls[0], csls[0].count)
                P.dma("sp", smc.ap[:], smtok[tsl, :], smsl[c % 2], writes=[smc])
                zslab = zsl[c % 2]
                if own:
                    osl_ = slice((c - C0) * 128, (c - C0 + 1) * 128)
                    for q4 in range(4):
                        P.dma("sp", zslab.ap[:, q4 * 16:(q4 + 1) * 16, :], zsv[:, q4 * 16:(q4 + 1) * 16, osl_], zsls[c % 2], writes=[zslab] if q4 == 0 else [])
                    zslab.w = ("d", zsls[c % 2], zsls[c % 2].count)
                for q4 in range(3):
                    P.dma("sp", csl_s.ap[:, q4 * 16:(q4 + 1) * 16, :], csv[:, q4 * 16:(q4 + 1) * 16, tsl], csls[1], writes=[csl_s] if q4 == 0 else [])
                csl_s.w = ("d", csls[1], csls[1].count)
                if c == C0:
                    P.op("dve", lambda e: e.tensor_scalar(out=Sg_t.ap[:], in0=Sg_t.ap[:], scalar1=flag.ap[:, 0:1], scalar2=None, op0=ALU.mult), reads=Sg + [flag], writes=Sg)
                    P.op("pool", lambda e: e.tensor_scalar(out=Ss_t.ap[:], in0=Ss_t.ap[:], scalar1=flag.ap[:, 0:1], scalar2=None, op0=ALU.mult), reads=Ss + [flag], writes=Ss)
                mm(pS, pS.ap[:, 0:32], tri, tri.ap[:], smc, smc.ap[:, 32:64])
                mm(pS, pS.ap[:, 32:96], tri, tri.ap[:], smc, smc.ap[:, 192:256])
                mm(pS, pS.ap[:, 128:160], ones, ones.ap[:], smc, smc.ap[:, 32:64])
                mm(pS, pS.ap[:, 160:224], ones, ones.ap[:], smc, smc.ap[:, 192:256])
                cp("dve", gcol, gcol.ap[:], pS, pS.ap[:, 0:96])
                cp("dve", glast, glast.ap[:], pS, pS.ap[:, 128:224])
                tsc("dve", ngcol, ngcol.ap[:], gcol, gcol.ap[:], -1.0, None, ALU.mult)
                tt("dve", ktl, ktl.ap[:], glast, glast.ap[:], gcol, gcol.ap[:], ALU.subtract)
                act(ktl, ktl.ap[:], ktl, ktl.ap[:], AF.Exp)
                act(eglast, eglast.ap[:], glast, glast.ap[:], AF.Exp)
                act(bg, bg.ap[:], gcol, gcol.ap[:, 0:32], AF.Exp)
                tt("dve", bg, bg.ap[:], bg, bg.ap[:], smc, smc.ap[:, 0:32], ALU.mult)
                tsc("dve", nbeta, nbeta.ap[:], smc, smc.ap[:, 0:32], -1.0, None, ALU.mult)

                def gdn_prep(hq, Pp):
                    st = Pp["st"]
                    cp("pool", Pp["kTf"], Pp["kTf"].ap[:], csl_g, csl_g.ap[:, 16 + hq, :])
                    tp(pP, pP.ap[:, 0, :], Pp["kTf"], Pp["kTf"].ap[:])
                    act(Pp["junk"], Pp["junk"].ap[:], pP, pP.ap[:, 0, :], AF.Square, wr=[st], accum_out=st.ap[:, 0:1])
                    P.op("act", lambda e: e.activation(out=st.ap[:, 1:2], in_=st.ap[:, 0:1], func=AF.Ln, bias=1e-6), reads=[Pp["junk"]], writes=[st])
                    P.op("act", lambda e: e.activation(out=st.ap[:, 2:3], in_=st.ap[:, 1:2], func=AF.Exp, scale=-0.5), reads=[st], writes=[st])
                    act(Pp["ktok"], Pp["ktok"].ap[:], pP, pP.ap[:, 0, :], AF.Copy, rd=[st], scale=st.ap[:, 2:3])
                    tp(pP, pP.ap[:, 1, :], Pp["ktok"], Pp["ktok"].ap[:])
                    cp("dve", Pp["kTn"], Pp["kTn"].ap[:], pP, pP.ap[:, 1, :])
                    if own:
                        cp("pool", Pp["qTf"], Pp["qTf"].ap[:], csl_g, csl_g.ap[:, hq, :])
                        tp(pP, pP.ap[:, 2, :], Pp["qTf"], Pp["qTf"].ap[:])
                        act(Pp["junk"], Pp["junk"].ap[:], pP, pP.ap[:, 2, :], AF.Square, wr=[st], accum_out=st.ap[:, 4:5])
                        P.op("act", lambda e: e.activation(out=st.ap[:, 5:6], in_=st.ap[:, 4:5], func=AF.Ln, bias=1e-6), reads=[Pp["junk"]], writes=[st])
                        P.op("act", lambda e: e.activation(out=st.ap[:, 6:7], in_=st.ap[:, 5:6], func=AF.Exp, scale=-0.5, bias=-2.4260151319598084), reads=[st], writes=[st])
                        act(Pp["qtok"], Pp["qtok"].ap[:], pP, pP.ap[:, 2, :], AF.Copy, rd=[st], scale=st.ap[:, 6:7])
                        tp(pP, pP.ap[:, 3, :], Pp["qtok"], Pp["qtok"].ap[:])
                        cp("dve", Pp["qTn"], Pp["qTn"].ap[:], pP, pP.ap[:, 3, :])
                    for e01 in range(2):
                        h = 2 * hq + e01
                        vT, vt = Pp[f"vTf{e01}"], Pp[f"vtok{e01}"]
                        cp("pool", vT, vT.ap[:], csl_g, csl_g.ap[:, 32 + h, :])
                        tp(pP, pP.ap[:, e01, :], vT, vT.ap[:])
                        cp("dve", vt, vt.ap[:], pP, pP.ap[:, e01, :])
                        if own:
                            zf = Pp[f"zf{e01}"]
                            cp("pool", zf, zf.ap[:], zslab, zslab.ap[:, 32 + h, :])

                if do_gdn:
                    for hq0 in range(0, 16, 3):
                        gens = []
                        for i3, hq in enumerate(range(hq0, min(16, hq0 + 3))):
                            gdn_prep(hq, prep[i3])
                            gens += [gdn_unit(c, 2 * hq + e01, prep[i3], e01, smc, zslab) for e01 in range(2)]
                        run_lanes(gens)
                    if own:
                        P.dma("act", gov[:, :, (c - C0) * 128:(c - C0 + 1) * 128], gost[c % 2].ap[:], gosl[c % 2], reads=[gost[c % 2]])

                for g in range(8 if do_ssd else 0):
                    Pp = prep[g % 3]
                    cp("pool", Pp["BTf"], Pp["BTf"].ap[:], csl_s, csl_s.ap[:, 32 + g, :])
                    tp(pP, pP.ap[:, 0, :], Pp["BTf"], Pp["BTf"].ap[:])
                    cp("dve", Pp["Btok"], Pp["Btok"].ap[:], pP, pP.ap[:, 0, :])
                    if own:
                        cp("pool", Pp["CTf"], Pp["CTf"].ap[:], csl_s, csl_s.ap[:, 40 + g, :])
                        mm(pP, pP.ap[:, 1, :], Pp["BTf"], Pp["BTf"].ap[:], Pp["CTf"], Pp["CTf"].ap[:])
                        act(Pp["KQg"], Pp["KQg"].ap[:], pP, pP.ap[:, 1, :], AF.Copy)
                    gens = []
                    for j2 in range(4):
                        xT_, xt_ = xTf[j2 % 2], xtk[j2]
                        cp("pool", xT_, xT_.ap[:], csl_s, csl_s.ap[:, 4 * g + j2, :])
                        tp(pP, pP.ap[:, 2 + (j2 % 2), :], xT_, xT_.ap[:])
                        cp("dve", xt_, xt_.ap[:], pP, pP.ap[:, 2 + (j2 % 2), :])
                        gens += [ssd_unit(c, g, 2 * j2 + e01, Pp, xt_, smc) for e01 in range(2)]
                    run_lanes(gens[0:4])
                    run_lanes(gens[4:8])
                    if own:
                        P.op("pool", lambda e, g=g, zslab=zslab: e.tensor_copy(out=zf4.ap[:], in_=zslab.ap[:, 4 * g:4 * g + 4, :]), reads=[zslab], writes=[zf4])
                        for i4 in range(4):
                            tp(pP, pP.ap[:, i4, :], zf4, zf4.ap[:, i4, :])
                        tt("dve", yz, yz.ap[:], pP, pP.ap[:].rearrange("p a b -> p (a b)"), yg, yg.ap[:], ALU.mult)
                        act(yg, yg.ap[:], yz, yz.ap[:], AF.Square, wr=[gst], accum_out=gst.ap[:, 0:1])
                        P.op("act", lambda e: e.activation(out=gst.ap[:, 1:2], in_=gst.ap[:, 0:1], func=AF.Ln, scale=1.0 / 512, bias=1e-6), reads=[yg], writes=[gst])
                        P.op("act", lambda e: e.activation(out=gst.ap[:, 2:3], in_=gst.ap[:, 1:2], func=AF.Exp, scale=-0.5), reads=[gst], writes=[gst])
                        tsc("dve", yz, yz.ap[:], yz, yz.ap[:], gst.ap[:, 2:3], None, ALU.mult, rd=[gst])
                        for i4 in range(4):
                            tp(pP, pP.ap[:, i4, :], yz, yz.ap[:, i4 * 128:(i4 + 1) * 128])
                        s_ = sost[c % 2]
                        act(s_, s_.ap[:, 4 * g:4 * g + 4, :], pP, pP.ap[:], AF.Copy)
                if own and do_ssd:
                    P.dma("act", sov[:, :, (c - C0) * 128:(c - C0 + 1) * 128], sost[c % 2].ap[:], sosl[c % 2], reads=[sost[c % 2]])
            P.replay()

        if upto >= 4:
          with ExitStack() as es:
            P = Prog(nc, es, "p4")
            big = sb(nc, es, "big", [128, 64, 512], BF16)
            xT = sb(nc, es, "xT", [128, 16, 512], F32)
            mT = sb(nc, es, "mT", [128, 16, 512], BF16)
            gt = [sb(nc, es, f"gt{i}", [128, 512], BF16) for i in range(4)]
            gsl = [P.slot() for _ in range(4)]
            tmpa = sb(nc, es, "tmpa", [128, 512], F32)
            tmpb = sb(nc, es, "tmpb", [128, 512], F32)
            rrr = sb(nc, es, "rrr", [128, 512], F32)
            wst = [sb(nc, es, f"w4st{i}", [128, 16, 128], F32) for i in range(3)]
            wsl = [P.slot() for _ in range(3)]
            wbf = [sb(nc, es, f"w4bf{i}", [128, 16, 128], BF16) for i in range(3)]
            xt4 = [sb(nc, es, f"x4t{i}", [128, D], F32) for i in range(2)]
            xsl = [P.slot() for _ in range(2)]
            st4 = [sb(nc, es, f"st4_{i}", [128, 4], F32) for i in range(2)]
            sq4 = sb(nc, es, "sq4", [128, D], F32)
            ysl = [P.slot() for _ in range(2)]
            bsl = P.slot()
            pm = [ps(nc, es, f"p4m{i}", [128, 512]) for i in range(4)]
            pt = [ps(nc, es, f"p4t{i}", [128, 4, 128]) for i in range(4)]
            wcount = [0]
            pcount = [0]

            def gemm(wview, KC, rhs_fn, rhs_bufs, scale=None):
                pb = pm[pcount[0] % 4]
                pcount[0] += 1
                for g0 in range(0, KC, 16):
                    n = min(16, KC - g0)
                    i = wcount[0] % 3
                    wcount[0] += 1
                    ws, wb = wst[i], wbf[i]
                    h = (n + 1) // 2
                    P.dma("sp", ws.ap[:, 0:h, :], wview(g0, h), wsl[i], writes=[ws])
                    P.dma("sp", ws.ap[:, h:n, :], wview(g0 + h, n - h), wsl[i])
                    ws.w = ("d", wsl[i], wsl[i].count)
                    if scale is None:
                        P.op("pool", lambda e, wb=wb, ws=ws, n=n: e.tensor_copy(out=wb.ap[:, 0:n, :], in_=ws.ap[:, 0:n, :]), reads=[ws], writes=[wb])
                    elif scale == "ssm":
                        P.op("pool", lambda e, wb=wb, ws=ws, n=n, g0=g0: e.tensor_tensor(
                            out=wb.ap[:, 0:n, :], in0=ws.ap[:, 0:n, :], in1=ssmnw.ap[:, g0:g0 + n].unsqueeze(2).to_broadcast([128, n, 128]), op=ALU.mult),
                            reads=[ws, ssmnw], writes=[wb])
                    else:
                        P.op("pool", lambda e, wb=wb, ws=ws, n=n: e.tensor_scalar(out=wb.ap[:, 0:n, :], in0=ws.ap[:, 0:n, :], scalar1=gdnnw.ap[:, 0:1],
                                                                                scalar2=None, op0=ALU.mult), reads=[ws, gdnnw], writes=[wb])
                    for k in range(n):
                        kc = g0 + k
                        P.op("pe", lambda e, pb=pb, wb=wb, k=k, kc=kc: e.matmul(pb.ap[:], lhsT=wb.ap[:, k, :], rhs=rhs_fn(kc), start=(kc == 0), stop=(kc == KC - 1)),
                             reads=[wb] + rhs_bufs, writes=[pb])
                return pb

            def wv(w, f0):
                v = w.rearrange("(kc p) f -> p kc f", p=128)
                return lambda k0, n: v[:, k0:k0 + n, f0:f0 + 128]

            sov = so.rearrange("(kc p) t -> p kc t", p=128)
            gov = go.rearrange("(kc p) t -> p kc t", p=128)
            for tb in range(4):
                ts_ = slice(tb * 512, (tb + 1) * 512)
                for half in range(4):
                    P.dma("sp", big.ap[:, half * 8:(half + 1) * 8, :], sov[:, half * 8:(half + 1) * 8, ts_], bsl, writes=[big] if half == 0 else [])
                for half in range(4):
                    P.dma("sp", big.ap[:, 32 + half * 8:32 + (half + 1) * 8, :], gov[:, half * 8:(half + 1) * 8, ts_], bsl)
                big.w = ("d", bsl, bsl.count)
                for tl in range(4):
                    ti = tb * 4 + tl
                    xb = xt4[ti % 2]
                    P.dma("sp", xb.ap[:, 0:1024], xin[TP + ti * 128:TP + (ti + 1) * 128, 0:1024], xsl[ti % 2], writes=[xb])
                    P.dma("sp", xb.ap[:, 1024:2048], xin[TP + ti * 128:TP + (ti + 1) * 128, 1024:2048], xsl[ti % 2])
                    xb.w = ("d", xsl[ti % 2], xsl[ti % 2].count)
                    for q in range(4):
                        pb = pt[q]
                        for j in range(4):
                            kc = q * 4 + j
                            P.op("pe", lambda e, pb=pb, j=j, kc=kc, xb=xb: e.transpose(pb.ap[:, j, :], xb.ap[:, kc * 128:(kc + 1) * 128], ident.ap[:]),
                                 reads=[xb, ident], writes=[pb])
                        eng = "act" if q % 2 == 0 else "dve"
                        if eng == "act":
                            P.op("act", lambda e, pb=pb, q=q, tl=tl: e.activation(out=xT.ap[:, q * 4:(q + 1) * 4, tl * 128:(tl + 1) * 128], in_=pb.ap[:], func=AF.Copy),
                                 reads=[pb], writes=[xT])
                        else:
                            P.op("dve", lambda e, pb=pb, q=q, tl=tl: e.tensor_copy(out=xT.ap[:, q * 4:(q + 1) * 4, tl * 128:(tl + 1) * 128], in_=pb.ap[:]),
                                 reads=[pb], writes=[xT])
                for f in range(16):
                    ga, gb = gt[(2 * f) % 4], gt[(2 * f + 1) % 4]
                    P.dma("sp", ga.ap[:], zs[8192 + f * 128:8192 + (f + 1) * 128, ts_], gsl[(2 * f) % 4], writes=[ga])
                    P.dma("sp", gb.ap[:], zs[10240 + f * 128:10240 + (f + 1) * 128, ts_], gsl[(2 * f + 1) % 4], writes=[gb])
                    pa = gemm(wv(w_ssm, f * 128), 32, lambda kc: big.ap[:, kc, :], [big], scale="ssm")
                    P.op("dve", lambda e, pa=pa, ga=ga: e.tensor_tensor(out=tmpa.ap[:], in0=pa.ap[:], in1=ga.ap[:], op=ALU.mult), reads=[pa, ga], writes=[tmpa])
                    pb_ = gemm(wv(w_gdn, f * 128), 32, lambda kc: big.ap[:, 32 + kc, :], [big], scale="gdn")
                    P.op("dve", lambda e, pb_=pb_, gb=gb: e.tensor_tensor(out=tmpb.ap[:], in0=pb_.ap[:], in1=gb.ap[:], op=ALU.mult), reads=[pb_, gb], writes=[tmpb])
                    P.op("pool", lambda e, f=f: e.tensor_tensor(out=mT.ap[:, f, :], in0=tmpa.ap[:], in1=tmpb.ap[:], op=ALU.add), reads=[tmpa, tmpb], writes=[mT])
                for f in range(16):
                    pc = gemm(wv(w_o, f * 128), 16, lambda kc: mT.ap[:, kc, :], [mT])
                    P.op("dve", lambda e, pc=pc, f=f: e.scalar_tensor_tensor(out=xT.ap[:, f, :], in0=pc.ap[:], scalar=ada.ap[:, 32 + f:33 + f], in1=xT.ap[:, f, :],
                                                                           op0=ALU.mult, op1=ALU.add), reads=[pc, xT, ada], writes=[xT])

                def rstd_rows():
                    pr = pm[pcount[0] % 4]
                    pcount[0] += 1
                    for kc in range(16):
                        P.op("act", lambda e, kc=kc: e.activation(out=tmpa.ap[:], in_=xT.ap[:, kc, :], func=AF.Square), reads=[xT], writes=[tmpa])
                        P.op("pe", lambda e, pr=pr, kc=kc: e.matmul(pr.ap[:], lhsT=ones.ap[:], rhs=tmpa.ap[:], start=(kc == 0), stop=(kc == 15)),
                             reads=[tmpa, ones], writes=[pr])
                    P.op("act", lambda e, pr=pr: e.activation(out=rrr.ap[:], in_=pr.ap[:], func=AF.Ln, scale=1.0 / D, bias=1e-6), reads=[pr], writes=[rrr])
                    P.op("act", lambda e: e.activation(out=rrr.ap[:], in_=rrr.ap[:], func=AF.Exp, scale=-0.5), reads=[rrr], writes=[rrr])

                rstd_rows()
                for kc in range(16):
                    P.op("dve", lambda e, kc=kc: e.tensor_tensor(out=tmpb.ap[:], in0=xT.ap[:, kc, :], in1=rrr.ap[:], op=ALU.mult), reads=[xT, rrr], writes=[tmpb])
                    P.op("dve", lambda e, kc=kc: e.tensor_scalar(out=mT.ap[:, kc, :], in0=tmpb.ap[:], scalar1=A2.ap[:, kc:kc + 1], scalar2=ada.ap[:, 48 + kc:49 + kc],
                                                                op0=ALU.mult, op1=ALU.add), reads=[tmpb, A2, ada], writes=[mT])
                for j in range(44):
                    pg = gemm(wv(w_gu, j * 128), 16, lambda kc: mT.ap[:, kc, :], [mT])
                    P.op("act", lambda e, pg=pg: e.activation(out=tmpa.ap[:], in_=pg.ap[:], func=AF.Silu), reads=[pg], writes=[tmpa])
                    pu = gemm(wv(w_gu, FFN + j * 128), 16, lambda kc: mT.ap[:, kc, :], [mT])
                    P.op("dve", lambda e, pu=pu, j=j: e.tensor_tensor(out=big.ap[:, j, :], in0=pu.ap[:], in1=tmpa.ap[:], op=ALU.mult), reads=[pu, tmpa], writes=[big])
                for f in range(16):
                    pd = gemm(wv(w_dn, f * 128), 44, lambda kc: big.ap[:, kc, :], [big])
                    P.op("dve", lambda e, pd=pd, f=f: e.scalar_tensor_tensor(out=xT.ap[:, f, :], in0=pd.ap[:], scalar=ada.ap[:, 80 + f:81 + f], in1=xT.ap[:, f, :],
                                                                           op0=ALU.mult, op1=ALU.add), reads=[pd, xT, ada], writes=[xT])
                for tl in range(4):
                    ti = tb * 4 + tl
                    xb = xt4[ti % 2]
                    s1 = st4[ti % 2]
                    for q in range(4):
                        pb = pt[q]
                        for j in range(4):
                            kc = q * 4 + j
                            P.op("pe", lambda e, pb=pb, j=j, kc=kc, tl=tl: e.transpose(pb.ap[:, j, :], xT.ap[:, kc, tl * 128:(tl + 1) * 128], ident.ap[:]),
                                 reads=[xT, ident], writes=[pb])
                        eng = "act" if q % 2 == 0 else "dve"
                        if eng == "act":
                            P.op("act", lambda e, pb=pb, q=q, xb=xb: e.activation(out=xb.ap[:, q * 512:(q + 1) * 512], in_=pb.ap[:].rearrange("p a b -> p (a b)"), func=AF.Copy),
                                 reads=[pb], writes=[xb])
                        else:
                            P.op("dve", lambda e, pb=pb, q=q, xb=xb: e.tensor_copy(out=xb.ap[:, q * 512:(q + 1) * 512], in_=pb.ap[:].rearrange("p a b -> p (a b)")),
                                 reads=[pb], writes=[xb])
                    P.op("act", lambda e, xb=xb, s1=s1: e.activation(out=sq4.ap[:], in_=xb.ap[:], func=AF.Square, accum_out=s1.ap[:, 0:1]), reads=[xb], writes=[sq4, s1])
                    P.op("act", lambda e, s1=s1: e.activation(out=s1.ap[:, 1:2], in_=s1.ap[:, 0:1], func=AF.Ln, scale=1.0 / D, bias=1e-6), reads=[s1], writes=[s1])
                    P.op("act", lambda e, s1=s1: e.activation(out=s1.ap[:, 2:3], in_=s1.ap[:, 1:2], func=AF.Exp, scale=-0.5), reads=[s1], writes=[s1])
                    P.op("dve", lambda e, xb=xb, s1=s1: e.scalar_tensor_tensor(out=xb.ap[:], in0=xb.ap[:], scalar=s1.ap[:, 2:3], in1=fnw_row.ap[:], op0=ALU.mult, op1=ALU.mult),
                         reads=[xb, s1, fnw_row], writes=[xb])
                    P.dma("sp", y_out[ti * 128:(ti + 1) * 128, :], xb.ap[:], ysl[ti % 2], reads=[xb])
            P.replay()

        if dbg:
            with ExitStack() as es:
                P = Prog(nc, es, "dbg")
                dsl = P.slot()
                d_cs = nc.dram_tensor("dbg_cs", [14336, 256], BF16, kind="ExternalOutput").ap()
                d_sm = nc.dram_tensor("dbg_sm", [TT, 256], F32, kind="ExternalOutput").ap()
                d_zs = nc.dram_tensor("dbg_zs", [12288, 128], BF16, kind="ExternalOutput").ap()
                d_ada = nc.dram_tensor("dbg_ada", [128, 96], F32, kind="ExternalOutput").ap()
                for t in range(14):
                    P.dma("sp", d_cs[t * 1024:(t + 1) * 1024, :], cs[t * 1024:(t + 1) * 1024, 1920:2176], dsl)
                for t in range(4):
                    P.dma("sp", d_sm[t * 1024:(t + 1) * 1024, :], smtok[t * 1024:(t + 1) * 1024, :], dsl)
                for t in range(12):
                    P.dma("sp", d_zs[t * 1024:(t + 1) * 1024, :], zs[t * 1024:(t + 1) * 1024, 0:128], dsl)
                P.dma("sp", d_ada, ada.ap[:], dsl)
                d_so = nc.dram_tensor("dbg_so", [4096, TQ], BF16, kind="ExternalOutput").ap()
                d_go = nc.dram_tensor("dbg_go", [4096, TQ], BF16, kind="ExternalOutput").ap()
                for t in range(4):
                    P.dma("sp", d_so[t * 1024:(t + 1) * 1024, :], so[t * 1024:(t + 1) * 1024, :], dsl)
                    P.dma("sp", d_go[t * 1024:(t + 1) * 1024, :], go[t * 1024:(t + 1) * 1024, :], dsl)
                P.replay()
    return nc


def _layout_inputs(inp):
    f = lambda a: np.ascontiguousarray(a, dtype=np.float32)
    col = lambda v, n: f(np.asarray(v).reshape(n, 128).T)
    x, c = inp["x"], inp["c"]
    cwS = np.asarray(inp["ssm_conv_w"][0])
    cwG = np.asarray(inp["gdn_conv_w"][0])
    cw_all = np.concatenate([cwS, cwG], axis=1)
    cw_l = f(cw_all.reshape(4, 112, 128).transpose(2, 1, 0))
    cb_all = np.concatenate([np.asarray(inp["ssm_conv_b"][0]), np.zeros(8192, np.float32)])
    cb_l = col(cb_all, 112)
    z32 = np.zeros(32, np.float32)
    smallbias = f(np.concatenate([z32, inp["gdn_dt_bias"][0], inp["ssm_dt_bias"][0]]).reshape(128, 1))
    alog = f(np.concatenate([z32, inp["gdn_a_log"][0], inp["ssm_a_log"][0]]).reshape(128, 1))
    idx = np.arange(128)
    common = {
        "w_ada": f(inp["w_ada"][0]), "b_ada_col": col(inp["b_ada"][0], 96),
        "nmw": col(inp["norm_mix_w"][0], 16), "nfw": col(inp["norm_ffn_w"][0], 16),
        "fnw_row": f(np.broadcast_to(np.asarray(inp["final_norm_w"])[None, :], (128, D))),
        "w_in": f(inp["w_in"][0]), "cw": cw_l, "cb": cb_l, "smallbias": smallbias, "alog": alog,
        "dskip_row": f(np.broadcast_to(np.asarray(inp["ssm_d_skip"][0])[None, :], (128, 64))),
        "ssmnw_col": col(inp["ssm_norm_w"][0], 32), "gdnnw_col": f(np.asarray(inp["gdn_norm_w"][0]).reshape(128, 1)),
        "w_ssm": f(inp["w_ssm_proj"][0]), "w_gdn": f(inp["w_gdn_proj"][0]), "w_o": f(inp["w_o"][0]),
        "w_gu": f(inp["w_gate_up"][0]), "w_dn": f(inp["w_down"][0]),
        "ident": np.eye(128, dtype=np.float32),
        "tri": (idx[:, None] <= idx[None, :]).astype(np.float32),
        "ones": np.ones((128, 128), np.float32),
        "maskT": np.where(idx[None, :] >= idx[:, None], 0.0, NEG).astype(np.float32),
        "maskLs": np.where(idx[None, :] < idx[:, None], 0.0, NEG).astype(np.float32),
    }
    maps = []
    for core in range(8):
        b, r = core // 2, core % 2
        xb = np.asarray(x[b])
        own = xb[r * TQ:(r + 1) * TQ]
        pre = xb[0:TP]
        m = dict(common)
        m["xin"] = f(np.concatenate([pre, own], axis=0))
        m["flag"] = np.full((128, 1), float(r), np.float32)
        m["c_col"] = col(c[b], 16)
        maps.append(m)
    return maps


_NC_CACHE = {}


def kernel(**inputs):
    inp = {k: np.asarray(v) for k, v in inputs.items()}
    if "nc" not in _NC_CACHE:
        _NC_CACHE["nc"] = build_program()
    nc = _NC_CACHE["nc"]
    maps = _layout_inputs(inp)
    res = run_bass_kernel_spmd(nc, maps, core_ids=list(range(8)))
    out = np.zeros((4, SEQ, D), np.float32)
    for core in range(8):
        b, r = core // 2, core % 2
        out[b, r * TQ:(r + 1) * TQ] = res.results[core]["y"]
    return out
```

```python
from contextlib import ExitStack
import numpy as np
import os as _os
import concourse.bass as bass
import concourse.mybir as mybir
from concourse.bass_utils import run_bass_kernel_spmd

F32 = mybir.dt.float32
BF16 = mybir.dt.bfloat16
AF = mybir.ActivationFunctionType
ALU = mybir.AluOpType

D = 2048
SEQ = 4096
TP = 2048
TQ = 2048
TT = TP + TQ
IN_DIM = 26752
FFN = 5632
NEG = -30000.0
OFF_Z, OFF_XBC, OFF_DT, OFF_QKV, OFF_GZ, OFF_BETA, OFF_A, OFF_GS, OFF_GG = (
    0, 4096, 10240, 10304, 18496, 22592, 22624, 22656, 24704)
ENGS = ("pe", "act", "dve", "pool", "sp")


class Buf:
    ALL = []

    def __init__(self, ap, const=False, excl=False):
        self.ap = ap
        self.excl = excl
        self.w = None
        self.r = {}
        self.const = const
        Buf.ALL.append(self)

    def __getitem__(self, k):
        return self.ap[k]


class Slot:
    def __init__(self, sem):
        self.sem = sem
        self.count = 0


class Prog:
    def __init__(self, nc, es, tag):
        self.nc = nc
        self.tag = tag
        self.streams = {e: [] for e in ENGS}
        self.waited = {e: {} for e in ENGS}
        self.sems = {e: es.enter_context(nc.semaphore(f"{tag}_s_{e}")) for e in ENGS}
        self.done = es.enter_context(nc.semaphore(f"{tag}_done"))
        self.es = es
        self.nslot = 0
        self.dma_toks = []
        for b in Buf.ALL:
            b.w = None
            b.r = {}

    def slot(self):
        self.nslot += 1
        return Slot(self.es.enter_context(self.nc.semaphore(f"{self.tag}_d{self.nslot}")))

    def _waits(self, eng, deps):
        out = []
        for d in deps:
            if d is None:
                continue
            if d[0] == "e":
                if d[1] == eng and eng == "pe":
                    continue
                key, val = ("e", d[1]), d[2]
            else:
                key, val = ("d", id(d[1])), d[2]
            if self.waited[eng].get(key, -1) >= val:
                continue
            self.waited[eng][key] = val
            out.append(d)
        return out

    def _collect(self, reads, writes, deps):
        al = list(deps)
        for b in reads:
            al.append(b.w)
        for b in writes:
            al.append(b.w)
            al.extend(b.r.values())
        return al

    def _update(self, tok, reads, writes):
        for b in reads:
            if not b.const:
                b.r[(tok[0], tok[1] if tok[0] == "e" else id(tok[1]))] = tok
        for b in writes:
            b.w = tok
            b.r = {}

    def op(self, eng, fn, reads=(), writes=(), deps=()):
        writes = list(writes) + [b for b in reads if b.excl]
        reads = [b for b in reads if not b.excl]
        waits = self._waits(eng, self._collect(reads, writes, deps))
        idx = len(self.streams[eng])
        self.streams[eng].append(["op", fn, waits, False, None])
        tok = ("e", eng, idx)
        self._update(tok, reads, writes)
        return tok

    def dma(self, q, out, in_, slot, reads=(), writes=(), deps=()):
        waits = self._waits(q, self._collect(reads, writes, deps))
        slot.count += 16
        self.streams[q].append(["dma", (out, in_), waits, False, slot])
        tok = ("d", slot, slot.count)
        self._update(tok, reads, writes)
        self.dma_toks.append(tok)
        return tok

    def replay(self):
        nc = self.nc
        for e in ENGS:
            for rec in self.streams[e]:
                for w in rec[2]:
                    if w[0] == "e":
                        self.streams[w[1]][w[2]][3] = True
        last = {}
        for e in ENGS:
            ops = [i for i, r in enumerate(self.streams[e]) if r[0] == "op"]
            if ops:
                self.streams[e][ops[-1]][3] = True
                last[e] = ops[-1]
        counts = {}
        for e in ENGS:
            c = 0
            cl = []
            for rec in self.streams[e]:
                if rec[0] == "op" and rec[3]:
                    c += 1
                cl.append(c)
            counts[e] = cl
        final_d = {}
        for t in self.dma_toks:
            final_d[id(t[1])] = (t[1], max(final_d.get(id(t[1]), (None, 0))[1], t[2]))

        def run(eng_name, eng):
            for rec in self.streams[eng_name]:
                for w in rec[2]:
                    if w[0] == "e":
                        eng.wait_ge(self.sems[w[1]], counts[w[1]][w[2]])
                    else:
                        eng.wait_ge(w[1].sem, w[2])
                if rec[0] == "op":
                    ins = rec[1](eng)
                    if rec[3]:
                        ins.then_inc(self.sems[eng_name], 1)
                else:
                    o, i = rec[1]
                    eng.dma_start(out=o, in_=i).then_inc(rec[4].sem, 16)
            if eng_name == "sp":
                for e2, li in last.items():
                    eng.wait_ge(self.sems[e2], counts[e2][li])
                for sl, v in final_d.values():
                    eng.wait_ge(sl.sem, v)
                eng.sem_inc(self.done, 1)
            else:
                eng.wait_ge(self.done, 1)

        with nc.Block() as block:
            block.sync(lambda e: run("sp", e))
            block.tensor(lambda e: run("pe", e))
            block.scalar(lambda e: run("act", e))
            block.vector(lambda e: run("dve", e))
            block.gpsimd(lambda e: run("pool", e))


_UID = [0]


def sb(nc, es, name, shape, dt, const=False):
    _UID[0] += 1
    return Buf(es.enter_context(nc.sbuf_tensor(f"s{_UID[0]}_{name}", list(shape), dt)), const=const)


def ps(nc, es, name, shape, dt=F32):
    _UID[0] += 1
    return Buf(es.enter_context(nc.psum_tensor(f"p{_UID[0]}_{name}", list(shape), dt)), excl=True)


def build_program(dbg=False, upto=99, p1only=False, mini=False):
    nc = bass.Bass("TRN2", target_bir_lowering=False)
    I = {}

    def din(name, shape, dt=F32):
        if mini and name in ("w_ada", "w_in", "w_ssm", "w_gdn", "w_o", "w_gu", "w_dn"):
            shape = [128, 128]
        I[name] = nc.dram_tensor(name, list(shape), dt, kind="ExternalInput").ap()
        return I[name]

    xin = din("xin", [TT, D])
    flag_d = din("flag", [128, 1])
    c_col_d = din("c_col", [128, 16])
    w_ada = din("w_ada", [D, 6 * D])
    b_ada_col_d = din("b_ada_col", [128, 96])
    nmw_d = din("nmw", [128, 16])
    nfw_d = din("nfw", [128, 16])
    fnw_row_d = din("fnw_row", [128, D])
    w_in = din("w_in", [D, IN_DIM])
    cw_d = din("cw", [128, 112, 4])
    cb_d = din("cb", [128, 112])
    smallbias_d = din("smallbias", [128, 1])
    alog_d = din("alog", [128, 1])
    dskip_d = din("dskip_row", [128, 64])
    ssmnw_d = din("ssmnw_col", [128, 32])
    gdnnw_d = din("gdnnw_col", [128, 1])
    w_ssm = din("w_ssm", [4096, D])
    w_gdn = din("w_gdn", [4096, D])
    w_o = din("w_o", [D, D])
    w_gu = din("w_gu", [D, 2 * FFN])
    w_dn = din("w_dn", [FFN, D])
    ident_d = din("ident", [128, 128])
    tri_d = din("tri", [128, 128])
    ones_d = din("ones", [128, 128])
    maskT_d = din("maskT", [128, 128])
    maskLs_d = din("maskLs", [128, 128])
    y_out = nc.dram_tensor("y", [TQ, D], F32, kind="ExternalOutput").ap()

    cs = nc.dram_tensor("cs", [14336, TT], BF16).ap()
    zs = nc.dram_tensor("zs", [12288, TQ], BF16).ap()
    smtok = nc.dram_tensor("smtok", [TT, 256], F32).ap()
    so = nc.dram_tensor("so", [4096, TQ], BF16).ap()
    go = nc.dram_tensor("go", [4096, TQ], BF16).ap()
    dbg_outs = {}

    with ExitStack() as top:
        ident = sb(nc, top, "ident", [128, 128], F32, const=True)
        identb = sb(nc, top, "identb", [128, 128], BF16, const=True)
        tri = sb(nc, top, "tri", [128, 128], F32, const=True)
        ones = sb(nc, top, "ones", [128, 128], F32, const=True)
        maskT = sb(nc, top, "maskT", [128, 128], F32, const=True)
        maskLs = sb(nc, top, "maskLs", [128, 128], F32, const=True)
        flag = sb(nc, top, "flagt", [128, 1], F32, const=True)
        ada = sb(nc, top, "ada", [128, 96], F32, const=True)
        A1 = sb(nc, top, "A1", [128, 16], F32, const=True)
        A2 = sb(nc, top, "A2", [128, 16], F32, const=True)
        fnw_row = sb(nc, top, "fnw_row", [128, D], F32, const=True)
        cw = sb(nc, top, "cw", [128, 112, 4], F32, const=True)
        cbias = sb(nc, top, "cbias", [128, 112], F32, const=True)
        smallbias = sb(nc, top, "smallbias", [128, 1], F32, const=True)
        negA = sb(nc, top, "negA", [128, 1], F32, const=True)
        dskip = sb(nc, top, "dskip", [128, 64], F32, const=True)
        ssmnw = sb(nc, top, "ssmnw", [128, 32], F32, const=True)
        gdnnw = sb(nc, top, "gdnnw", [128, 1], F32, const=True)
        hist = sb(nc, top, "hist", [128, 112, 3], F32)

        with ExitStack() as es:
            P = Prog(nc, es, "p0")
            ld = P.slot()
            for buf, src in ((ident, ident_d), (tri, tri_d), (ones, ones_d), (maskT, maskT_d), (maskLs, maskLs_d),
                             (flag, flag_d), (fnw_row, fnw_row_d), (cw, cw_d), (cbias, cb_d), (smallbias, smallbias_d),
                             (dskip, dskip_d), (ssmnw, ssmnw_d), (gdnnw, gdnnw_d)):
                P.dma("sp", buf.ap[:], src, ld, writes=[buf])
            ccol = sb(nc, es, "ccol", [128, 16], F32)
            cact = sb(nc, es, "cact", [128, 16], F32)
            bada = sb(nc, es, "bada", [128, 96], F32)
            nmw = sb(nc, es, "nmw", [128, 16], F32)
            nfw = sb(nc, es, "nfw", [128, 16], F32)
            alog = sb(nc, es, "alog", [128, 1], F32)
            tmp16 = sb(nc, es, "tmp16", [128, 16], F32)
            for buf, src in ((ccol, c_col_d), (bada, b_ada_col_d), (nmw, nmw_d), (nfw, nfw_d), (alog, alog_d)):
                P.dma("sp", buf.ap[:], src, ld, writes=[buf])
            P.op("act", lambda e: e.activation(out=cact.ap[:], in_=ccol.ap[:], func=AF.Silu), reads=[ccol], writes=[cact])
            P.op("act", lambda e: e.activation(out=negA.ap[:], in_=alog.ap[:], func=AF.Exp), reads=[alog], writes=[negA])
            P.op("dve", lambda e: e.tensor_scalar(out=negA.ap[:], in0=negA.ap[:], scalar1=-1.0, scalar2=None, op0=ALU.mult),
                 reads=[negA], writes=[negA])
            P.op("dve", lambda e: e.tensor_copy(out=identb.ap[:], in_=ident.ap[:]), reads=[ident], writes=[identb])
            P.op("dve", lambda e: e.memset(hist.ap[:], 0.0), writes=[hist])
            wa = [sb(nc, es, f"wa{i}", [128, 16, 128], F32) for i in range(3)]
            wsl = [P.slot() for _ in range(3)]
            pada = ps(nc, es, "pada", [128, 96])
            wav = w_ada.rearrange("(kc p) f -> p kc f", p=128)
            if mini:
                P.op("pe", lambda e: e.matmul(pada.ap[:, 0:96], lhsT=ident.ap[:], rhs=ones.ap[:, 0:96], start=True, stop=True),
                     reads=[ident, ones], writes=[pada])
            for ft in range(0 if mini else 96):
                wb = wa[ft % 3]
                P.dma("sp", wb.ap[:, 0:8, :], wav[:, 0:8, ft * 128:(ft + 1) * 128], wsl[ft % 3], writes=[wb])
                P.dma("sp", wb.ap[:, 8:16, :], wav[:, 8:16, ft * 128:(ft + 1) * 128], wsl[ft % 3], writes=[])
                wb.w = ("d", wsl[ft % 3], wsl[ft % 3].count)
                for kc in range(16):
                    P.op("pe", lambda e, wb=wb, kc=kc, ft=ft: e.matmul(pada.ap[:, ft:ft + 1], lhsT=wb.ap[:, kc, :],
                                                                      rhs=cact.ap[:, kc:kc + 1], start=(kc == 0), stop=(kc == 15)),
                         reads=[wb, cact], writes=[pada])
            P.op("dve", lambda e: e.tensor_tensor(out=ada.ap[:], in0=pada.ap[:], in1=bada.ap[:], op=ALU.add),
                 reads=[pada, bada], writes=[ada])
            for (Ax, nw, c0) in ((A1, nmw, 16), (A2, nfw, 64)):
                P.op("dve", lambda e, c0=c0: e.tensor_scalar(out=tmp16.ap[:], in0=ada.ap[:, c0:c0 + 16], scalar1=1.0, scalar2=None,
                                                            op0=ALU.add), reads=[ada], writes=[tmp16])
                P.op("dve", lambda e, Ax=Ax, nw=nw: e.tensor_tensor(out=Ax.ap[:], in0=tmp16.ap[:], in1=nw.ap[:], op=ALU.mult),
                     reads=[tmp16, nw], writes=[Ax])
            P.replay()

        for pas in range(2):
            if upto < 1 + pas:
                continue
            t0 = pas * TP
            NT = 16
            import os as _os
            NT1 = int(_os.environ.get('NT1', '16'))
            CUT = int(_os.environ.get('CUT', '99'))
            with ExitStack() as es:
                P = Prog(nc, es, f"p2{pas}")
                hT = sb(nc, es, "hT", [128, 16, 2048], BF16)
                xt = [sb(nc, es, f"xt{i}", [128, D], F32) for i in range(2)]
                xsl = [P.slot() for _ in range(2)]
                sq = sb(nc, es, "sqjunk", [128, D], F32)
                st1 = [sb(nc, es, f"st1_{i}", [128, 4], F32) for i in range(2)]
                ptr = [ps(nc, es, f"ptr{i}", [128, 4, 128]) for i in range(4)]
                p1_last = {}
                for ti in range(NT1):
                    xb = xt[ti % 2]
                    s1 = st1[ti % 2]
                    P.dma("sp", xb.ap[:, 0:1024], xin[t0 + ti * 128:t0 + (ti + 1) * 128, 0:1024], xsl[ti % 2], writes=[xb])
                    P.dma("sp", xb.ap[:, 1024:2048], xin[t0 + ti * 128:t0 + (ti + 1) * 128, 1024:2048], xsl[ti % 2])
                    xb.w = ("d", xsl[ti % 2], xsl[ti % 2].count)
                    P.op("act", lambda e, xb=xb, s1=s1: e.activation(out=sq.ap[:], in_=xb.ap[:], func=AF.Square, accum_out=s1.ap[:, 0:1]),
                         reads=[xb], writes=[sq, s1])
                    if CUT < 1:
                        continue
                    P.op("act", lambda e, s1=s1: e.activation(out=s1.ap[:, 1:2], in_=s1.ap[:, 0:1], func=AF.Ln, scale=1.0 / D, bias=1e-6),
                         reads=[s1], writes=[s1])
                    P.op("act", lambda e, s1=s1: e.activation(out=s1.ap[:, 2:3], in_=s1.ap[:, 1:2], func=AF.Exp, scale=-0.5),
                         reads=[s1], writes=[s1])
                    if CUT < 2:
                        continue
                    P.op("dve", lambda e, xb=xb, s1=s1: e.tensor_scalar(out=xb.ap[:], in0=xb.ap[:], scalar1=s1.ap[:, 2:3], scalar2=None,
                                                                       op0=ALU.mult), reads=[xb, s1], writes=[xb])
                    for q in range(4):
                        if CUT < 3:
                            continue
                        pb = ptr[q]
                        for j in range(4):
                            kc = q * 4 + j
                            P.op("pe", lambda e, pb=pb, j=j, kc=kc, xb=xb: e.transpose(pb.ap[:, j, :], xb.ap[:, kc * 128:(kc + 1) * 128], ident.ap[:]),
                                 reads=[xb, ident], writes=[pb])
                        for j in range(4):
                            if CUT < 4:
                                continue
                            kc = q * 4 + j
                            eng = "act" if (q % 2 == 0) else "dve"
                            if eng == "act":
                                P.op("act", lambda e, pb=pb, j=j, kc=kc, ti=ti: e.activation(
                                    out=hT.ap[:, kc, ti * 128:(ti + 1) * 128], in_=pb.ap[:, j, :], func=AF.Identity,
                                    scale=A1.ap[:, kc:kc + 1], bias=ada.ap[:, kc:kc + 1]), reads=[pb, A1, ada])
                                p1_last["act"] = P.streams["act"] and ("e", "act", len(P.streams["act"]) - 1)
                            else:
                                P.op("dve", lambda e, pb=pb, j=j, kc=kc, ti=ti: e.tensor_scalar(
                                    out=hT.ap[:, kc, ti * 128:(ti + 1) * 128], in0=pb.ap[:, j, :], scalar1=A1.ap[:, kc:kc + 1],
                                    scalar2=ada.ap[:, kc:kc + 1], op0=ALU.mult, op1=ALU.add), reads=[pb, A1, ada])
                                p1_last["dve"] = ("e", "dve", len(P.streams["dve"]) - 1)
                tiles = []
                for t in range(48):
                    kd = "hist" if (pas == 0 and t >= 40) else "conv"
                    tiles.append((kd, t, [(OFF_XBC + t * 128, 128, 0)]))
                for t in range(64):
                    kd = "hist" if (pas == 0 and t < 16) else "conv"
                    tiles.append((kd, 48 + t, [(OFF_QKV + t * 128, 128, 0)]))
                tiles.append(("small", 0, [(OFF_BETA, 64, 0), (OFF_DT, 64, 64)]))
                if p1only:
                    tiles = []
                if pas == 1:
                    for t in range(32):
                        tiles.append(("silu", t, [(OFF_Z + t * 128, 128, 0)]))
                    for t in range(32):
                        tiles.append(("silu", 32 + t, [(OFF_GZ + t * 128, 128, 0)]))
                    for t in range(32):
                        tiles.append(("sig", 64 + t, [(OFF_GS + t * 128, 128, 0)]))
                wst = [sb(nc, es, f"wst{i}", [128, 16, 128], F32) for i in range(3)]
                wsl = [P.slot() for _ in range(3)]
                wbf = [sb(nc, es, f"wbf{i}", [128, 16, 128], BF16) for i in range(2)]
                pmm = [ps(nc, es, f"pmm{i}", [128, 512]) for i in range(4)]
                cbuf = [sb(nc, es, f"cbuf{i}", [128, 3 + 2048], F32) for i in range(2)]
                acc = [sb(nc, es, f"acc{i}", [128, 2048], F32) for i in range(2)]
                ost = [sb(nc, es, f"ost{i}", [128, 2048], BF16) for i in range(2)]
                osl = [P.slot() for _ in range(2)]
                sm1 = sb(nc, es, "sm1", [128, 2048], F32)
                sm2 = sb(nc, es, "sm2", [128, 2048], F32)
                sm3 = sb(nc, es, "sm3", [128, 2048], F32)
                smt = [sb(nc, es, f"smt{i}", [128, 256], F32) for i in range(2)]
                smsl = [P.slot() for _ in range(2)]
                winv = w_in.rearrange("(kc p) f -> p kc f", p=128)
                nconv = 0
                for tix, (kind, tidx, segs) in enumerate(tiles):
                    ws = wst[tix % 3]
                    first = True
                    for (c0, ncol, f0) in segs:
                        for half in range(2):
                            P.dma("sp", ws.ap[:, half * 8:(half + 1) * 8, f0:f0 + ncol], winv[:, half * 8:(half + 1) * 8, c0:c0 + ncol],
                                  wsl[tix % 3], writes=[ws] if first else [])
                            first = False
                    ws.w = ("d", wsl[tix % 3], wsl[tix % 3].count)
                    wb = wbf[tix % 2]
                    P.op("pool", lambda e, wb=wb, ws=ws: e.tensor_copy(out=wb.ap[:], in_=ws.ap[:]), reads=[ws], writes=[wb])
                    if kind in ("conv", "hist"):
                        cbf = cbuf[nconv % 2]
                        ac = acc[nconv % 2]
                        nconv += 1
                        if pas == 0:
                            P.op("pool", lambda e, cbf=cbf: e.memset(cbf.ap[:, 0:3], 0.0), writes=[cbf])
                        else:
                            P.op("pool", lambda e, cbf=cbf, tidx=tidx: e.tensor_scalar(out=cbf.ap[:, 0:3], in0=hist.ap[:, tidx, :], scalar1=flag.ap[:, 0:1],
                                                                                    scalar2=None, op0=ALU.mult), reads=[hist], writes=[cbf])
                    for blk in range(4):
                        if kind == "hist" and blk < 3:
                            continue
                        pb = pmm[(tix * 4 + blk) % 4]
                        for kc in range(16):
                            P.op("pe", lambda e, pb=pb, wb=wb, kc=kc, blk=blk: e.matmul(pb.ap[:], lhsT=wb.ap[:, kc, :], rhs=hT.ap[:, kc, blk * 512:(blk + 1) * 512],
                                                                                  start=(kc == 0), stop=(kc == 15)), reads=[wb], writes=[pb], deps=list(p1_last.values()))
                        sl_ = slice(blk * 512, (blk + 1) * 512)
                        if kind in ("conv", "hist"):
                            P.op("act", lambda e, pb=pb, cbf=cbf, blk=blk: e.activation(out=cbf.ap[:, 3 + blk * 512:3 + (blk + 1) * 512], in_=pb.ap[:], func=AF.Copy),
                                 reads=[pb], writes=[cbf])
                        elif kind == "small":
                            P.op("act", lambda e, pb=pb, sl_=sl_: e.activation(out=sm1.ap[:, sl_], in_=pb.ap[:], func=AF.Identity, bias=smallbias.ap[:, 0:1]),
                                 reads=[pb, smallbias], writes=[sm1])
                        else:
                            o_ = ost[tix % 2]
                            fn_ = AF.Silu if kind == "silu" else AF.Sigmoid
                            P.op("act", lambda e, pb=pb, o_=o_, sl_=sl_, fn_=fn_: e.activation(out=o_.ap[:, sl_], in_=pb.ap[:], func=fn_),
                                 reads=[pb], writes=[o_])
                    if kind == "hist":
                        P.op("pool", lambda e, cbf=cbf, tidx=tidx: e.tensor_copy(out=hist.ap[:, tidx, :], in_=cbf.ap[:, 2048:2051]), reads=[cbf], writes=[hist])
                    elif kind == "conv":
                        P.op("dve", lambda e, cbf=cbf, ac=ac, tidx=tidx: e.tensor_scalar(out=ac.ap[:], in0=cbf.ap[:, 0:2048], scalar1=cw.ap[:, tidx, 0:1],
                                                                                    scalar2=cbias.ap[:, tidx:tidx + 1], op0=ALU.mult, op1=ALU.add),
                             reads=[cbf, cw, cbias], writes=[ac])
                        for k in range(1, 4):
                            P.op("dve", lambda e, cbf=cbf, ac=ac, tidx=tidx, k=k: e.scalar_tensor_tensor(out=ac.ap[:], in0=cbf.ap[:, k:k + 2048], scalar=cw.ap[:, tidx, k:k + 1],
                                                                                                 in1=ac.ap[:], op0=ALU.mult, op1=ALU.add),
                                 reads=[cbf, cw, ac], writes=[ac])
                        if pas == 0:
                            P.op("pool", lambda e, cbf=cbf, tidx=tidx: e.tensor_copy(out=hist.ap[:, tidx, :], in_=cbf.ap[:, 2048:2051]), reads=[cbf], writes=[hist])
                        o_ = ost[tix % 2]
                        P.op("act", lambda e, ac=ac, o_=o_: e.activation(out=o_.ap[:], in_=ac.ap[:], func=AF.Silu), reads=[ac], writes=[o_])
                        P.dma("act", cs[tidx * 128:(tidx + 1) * 128, t0:t0 + 2048], o_.ap[:], osl[tix % 2], reads=[o_])
                    elif kind == "small":
                        P.op("act", lambda e: e.activation(out=sm2.ap[:], in_=sm1.ap[:], func=AF.Exp), reads=[sm1], writes=[sm2])
                        P.op("act", lambda e: e.activation(out=sm2.ap[:], in_=sm2.ap[:], func=AF.Ln, bias=1.0), reads=[sm2], writes=[sm2])
                        P.op("act", lambda e: e.activation(out=sm1.ap[0:32, :], in_=sm1.ap[0:32, :], func=AF.Sigmoid), reads=[sm1, sm2], writes=[sm1])
                        P.op("dve", lambda e: e.tensor_scalar(out=sm3.ap[:], in0=sm2.ap[:], scalar1=negA.ap[:, 0:1], scalar2=None, op0=ALU.mult),
                             reads=[sm2, negA], writes=[sm3])
                        P.op("dve", lambda e: e.tensor_copy(out=sm1.ap[32:64, :], in_=sm3.ap[32:64, :]), reads=[sm3, sm1], writes=[sm1])
                        P.op("dve", lambda e: e.tensor_copy(out=sm1.ap[64:128, :], in_=sm2.ap[64:128, :]), reads=[sm2, sm1], writes=[sm1])
                        for ti in range(NT):
                            pb = ptr[ti % 4]
                            stt = smt[ti % 2]
                            P.op("pe", lambda e, pb=pb, ti=ti: e.transpose(pb.ap[:, 0, :], sm1.ap[:, ti * 128:(ti + 1) * 128], ident.ap[:]), reads=[sm1], writes=[pb])
                            P.op("pe", lambda e, pb=pb, ti=ti: e.transpose(pb.ap[:, 1, :], sm3.ap[:, ti * 128:(ti + 1) * 128], ident.ap[:]), reads=[sm3], writes=[pb])
                            P.op("dve", lambda e, pb=pb, stt=stt: e.tensor_copy(out=stt.ap[:], in_=pb.ap[:, 0:2, :].rearrange("p a b -> p (a b)")), reads=[pb], writes=[stt])
                            P.dma("act", smtok[t0 + ti * 128:t0 + (ti + 1) * 128, :], stt.ap[:], smsl[ti % 2], reads=[stt])
                    else:
                        P.dma("act", zs[tidx * 128:(tidx + 1) * 128, :], o_.ap[:], osl[tix % 2], reads=[o_])
                P.replay()

        if upto >= 3:
          with ExitStack() as es:
            P = Prog(nc, es, "p3")
            KL = 6
            NCH = TT // 128
            C0 = TP // 128
            c_lo = int(_os.environ.get("MIX_CLO", "0"))
            c_hi = int(_os.environ.get("MIX_CHI", str(NCH)))
            do_gdn = int(_os.environ.get("MIX_GDN", "1"))
            do_ssd = int(_os.environ.get("MIX_SSD", "1"))

            def T(name, shape, dt=F32):
                return sb(nc, es, name, shape, dt)

            def mm(ob, oap, lb, lap, rb, rap, start=True, stop=True):
                return P.op("pe", lambda e: e.matmul(oap, lhsT=lap, rhs=rap, start=start, stop=stop), reads=[lb, rb], writes=[ob])

            def tp(ob, oap, ib, iap):
                return P.op("pe", lambda e: e.transpose(oap, iap, ident.ap[:]), reads=[ib, ident], writes=[ob])

            def act(ob, oap, ib, iap, func, rd=(), wr=(), **kw):
                return P.op("act", lambda e: e.activation(out=oap, in_=iap, func=func, **kw), reads=[ib] + list(rd), writes=[ob] + list(wr))

            def cp(eng, ob, oap, ib, iap):
                return P.op(eng, lambda e: e.tensor_copy(out=oap, in_=iap), reads=[ib], writes=[ob])

            def tt(eng, ob, oap, ab, aap, bb, bap, op):
                return P.op(eng, lambda e: e.tensor_tensor(out=oap, in0=aap, in1=bap, op=op), reads=[ab, bb], writes=[ob])

            def tsc(eng, ob, oap, ab, aap, s1, s2, op0, op1=None, rd=()):
                if op1 is None:
                    return P.op(eng, lambda e: e.tensor_scalar(out=oap, in0=aap, scalar1=s1, scalar2=None, op0=op0), reads=[ab] + list(rd), writes=[ob])
                return P.op(eng, lambda e: e.tensor_scalar(out=oap, in0=aap, scalar1=s1, scalar2=s2, op0=op0, op1=op1), reads=[ab] + list(rd), writes=[ob])

            def stt(eng, ob, oap, ab, aap, sc, cb_, cap, op0, op1, rd=()):
                return P.op(eng, lambda e: e.scalar_tensor_tensor(out=oap, in0=aap, scalar=sc, in1=cap, op0=op0, op1=op1),
                            reads=[ab, cb_] + list(rd), writes=[ob])

            csl_s = T("csl_s", [128, 48, 128], BF16)
            csl_g = T("csl_g", [128, 64, 128], BF16)
            csls = [P.slot() for _ in range(2)]
            zsl = [T(f"zsl{i}", [128, 64, 128], BF16) for i in range(1)] * 2
            zsls = [P.slot() for _ in range(2)]
            smb = [T(f"smb{i}", [128, 256]) for i in range(2)]
            smsl = [P.slot() for _ in range(2)]
            Sg_t = T("Sg", [128, 32, 128])
            Ss_t = T("Ss", [128, 64, 64])
            Sg = [Buf(Sg_t.ap[:, h, :]) for h in range(32)]
            Ss = [Buf(Ss_t.ap[:, h, :]) for h in range(64)]
            gcol, ngcol, glast, ktl, eglast = [T(n, [128, 96]) for n in ("gcol", "ngcol", "glast", "ktl", "eglast")]
            bg, nbeta = T("bg", [128, 32]), T("nbeta", [128, 32])
            gost = [T(f"gost{i}", [128, 32, 128], BF16) for i in range(1)] * 2
            sost = [T(f"sost{i}", [128, 32, 128], BF16) for i in range(1)] * 2
            gosl = [P.slot() for _ in range(2)]
            sosl = [P.slot() for _ in range(2)]
            pS = ps(nc, es, "pS", [128, 512])
            pP = ps(nc, es, "pP", [128, 4, 128])
            lanes = []
            for i in range(KL):
                L = {"pA": ps(nc, es, f"pA{i}", [128, 4, 128])}
                for n in ("t1", "Ma", "Mb", "Ba", "Bb", "X", "PT", "qdT", "bkg", "bv", "nwT", "vnew", "kt", "on"):
                    L[n] = T(f"L{i}_{n}", [128, 128])
                L["decs"], L["decT"], L["egrr"] = L["Mb"], L["Bb"], L["nwT"]
                L["st"] = T(f"L{i}_st", [128, 4])
                L["junk"] = L["on"]
                lanes.append(L)
            prep = []
            for i in range(3):
                Pp = {n: T(f"pp{i}_{n}", [128, 128]) for n in ("kTf", "ktok", "kTn", "qtok", "qTn", "vTf0", "vTf1", "vtok0", "vtok1", "zf0", "zf1", "junk")}
                Pp["qTf"] = Pp["kTf"]
                Pp["BTf"], Pp["Btok"], Pp["CTf"], Pp["KQg"] = Pp["kTn"], Pp["ktok"], Pp["qTn"], Pp["qtok"]
                Pp["st"] = T(f"pp{i}_st", [128, 8])
                prep.append(Pp)
            xTf = [T(f"xTf{i}", [128, 128]) for i in range(2)]
            xtk = [T(f"xtk{i}", [128, 128]) for i in range(4)]
            yg = T("yg", [128, 512])
            yz = T("yz", [128, 512])
            zf4 = T("zf4", [128, 4, 128])
            gst = T("gst", [128, 4])
            csv = cs.rearrange("(t p) k -> p t k", p=128)
            zsv = zs.rearrange("(t p) k -> p t k", p=128)
            sov = so.rearrange("(t p) k -> p t k", p=128)
            gov = go.rearrange("(t p) k -> p t k", p=128)

            P.op("dve", lambda e: e.memset(Sg_t.ap[:], 0.0), writes=Sg)
            P.op("pool", lambda e: e.memset(Ss_t.ap[:], 0.0), writes=Ss)

            def run_lanes(gens):
                gens = list(gens)
                while gens:
                    batch, gens = gens[:KL], gens[KL:]
                    live = [g(lanes[i]) for i, g in enumerate(batch)]
                    while live:
                        for g in list(live):
                            try:
                                next(g)
                            except StopIteration:
                                live.remove(g)

            def neumann(L, own):
                pA = pB = L["pA"]
                tp(pA, pA.ap[:, 3, :], L["Ma"], L["Ma"].ap[:])
                act(L["Ba"], L["Ba"].ap[:], pA, pA.ap[:, 3, :], AF.Copy)
                tt("dve", L["X"], L["X"].ap[:], pA, pA.ap[:, 3, :], ident, ident.ap[:], ALU.add)
                yield
                M, B, Mn, Bn = L["Ma"], L["Ba"], L["Mb"], L["Bb"]
                for j in range(1, 7):
                    mm(pB, pB.ap[:, 0, :], B, B.ap[:], M, M.ap[:])
                    if j < 6:
                        mm(pB, pB.ap[:, 1, :], M, M.ap[:], B, B.ap[:])
                    act(Mn, Mn.ap[:], pB, pB.ap[:, 0, :], AF.Copy)
                    if j < 6:
                        cp("dve", Bn, Bn.ap[:], pB, pB.ap[:, 1, :])
                    yield
                    mm(pB, pB.ap[:, 2, :], Mn, Mn.ap[:], L["X"], L["X"].ap[:])
                    tt("dve", L["X"], L["X"].ap[:], pB, pB.ap[:, 2, :], L["X"], L["X"].ap[:], ALU.add)
                    yield
                    M, B, Mn, Bn = Mn, Bn, M, B

            def gdn_unit(c, h, Pp, e01, smc, zslab):
                own = c >= C0
                col = h

                def gen(L):
                    pA = pB = L["pA"]
                    S = Sg[h]
                    vtok = Pp[f"vtok{e01}"]
                    mm(pA, pA.ap[:, 0, :], smc, smc.ap[:, 32 + h:33 + h].to_broadcast([128, 128]), tri, tri.ap[:])
                    mm(pA, pA.ap[:, 1, :], Pp["kTn"], Pp["kTn"].ap[:], Pp["kTn"], Pp["kTn"].ap[:])
                    if own:
                        mm(pA, pA.ap[:, 2, :], Pp["kTn"], Pp["kTn"].ap[:], Pp["qTn"], Pp["qTn"].ap[:])
                    yield
                    stt("dve", L["t1"], L["t1"].ap[:], pA, pA.ap[:, 0, :], -1.0, maskLs, maskLs.ap[:], ALU.mult, ALU.add)
                    act(L["decs"], L["decs"].ap[:], L["t1"], L["t1"].ap[:], AF.Exp, rd=[gcol], bias=gcol.ap[:, col:col + 1])
                    if own:
                        tt("dve", L["junk"], L["junk"].ap[:], pA, pA.ap[:, 0, :], maskT, maskT.ap[:], ALU.add)
                        act(L["decT"], L["decT"].ap[:], L["junk"], L["junk"].ap[:], AF.Exp, rd=[ngcol], bias=ngcol.ap[:, col:col + 1])
                        act(L["egrr"], L["egrr"].ap[:], pA, pA.ap[:, 0, :], AF.Exp)
                    yield
                    stt("dve", L["Ma"], L["Ma"].ap[:], pA, pA.ap[:, 1, :], nbeta.ap[:, h:h + 1], L["decs"], L["decs"].ap[:], ALU.mult, ALU.mult, rd=[nbeta])
                    if own:
                        tt("dve", L["PT"], L["PT"].ap[:], pA, pA.ap[:, 2, :], L["decT"], L["decT"].ap[:], ALU.mult)
                        tt("dve", L["qdT"], L["qdT"].ap[:], Pp["qTn"], Pp["qTn"].ap[:], L["egrr"], L["egrr"].ap[:], ALU.mult)
                    act(L["bkg"], L["bkg"].ap[:], Pp["ktok"], Pp["ktok"].ap[:], AF.Copy, rd=[bg], scale=bg.ap[:, h:h + 1])
                    act(L["bv"], L["bv"].ap[:], vtok, vtok.ap[:], AF.Copy, rd=[smc], scale=smc.ap[:, h:h + 1])
                    act(L["kt"], L["kt"].ap[:], Pp["ktok"], Pp["ktok"].ap[:], AF.Copy, rd=[ktl], scale=ktl.ap[:, col:col + 1])
                    yield
                    yield from neumann(L, own)
                    mm(pA, pA.ap[:, 0, :], L["bkg"], L["bkg"].ap[:], L["X"], L["X"].ap[:])
                    P.op("act", lambda e: e.mul(out=L["nwT"].ap[:], in_=pA.ap[:, 0, :], mul=-1.0), reads=[pA], writes=[L["nwT"]])
                    yield
                    mm(pA, pA.ap[:, 1, :], L["X"], L["X"].ap[:], L["bv"], L["bv"].ap[:], start=True, stop=False)
                    mm(pA, pA.ap[:, 1, :], L["nwT"], L["nwT"].ap[:], S, S.ap, start=False, stop=True)
                    act(L["vnew"], L["vnew"].ap[:], pA, pA.ap[:, 1, :], AF.Copy)
                    yield
                    if own:
                        mm(pB, pB.ap[:, 3, :], L["qdT"], L["qdT"].ap[:], S, S.ap, start=True, stop=False)
                        mm(pB, pB.ap[:, 3, :], L["PT"], L["PT"].ap[:], L["vnew"], L["vnew"].ap[:], start=False, stop=True)
                    mm(pA, pA.ap[:, 2, :], L["kt"], L["kt"].ap[:], L["vnew"], L["vnew"].ap[:])
                    stt("dve", S, S.ap, S, S.ap, eglast.ap[:, col:col + 1], pA, pA.ap[:, 2, :], ALU.mult, ALU.add, rd=[eglast])
                    yield
                    if own:
                        zf = Pp[f"zf{e01}"]
                        st = L["st"]
                        act(L["junk"], L["junk"].ap[:], pB, pB.ap[:, 3, :], AF.Square, wr=[st], accum_out=st.ap[:, 0:1])
                        P.op("act", lambda e: e.activation(out=st.ap[:, 1:2], in_=st.ap[:, 0:1], func=AF.Ln, scale=1.0 / 128, bias=1e-6), reads=[L["junk"]], writes=[st])
                        P.op("act", lambda e: e.activation(out=st.ap[:, 2:3], in_=st.ap[:, 1:2], func=AF.Exp, scale=-0.5), reads=[st], writes=[st])
                        tp(pA, pA.ap[:, 0, :], zf, zf.ap[:])
                        tsc("dve", L["on"], L["on"].ap[:], pB, pB.ap[:, 3, :], st.ap[:, 2:3], None, ALU.mult, rd=[st])
                        yield
                        tt("dve", L["on"], L["on"].ap[:], pA, pA.ap[:, 0, :], L["on"], L["on"].ap[:], ALU.mult)
                        tp(pB, pB.ap[:, 1, :], L["on"], L["on"].ap[:])
                        g_ = gost[c % 2]
                        act(g_, g_.ap[:, h, :], pB, pB.ap[:, 1, :], AF.Copy)
                        yield
                return gen

            def ssd_unit(c, g, j, Pp, xt_, smc):
                own = c >= C0
                hs = 8 * g + j
                col = 32 + hs

                def gen(L):
                    pA = pB = L["pA"]
                    S = Ss[hs]
                    v = L["bv"]
                    act(v, v.ap[:, 0:64], xt_, xt_.ap[:, (hs % 2) * 64:(hs % 2) * 64 + 64], AF.Copy, rd=[smc], scale=smc.ap[:, 64 + hs:65 + hs])
                    act(L["kt"], L["kt"].ap[:], Pp["Btok"], Pp["Btok"].ap[:], AF.Copy, rd=[ktl], scale=ktl.ap[:, col:col + 1])
                    if own:
                        mm(pA, pA.ap[:, 0, :], smc, smc.ap[:, 192 + hs:193 + hs].to_broadcast([128, 128]), tri, tri.ap[:])
                        yield
                        tt("dve", L["junk"], L["junk"].ap[:], pA, pA.ap[:, 0, :], maskT, maskT.ap[:], ALU.add)
                        act(L["decT"], L["decT"].ap[:], L["junk"], L["junk"].ap[:], AF.Exp, rd=[ngcol], bias=ngcol.ap[:, col:col + 1])
                        act(L["egrr"], L["egrr"].ap[:], pA, pA.ap[:, 0, :], AF.Exp)
                        yield
                        tt("dve", L["PT"], L["PT"].ap[:], Pp["KQg"], Pp["KQg"].ap[:], L["decT"], L["decT"].ap[:], ALU.mult)
                        tt("dve", L["qdT"], L["qdT"].ap[:], Pp["CTf"], Pp["CTf"].ap[:], L["egrr"], L["egrr"].ap[:], ALU.mult)
                        yield
                        mm(pB, pB.ap[:, 3, 0:64], L["qdT"], L["qdT"].ap[:], S, S.ap, start=True, stop=False)
                        mm(pB, pB.ap[:, 3, 0:64], L["PT"], L["PT"].ap[:], v, v.ap[:, 0:64], start=False, stop=True)
                    mm(pA, pA.ap[:, 2, 0:64], L["kt"], L["kt"].ap[:], v, v.ap[:, 0:64])
                    stt("dve", S, S.ap, S, S.ap, eglast.ap[:, col:col + 1], pA, pA.ap[:, 2, 0:64], ALU.mult, ALU.add, rd=[eglast])
                    yield
                    if own:
                        stt("dve", yg, yg.ap[:, j * 64:(j + 1) * 64], xt_, xt_.ap[:, (hs % 2) * 64:(hs % 2) * 64 + 64], dskip.ap[:, hs:hs + 1],
                            pB, pB.ap[:, 3, 0:64], ALU.mult, ALU.add, rd=[dskip])
                        yield
                return gen

            for c in range(c_lo, c_hi):
                own = c >= C0
                smc = smb[c % 2]
                tsl = slice(c * 128, (c + 1) * 128)
                for q4 in range(4):
                    P.dma("sp", csl_g.ap[:, q4 * 16:(q4 + 1) * 16, :], csv[:, 48 + q4 * 16:48 + (q4 + 1) * 16, tsl], csls[0], writes=[csl_g] if q4 == 0 else [])
                csl_g.w = ("d", csls[0], csls[0].count)
                P.dma("sp", smc.ap[:], smtok[tsl, :], smsl[c % 2], writes=[smc])
                zslab = zsl[c % 2]
                if own:
                    osl_ = slice((c - C0) * 128, (c - C0 + 1) * 128)
                    for q4 in range(4):
                        P.dma("sp", zslab.ap[:, q4 * 16:(q4 + 1) * 16, :], zsv[:, q4 * 16:(q4 + 1) * 16, osl_], zsls[c % 2], writes=[zslab] if q4 == 0 else [])
                    zslab.w = ("d", zsls[c % 2], zsls[c % 2].count)
                for q4 in range(3):
                    P.dma("sp", csl_s.ap[:, q4 * 16:(q4 + 1) * 16, :], csv[:, q4 * 16:(q4 + 1) * 16, tsl], csls[1], writes=[csl_s] if q4 == 0 else [])
                csl_s.w = ("d", csls[1], csls[1].count)
                if c == C0:
                    P.op("dve", lambda e: e.tensor_scalar(out=Sg_t.ap[:], in0=Sg_t.ap[:], scalar1=flag.ap[:, 0:1], scalar2=None, op0=ALU.mult), reads=Sg + [flag], writes=Sg)
                    P.op("pool", lambda e: e.tensor_scalar(out=Ss_t.ap[:], in0=Ss_t.ap[:], scalar1=flag.ap[:, 0:1], scalar2=None, op0=ALU.mult), reads=Ss + [flag], writes=Ss)
                mm(pS, pS.ap[:, 0:32], tri, tri.ap[:], smc, smc.ap[:, 32:64])
                mm(pS, pS.ap[:, 32:96], tri, tri.ap[:], smc, smc.ap[:, 192:256])
                mm(pS, pS.ap[:, 128:160], ones, ones.ap[:], smc, smc.ap[:, 32:64])
                mm(pS, pS.ap[:, 160:224], ones, ones.ap[:], smc, smc.ap[:, 192:256])
                cp("dve", gcol, gcol.ap[:], pS, pS.ap[:, 0:96])
                cp("dve", glast, glast.ap[:], pS, pS.ap[:, 128:224])
                tsc("dve", ngcol, ngcol.ap[:], gcol, gcol.ap[:], -1.0, None, ALU.mult)
                tt("dve", ktl, ktl.ap[:], glast, glast.ap[:], gcol, gcol.ap[:], ALU.subtract)
                act(ktl, ktl.ap[:], ktl, ktl.ap[:], AF.Exp)
                act(eglast, eglast.ap[:], glast, glast.ap[:], AF.Exp)
                act(bg, bg.ap[:], gcol, gcol.ap[:, 0:32], AF.Exp)
                tt("dve", bg, bg.ap[:], bg, bg.ap[:], smc, smc.ap[:, 0:32], ALU.mult)
                tsc("dve", nbeta, nbeta.ap[:], smc, smc.ap[:, 0:32], -1.0, None, ALU.mult)

                def gdn_prep(hq, Pp):
                    st = Pp["st"]
                    cp("pool", Pp["kTf"], Pp["kTf"].ap[:], csl_g, csl_g.ap[:, 16 + hq, :])
                    tp(pP, pP.ap[:, 0, :], Pp["kTf"], Pp["kTf"].ap[:])
                    act(Pp["junk"], Pp["junk"].ap[:], pP, pP.ap[:, 0, :], AF.Square, wr=[st], accum_out=st.ap[:, 0:1])
                    P.op("act", lambda e: e.activation(out=st.ap[:, 1:2], in_=st.ap[:, 0:1], func=AF.Ln, bias=1e-6), reads=[Pp["junk"]], writes=[st])
                    P.op("act", lambda e: e.activation(out=st.ap[:, 2:3], in_=st.ap[:, 1:2], func=AF.Exp, scale=-0.5), reads=[st], writes=[st])
                    act(Pp["ktok"], Pp["ktok"].ap[:], pP, pP.ap[:, 0, :], AF.Copy, rd=[st], scale=st.ap[:, 2:3])
                    tp(pP, pP.ap[:, 1, :], Pp["ktok"], Pp["ktok"].ap[:])
                    cp("dve", Pp["kTn"], Pp["kTn"].ap[:], pP, pP.ap[:, 1, :])
                    if own:
                        cp("pool", Pp["qTf"], Pp["qTf"].ap[:], csl_g, csl_g.ap[:, hq, :])
                        tp(pP, pP.ap[:, 2, :], Pp["qTf"], Pp["qTf"].ap[:])
                        act(Pp["junk"], Pp["junk"].ap[:], pP, pP.ap[:, 2, :], AF.Square, wr=[st], accum_out=st.ap[:, 4:5])
                        P.op("act", lambda e: e.activation(out=st.ap[:, 5:6], in_=st.ap[:, 4:5], func=AF.Ln, bias=1e-6), reads=[Pp["junk"]], writes=[st])
                        P.op("act", lambda e: e.activation(out=st.ap[:, 6:7], in_=st.ap[:, 5:6], func=AF.Exp, scale=-0.5, bias=-2.4260151319598084), reads=[st], writes=[st])
                        act(Pp["qtok"], Pp["qtok"].ap[:], pP, pP.ap[:, 2, :], AF.Copy, rd=[st], scale=st.ap[:, 6:7])
                        tp(pP, pP.ap[:, 3, :], Pp["qtok"], Pp["qtok"].ap[:])
                        cp("dve", Pp["qTn"], Pp["qTn"].ap[:], pP, pP.ap[:, 3, :])
                    for e01 in range(2):
                        h = 2 * hq + e01
                        vT, vt = Pp[f"vTf{e01}"], Pp[f"vtok{e01}"]
                        cp("pool", vT, vT.ap[:], csl_g, csl_g.ap[:, 32 + h, :])
                        tp(pP, pP.ap[:, e01, :], vT, vT.ap[:])
                        cp("dve", vt, vt.ap[:], pP, pP.ap[:, e01, :])
                        if own:
                            zf = Pp[f"zf{e01}"]
                            cp("pool", zf, zf.ap[:], zslab, zslab.ap[:, 32 + h, :])

                if do_gdn:
                    for hq0 in range(0, 16, 3):
                        gens = []
                        for i3, hq in enumerate(range(hq0, min(16, hq0 + 3))):
                            gdn_prep(hq, prep[i3])
                            gens += [gdn_unit(c, 2 * hq + e01, prep[i3], e01, smc, zslab) for e01 in range(2)]
                        run_lanes(gens)
                    if own:
                        P.dma("act", gov[:, :, (c - C0) * 128:(c - C0 + 1) * 128], gost[c % 2].ap[:], gosl[c % 2], reads=[gost[c % 2]])

                for g in range(8 if do_ssd else 0):
                    Pp = prep[g % 3]
                    cp("pool", Pp["BTf"], Pp["BTf"].ap[:], csl_s, csl_s.ap[:, 32 + g, :])
                    tp(pP, pP.ap[:, 0, :], Pp["BTf"], Pp["BTf"].ap[:])
                    cp("dve", Pp["Btok"], Pp["Btok"].ap[:], pP, pP.ap[:, 0, :])
                    if own:
                        cp("pool", Pp["CTf"], Pp["CTf"].ap[:], csl_s, csl_s.ap[:, 40 + g, :])
                        mm(pP, pP.ap[:, 1, :], Pp["BTf"], Pp["BTf"].ap[:], Pp["CTf"], Pp["CTf"].ap[:])
                        act(Pp["KQg"], Pp["KQg"].ap[:], pP, pP.ap[:, 1, :], AF.Copy)
                    gens = []
                    for j2 in range(4):
                        xT_, xt_ = xTf[j2 % 2], xtk[j2]
                        cp("pool", xT_, xT_.ap[:], csl_s, csl_s.ap[:, 4 * g + j2, :])
                        tp(pP, pP.ap[:, 2 + (j2 % 2), :], xT_, xT_.ap[:])
                        cp("dve", xt_, xt_.ap[:], pP, pP.ap[:, 2 + (j2 % 2), :])
                        gens += [ssd_unit(c, g, 2 * j2 + e01, Pp, xt_, smc) for e01 in range(2)]
                    run_lanes(gens[0:4])
                    run_lanes(gens[4:8])
                    if own:
                        P.op("pool", lambda e, g=g, zslab=zslab: e.tensor_copy(out=zf4.ap[:], in_=zslab.ap[:, 4 * g:4 * g + 4, :]), reads=[zslab], writes=[zf4])
                        for i4 in range(4):
                            tp(pP, pP.ap[:, i4, :], zf4, zf4.ap[:, i4, :])
                        tt("dve", yz, yz.ap[:], pP, pP.ap[:].rearrange("p a b -> p (a b)"), yg, yg.ap[:], ALU.mult)
                        act(yg, yg.ap[:], yz, yz.ap[:], AF.Square, wr=[gst], accum_out=gst.ap[:, 0:1])
                        P.op("act", lambda e: e.activation(out=gst.ap[:, 1:2], in_=gst.ap[:, 0:1], func=AF.Ln, scale=1.0 / 512, bias=1e-6), reads=[yg], writes=[gst])
                        P.op("act", lambda e: e.activation(out=gst.ap[:, 2:3], in_=gst.ap[:, 1:2], func=AF.Exp, scale=-0.5), reads=[gst], writes=[gst])
                        tsc("dve", yz, yz.ap[:], yz, yz.ap[:], gst.ap[:, 2:3], None, ALU.mult, rd=[gst])
                        for i4 in range(4):
                            tp(pP, pP.ap[:, i4, :], yz, yz.ap[:, i4 * 128:(i4 + 1) * 128])
                        s_ = sost[c % 2]
                        act(s_, s_.ap[:, 4 * g:4 * g + 4, :], pP, pP.ap[:], AF.Copy)
                if own and do_ssd:
                    P.dma("act", sov[:, :, (c - C0) * 128:(c - C0 + 1) * 128], sost[c % 2].ap[:], sosl[c % 2], reads=[sost[c % 2]])
            P.replay()

        if upto >= 4:
          with ExitStack() as es:
            P = Prog(nc, es, "p4")
            big = sb(nc, es, "big", [128, 64, 512], BF16)
            xT = sb(nc, es, "xT", [128, 16, 512], F32)
            mT = sb(nc, es, "mT", [128, 16, 512], BF16)
            gt = [sb(nc, es, f"gt{i}", [128, 512], BF16) for i in range(4)]
            gsl = [P.slot() for _ in range(4)]
            tmpa = sb(nc, es, "tmpa", [128, 512], F32)
            tmpb = sb(nc, es, "tmpb", [128, 512], F32)
            rrr = sb(nc, es, "rrr", [128, 512], F32)
            wst = [sb(nc, es, f"w4st{i}", [128, 16, 128], F32) for i in range(3)]
            wsl = [P.slot() for _ in range(3)]
            wbf = [sb(nc, es, f"w4bf{i}", [128, 16, 128], BF16) for i in range(3)]
            xt4 = [sb(nc, es, f"x4t{i}", [128, D], F32) for i in range(2)]
            xsl = [P.slot() for _ in range(2)]
            st4 = [sb(nc, es, f"st4_{i}", [128, 4], F32) for i in range(2)]
            sq4 = sb(nc, es, "sq4", [128, D], F32)
            ysl = [P.slot() for _ in range(2)]
            bsl = P.slot()
            pm = [ps(nc, es, f"p4m{i}", [128, 512]) for i in range(4)]
            pt = [ps(nc, es, f"p4t{i}", [128, 4, 128]) for i in range(4)]
            wcount = [0]
            pcount = [0]

            def gemm(wview, KC, rhs_fn, rhs_bufs, scale=None):
                pb = pm[pcount[0] % 4]
                pcount[0] += 1
                for g0 in range(0, KC, 16):
                    n = min(16, KC - g0)
                    i = wcount[0] % 3
                    wcount[0] += 1
                    ws, wb = wst[i], wbf[i]
                    h = (n + 1) // 2
                    P.dma("sp", ws.ap[:, 0:h, :], wview(g0, h), wsl[i], writes=[ws])
                    P.dma("sp", ws.ap[:, h:n, :], wview(g0 + h, n - h), wsl[i])
                    ws.w = ("d", wsl[i], wsl[i].count)
                    if scale is None:
                        if wcount[0] % 2 == 0:
                            P.op("act", lambda e, wb=wb, ws=ws, n=n: e.activation(out=wb.ap[:, 0:n, :], in_=ws.ap[:, 0:n, :], func=AF.Copy), reads=[ws], writes=[wb])
                        else:
                            P.op("dve", lambda e, wb=wb, ws=ws, n=n: e.tensor_copy(out=wb.ap[:, 0:n, :], in_=ws.ap[:, 0:n, :]), reads=[ws], writes=[wb])
                    elif scale == "ssm":
                        P.op("dve", lambda e, wb=wb, ws=ws, n=n, g0=g0: e.tensor_tensor(
                            out=wb.ap[:, 0:n, :], in0=ws.ap[:, 0:n, :], in1=ssmnw.ap[:, g0:g0 + n].unsqueeze(2).to_broadcast([128, n, 128]), op=ALU.mult),
                            reads=[ws, ssmnw], writes=[wb])
                    else:
                        P.op("act", lambda e, wb=wb, ws=ws, n=n: e.activation(out=wb.ap[:, 0:n, :], in_=ws.ap[:, 0:n, :], func=AF.Copy, scale=gdnnw.ap[:, 0:1]),
                             reads=[ws, gdnnw], writes=[wb])
                    for k in range(n):
                        kc = g0 + k
                        P.op("pe", lambda e, pb=pb, wb=wb, k=k, kc=kc: e.matmul(pb.ap[:], lhsT=wb.ap[:, k, :], rhs=rhs_fn(kc), start=(kc == 0), stop=(kc == KC - 1)),
                             reads=[wb] + rhs_bufs, writes=[pb])
                return pb

            def wv(w, f0):
                v = w.rearrange("(kc p) f -> p kc f", p=128)
                return lambda k0, n: v[:, k0:k0 + n, f0:f0 + 128]

            sov = so.rearrange("(kc p) t -> p kc t", p=128)
            gov = go.rearrange("(kc p) t -> p kc t", p=128)
            for tb in range(4):
                ts_ = slice(tb * 512, (tb + 1) * 512)
                for half in range(4):
                    P.dma("sp", big.ap[:, half * 8:(half + 1) * 8, :], sov[:, half * 8:(half + 1) * 8, ts_], bsl, writes=[big] if half == 0 else [])
                for half in range(4):
                    P.dma("sp", big.ap[:, 32 + half * 8:32 + (half + 1) * 8, :], gov[:, half * 8:(half + 1) * 8, ts_], bsl)
                big.w = ("d", bsl, bsl.count)
                for tl in range(4):
                    ti = tb * 4 + tl
                    xb = xt4[ti % 2]
                    P.dma("sp", xb.ap[:, 0:1024], xin[TP + ti * 128:TP + (ti + 1) * 128, 0:1024], xsl[ti % 2], writes=[xb])
                    P.dma("sp", xb.ap[:, 1024:2048], xin[TP + ti * 128:TP + (ti + 1) * 128, 1024:2048], xsl[ti % 2])
                    xb.w = ("d", xsl[ti % 2], xsl[ti % 2].count)
                    for q in range(4):
                        pb = pt[q]
                        for j in range(4):
                            kc = q * 4 + j
                            P.op("pe", lambda e, pb=pb, j=j, kc=kc, xb=xb: e.transpose(pb.ap[:, j, :], xb.ap[:, kc * 128:(kc + 1) * 128], ident.ap[:]),
                                 reads=[xb, ident], writes=[pb])
                        eng = "act" if q % 2 == 0 else "dve"
                        if eng == "act":
                            P.op("act", lambda e, pb=pb, q=q, tl=tl: e.activation(out=xT.ap[:, q * 4:(q + 1) * 4, tl * 128:(tl + 1) * 128], in_=pb.ap[:], func=AF.Copy),
                                 reads=[pb], writes=[xT])
                        else:
                            P.op("dve", lambda e, pb=pb, q=q, tl=tl: e.tensor_copy(out=xT.ap[:, q * 4:(q + 1) * 4, tl * 128:(tl + 1) * 128], in_=pb.ap[:]),
                                 reads=[pb], writes=[xT])
                for f in range(16):
                    ga, gb = gt[(2 * f) % 4], gt[(2 * f + 1) % 4]
                    P.dma("sp", ga.ap[:], zs[8192 + f * 128:8192 + (f + 1) * 128, ts_], gsl[(2 * f) % 4], writes=[ga])
                    P.dma("sp", gb.ap[:], zs[10240 + f * 128:10240 + (f + 1) * 128, ts_], gsl[(2 * f + 1) % 4], writes=[gb])
                    pa = gemm(wv(w_ssm, f * 128), 32, lambda kc: big.ap[:, kc, :], [big], scale="ssm")
                    P.op("dve", lambda e, pa=pa, ga=ga: e.tensor_tensor(out=tmpa.ap[:], in0=pa.ap[:], in1=ga.ap[:], op=ALU.mult), reads=[pa, ga], writes=[tmpa])
                    pb_ = gemm(wv(w_gdn, f * 128), 32, lambda kc: big.ap[:, 32 + kc, :], [big], scale="gdn")
                    P.op("dve", lambda e, pb_=pb_, gb=gb: e.tensor_tensor(out=tmpb.ap[:], in0=pb_.ap[:], in1=gb.ap[:], op=ALU.mult), reads=[pb_, gb], writes=[tmpb])
                    P.op("pool", lambda e, f=f: e.tensor_tensor(out=mT.ap[:, f, :], in0=tmpa.ap[:], in1=tmpb.ap[:], op=ALU.add), reads=[tmpa, tmpb], writes=[mT])
                for f in range(16):
                    pc = gemm(wv(w_o, f * 128), 16, lambda kc: mT.ap[:, kc, :], [mT])
                    P.op("dve", lambda e, pc=pc, f=f: e.scalar_tensor_tensor(out=xT.ap[:, f, :], in0=pc.ap[:], scalar=ada.ap[:, 32 + f:33 + f], in1=xT.ap[:, f, :],
                                                                           op0=ALU.mult, op1=ALU.add), reads=[pc, xT, ada], writes=[xT])

                def rstd_rows():
                    pr = pm[pcount[0] % 4]
                    pcount[0] += 1
                    for kc in range(16):
                        P.op("act", lambda e, kc=kc: e.activation(out=tmpa.ap[:], in_=xT.ap[:, kc, :], func=AF.Square), reads=[xT], writes=[tmpa])
                        P.op("pe", lambda e, pr=pr, kc=kc: e.matmul(pr.ap[:], lhsT=ones.ap[:], rhs=tmpa.ap[:], start=(kc == 0), stop=(kc == 15)),
                             reads=[tmpa, ones], writes=[pr])
                    P.op("act", lambda e, pr=pr: e.activation(out=rrr.ap[:], in_=pr.ap[:], func=AF.Ln, scale=1.0 / D, bias=1e-6), reads=[pr], writes=[rrr])
                    P.op("act", lambda e: e.activation(out=rrr.ap[:], in_=rrr.ap[:], func=AF.Exp, scale=-0.5), reads=[rrr], writes=[rrr])

                rstd_rows()
                for kc in range(16):
                    P.op("dve", lambda e, kc=kc: e.tensor_tensor(out=tmpb.ap[:], in0=xT.ap[:, kc, :], in1=rrr.ap[:], op=ALU.mult), reads=[xT, rrr], writes=[tmpb])
                    P.op("dve", lambda e, kc=kc: e.tensor_scalar(out=mT.ap[:, kc, :], in0=tmpb.ap[:], scalar1=A2.ap[:, kc:kc + 1], scalar2=ada.ap[:, 48 + kc:49 + kc],
                                                                op0=ALU.mult, op1=ALU.add), reads=[tmpb, A2, ada], writes=[mT])
                for j in range(44):
                    pg = gemm(wv(w_gu, j * 128), 16, lambda kc: mT.ap[:, kc, :], [mT])
                    P.op("act", lambda e, pg=pg: e.activation(out=tmpa.ap[:], in_=pg.ap[:], func=AF.Silu), reads=[pg], writes=[tmpa])
                    pu = gemm(wv(w_gu, FFN + j * 128), 16, lambda kc: mT.ap[:, kc, :], [mT])
                    P.op("dve", lambda e, pu=pu, j=j: e.tensor_tensor(out=big.ap[:, j, :], in0=pu.ap[:], in1=tmpa.ap[:], op=ALU.mult), reads=[pu, tmpa], writes=[big])
                for f in range(16):
                    pd = gemm(wv(w_dn, f * 128), 44, lambda kc: big.ap[:, kc, :], [big])
                    P.op("dve", lambda e, pd=pd, f=f: e.scalar_tensor_tensor(out=xT.ap[:, f, :], in0=pd.ap[:], scalar=ada.ap[:, 80 + f:81 + f], in1=xT.ap[:, f, :],
                                                                           op0=ALU.mult, op1=ALU.add), reads=[pd, xT, ada], writes=[xT])
                for tl in range(4):
                    ti = tb * 4 + tl
                    xb = xt4[ti % 2]
                    s1 = st4[ti % 2]
                    for q in range(4):
                        pb = pt[q]
                        for j in range(4):
                            kc = q * 4 + j
                            P.op("pe", lambda e, pb=pb, j=j, kc=kc, tl=tl: e.transpose(pb.ap[:, j, :], xT.ap[:, kc, tl * 128:(tl + 1) * 128], ident.ap[:]),
                                 reads=[xT, ident], writes=[pb])
                        eng = "act" if q % 2 == 0 else "dve"
                        if eng == "act":
                            P.op("act", lambda e, pb=pb, q=q, xb=xb: e.activation(out=xb.ap[:, q * 512:(q + 1) * 512], in_=pb.ap[:].rearrange("p a b -> p (a b)"), func=AF.Copy),
                                 reads=[pb], writes=[xb])
                        else:
                            P.op("dve", lambda e, pb=pb, q=q, xb=xb: e.tensor_copy(out=xb.ap[:, q * 512:(q + 1) * 512], in_=pb.ap[:].rearrange("p a b -> p (a b)")),
                                 reads=[pb], writes=[xb])
                    P.op("act", lambda e, xb=xb, s1=s1: e.activation(out=sq4.ap[:], in_=xb.ap[:], func=AF.Square, accum_out=s1.ap[:, 0:1]), reads=[xb], writes=[sq4, s1])
                    P.op("act", lambda e, s1=s1: e.activation(out=s1.ap[:, 1:2], in_=s1.ap[:, 0:1], func=AF.Ln, scale=1.0 / D, bias=1e-6), reads=[s1], writes=[s1])
                    P.op("act", lambda e, s1=s1: e.activation(out=s1.ap[:, 2:3], in_=s1.ap[:, 1:2], func=AF.Exp, scale=-0.5), reads=[s1], writes=[s1])
                    P.op("dve", lambda e, xb=xb, s1=s1: e.scalar_tensor_tensor(out=xb.ap[:], in0=xb.ap[:], scalar=s1.ap[:, 2:3], in1=fnw_row.ap[:], op0=ALU.mult, op1=ALU.mult),
                         reads=[xb, s1, fnw_row], writes=[xb])
                    P.dma("sp", y_out[ti * 128:(ti + 1) * 128, :], xb.ap[:], ysl[ti % 2], reads=[xb])
            P.replay()

        if dbg:
            with ExitStack() as es:
                P = Prog(nc, es, "dbg")
                dsl = P.slot()
                d_cs = nc.dram_tensor("dbg_cs", [14336, 256], BF16, kind="ExternalOutput").ap()
                d_sm = nc.dram_tensor("dbg_sm", [TT, 256], F32, kind="ExternalOutput").ap()
                d_zs = nc.dram_tensor("dbg_zs", [12288, 128], BF16, kind="ExternalOutput").ap()
                d_ada = nc.dram_tensor("dbg_ada", [128, 96], F32, kind="ExternalOutput").ap()
                for t in range(14):
                    P.dma("sp", d_cs[t * 1024:(t + 1) * 1024, :], cs[t * 1024:(t + 1) * 1024, 1920:2176], dsl)
                for t in range(4):
                    P.dma("sp", d_sm[t * 1024:(t + 1) * 1024, :], smtok[t * 1024:(t + 1) * 1024, :], dsl)
                for t in range(12):
                    P.dma("sp", d_zs[t * 1024:(t + 1) * 1024, :], zs[t * 1024:(t + 1) * 1024, 0:128], dsl)
                P.dma("sp", d_ada, ada.ap[:], dsl)
                d_so = nc.dram_tensor("dbg_so", [4096, TQ], BF16, kind="ExternalOutput").ap()
                d_go = nc.dram_tensor("dbg_go", [4096, TQ], BF16, kind="ExternalOutput").ap()
                for t in range(4):
                    P.dma("sp", d_so[t * 1024:(t + 1) * 1024, :], so[t * 1024:(t + 1) * 1024, :], dsl)
                    P.dma("sp", d_go[t * 1024:(t + 1) * 1024, :], go[t * 1024:(t + 1) * 1024, :], dsl)
                P.replay()
    return nc


def _layout_inputs(inp):
    f = lambda a: np.ascontiguousarray(a, dtype=np.float32)
    col = lambda v, n: f(np.asarray(v).reshape(n, 128).T)
    x, c = inp["x"], inp["c"]
    cwS = np.asarray(inp["ssm_conv_w"][0])
    cwG = np.asarray(inp["gdn_conv_w"][0])
    cw_all = np.concatenate([cwS, cwG], axis=1)
    cw_l = f(cw_all.reshape(4, 112, 128).transpose(2, 1, 0))
    cb_all = np.concatenate([np.asarray(inp["ssm_conv_b"][0]), np.zeros(8192, np.float32)])
    cb_l = col(cb_all, 112)
    z32 = np.zeros(32, np.float32)
    smallbias = f(np.concatenate([z32, inp["gdn_dt_bias"][0], inp["ssm_dt_bias"][0]]).reshape(128, 1))
    alog = f(np.concatenate([z32, inp["gdn_a_log"][0], inp["ssm_a_log"][0]]).reshape(128, 1))
    idx = np.arange(128)
    common = {
        "w_ada": f(inp["w_ada"][0]), "b_ada_col": col(inp["b_ada"][0], 96),
        "nmw": col(inp["norm_mix_w"][0], 16), "nfw": col(inp["norm_ffn_w"][0], 16),
        "fnw_row": f(np.broadcast_to(np.asarray(inp["final_norm_w"])[None, :], (128, D))),
        "w_in": f(inp["w_in"][0]), "cw": cw_l, "cb": cb_l, "smallbias": smallbias, "alog": alog,
        "dskip_row": f(np.broadcast_to(np.asarray(inp["ssm_d_skip"][0])[None, :], (128, 64))),
        "ssmnw_col": col(inp["ssm_norm_w"][0], 32), "gdnnw_col": f(np.asarray(inp["gdn_norm_w"][0]).reshape(128, 1)),
        "w_ssm": f(inp["w_ssm_proj"][0]), "w_gdn": f(inp["w_gdn_proj"][0]), "w_o": f(inp["w_o"][0]),
        "w_gu": f(inp["w_gate_up"][0]), "w_dn": f(inp["w_down"][0]),
        "ident": np.eye(128, dtype=np.float32),
        "tri": (idx[:, None] <= idx[None, :]).astype(np.float32),
        "ones": np.ones((128, 128), np.float32),
        "maskT": np.where(idx[None, :] >= idx[:, None], 0.0, NEG).astype(np.float32),
        "maskLs": np.where(idx[None, :] < idx[:, None], 0.0, NEG).astype(np.float32),
    }
    maps = []
    for core in range(8):
        b, r = core // 2, core % 2
        xb = np.asarray(x[b])
        own = xb[r * TQ:(r + 1) * TQ]
        pre = xb[0:TP]
        m = dict(common)
        m["xin"] = f(np.concatenate([pre, own], axis=0))
        m["flag"] = np.full((128, 1), float(r), np.float32)
        m["c_col"] = col(c[b], 16)
        maps.append(m)
    return maps


_NC_CACHE = {}


def kernel(**inputs):
    inp = {k: np.asarray(v) for k, v in inputs.items()}
    if "nc" not in _NC_CACHE:
        _NC_CACHE["nc"] = build_program()
    nc = _NC_CACHE["nc"]
    maps = _layout_inputs(inp)
    res = run_bass_kernel_spmd(nc, maps, core_ids=list(range(8)))
    out = np.zeros((4, SEQ, D), np.float32)
    for core in range(8):
        b, r = core // 2, core % 2
        out[b, r * TQ:(r + 1) * TQ] = res.results[core]["y"]
    return out
```

```python
from contextlib import ExitStack
import numpy as np
import os as _os
import concourse.bass as bass
import concourse.mybir as mybir
from concourse.bass_utils import run_bass_kernel_spmd

F32 = mybir.dt.float32
BF16 = mybir.dt.bfloat16
AF = mybir.ActivationFunctionType
ALU = mybir.AluOpType

D = 2048
SEQ = 4096
TP = 2048
TQ = 2048
TT = TP + TQ
IN_DIM = 26752
FFN = 5632
NEG = -30000.0
OFF_Z, OFF_XBC, OFF_DT, OFF_QKV, OFF_GZ, OFF_BETA, OFF_A, OFF_GS, OFF_GG = (
    0, 4096, 10240, 10304, 18496, 22592, 22624, 22656, 24704)
ENGS = ("pe", "act", "dve", "pool", "sp")


class Buf:
    ALL = []

    def __init__(self, ap, const=False, excl=False):
        self.ap = ap
        self.excl = excl
        self.w = None
        self.r = {}
        self.const = const
        Buf.ALL.append(self)

    def __getitem__(self, k):
        return self.ap[k]


class Slot:
    def __init__(self, sem):
        self.sem = sem
        self.count = 0


class Prog:
    def __init__(self, nc, es, tag):
        self.nc = nc
        self.tag = tag
        self.streams = {e: [] for e in ENGS}
        self.waited = {e: {} for e in ENGS}
        self.sems = {e: es.enter_context(nc.semaphore(f"{tag}_s_{e}")) for e in ENGS}
        self.done = es.enter_context(nc.semaphore(f"{tag}_done"))
        self.es = es
        self.nslot = 0
        self.dma_toks = []
        for b in Buf.ALL:
            b.w = None
            b.r = {}

    def slot(self):
        self.nslot += 1
        return Slot(self.es.enter_context(self.nc.semaphore(f"{self.tag}_d{self.nslot}")))

    def _waits(self, eng, deps):
        out = []
        for d in deps:
            if d is None:
                continue
            if d[0] == "e":
                if d[1] == eng and eng == "pe":
                    continue
                key, val = ("e", d[1]), d[2]
            else:
                key, val = ("d", id(d[1])), d[2]
            if self.waited[eng].get(key, -1) >= val:
                continue
            self.waited[eng][key] = val
            out.append(d)
        return out

    def _collect(self, reads, writes, deps):
        al = list(deps)
        for b in reads:
            al.append(b.w)
        for b in writes:
            al.append(b.w)
            al.extend(b.r.values())
        return al

    def _update(self, tok, reads, writes):
        for b in reads:
            if not b.const:
                b.r[(tok[0], tok[1] if tok[0] == "e" else id(tok[1]))] = tok
        for b in writes:
            b.w = tok
            b.r = {}

    def op(self, eng, fn, reads=(), writes=(), deps=()):
        writes = list(writes) + [b for b in reads if b.excl]
        reads = [b for b in reads if not b.excl]
        waits = self._waits(eng, self._collect(reads, writes, deps))
        idx = len(self.streams[eng])
        self.streams[eng].append(["op", fn, waits, False, None])
        tok = ("e", eng, idx)
        self._update(tok, reads, writes)
        return tok

    def dma(self, q, out, in_, slot, reads=(), writes=(), deps=()):
        waits = self._waits(q, self._collect(reads, writes, deps))
        slot.count += 16
        self.streams[q].append(["dma", (out, in_), waits, False, slot])
        tok = ("d", slot, slot.count)
        self._update(tok, reads, writes)
        self.dma_toks.append(tok)
        return tok

    def replay(self):
        nc = self.nc
        for e in ENGS:
            for rec in self.streams[e]:
                for w in rec[2]:
                    if w[0] == "e":
                        self.streams[w[1]][w[2]][3] = True
        last = {}
        for e in ENGS:
            ops = [i for i, r in enumerate(self.streams[e]) if r[0] == "op"]
            if ops:
                self.streams[e][ops[-1]][3] = True
                last[e] = ops[-1]
        counts = {}
        for e in ENGS:
            c = 0
            cl = []
            for rec in self.streams[e]:
                if rec[0] == "op" and rec[3]:
                    c += 1
                cl.append(c)
            counts[e] = cl
        final_d = {}
        for t in self.dma_toks:
            final_d[id(t[1])] = (t[1], max(final_d.get(id(t[1]), (None, 0))[1], t[2]))

        def run(eng_name, eng):
            for rec in self.streams[eng_name]:
                for w in rec[2]:
                    if w[0] == "e":
                        eng.wait_ge(self.sems[w[1]], counts[w[1]][w[2]])
                    else:
                        eng.wait_ge(w[1].sem, w[2])
                if rec[0] == "op":
                    ins = rec[1](eng)
                    if rec[3]:
                        ins.then_inc(self.sems[eng_name], 1)
                else:
                    o, i = rec[1]
                    eng.dma_start(out=o, in_=i).then_inc(rec[4].sem, 16)
            if eng_name == "sp":
                for e2, li in last.items():
                    eng.wait_ge(self.sems[e2], counts[e2][li])
                for sl, v in final_d.values():
                    eng.wait_ge(sl.sem, v)
                eng.sem_inc(self.done, 1)
            else:
                eng.wait_ge(self.done, 1)

        with nc.Block() as block:
            block.sync(lambda e: run("sp", e))
            block.tensor(lambda e: run("pe", e))
            block.scalar(lambda e: run("act", e))
            block.vector(lambda e: run("dve", e))
            block.gpsimd(lambda e: run("pool", e))


_UID = [0]


def sb(nc, es, name, shape, dt, const=False):
    _UID[0] += 1
    return Buf(es.enter_context(nc.sbuf_tensor(f"s{_UID[0]}_{name}", list(shape), dt)), const=const)


def ps(nc, es, name, shape, dt=F32):
    _UID[0] += 1
    return Buf(es.enter_context(nc.psum_tensor(f"p{_UID[0]}_{name}", list(shape), dt)), excl=True)


def build_program(dbg=False, upto=99, p1only=False, mini=False):
    nc = bass.Bass("TRN2", target_bir_lowering=False)
    I = {}

    def din(name, shape, dt=F32):
        if mini and name in ("w_ada", "w_in", "w_ssm", "w_gdn", "w_o", "w_gu", "w_dn"):
            shape = [128, 128]
        I[name] = nc.dram_tensor(name, list(shape), dt, kind="ExternalInput").ap()
        return I[name]

    xin = din("xin", [TT, D])
    flag_d = din("flag", [128, 1])
    c_col_d = din("c_col", [128, 16])
    w_ada = din("w_ada", [D, 6 * D])
    b_ada_col_d = din("b_ada_col", [128, 96])
    nmw_d = din("nmw", [128, 16])
    nfw_d = din("nfw", [128, 16])
    fnw_row_d = din("fnw_row", [128, D])
    w_in = din("w_in", [D, IN_DIM])
    cw_d = din("cw", [128, 112, 4])
    cb_d = din("cb", [128, 112])
    smallbias_d = din("smallbias", [128, 1])
    alog_d = din("alog", [128, 1])
    dskip_d = din("dskip_row", [128, 64])
    ssmnw_d = din("ssmnw_col", [128, 32])
    gdnnw_d = din("gdnnw_col", [128, 1])
    w_ssm = din("w_ssm", [4096, D])
    w_gdn = din("w_gdn", [4096, D])
    w_o = din("w_o", [D, D])
    w_gu = din("w_gu", [D, 2 * FFN])
    w_dn = din("w_dn", [FFN, D])
    ident_d = din("ident", [128, 128])
    tri_d = din("tri", [128, 128])
    ones_d = din("ones", [128, 128])
    maskT_d = din("maskT", [128, 128])
    maskLs_d = din("maskLs", [128, 128])
    y_out = nc.dram_tensor("y", [TQ, D], F32, kind="ExternalOutput").ap()

    cs = nc.dram_tensor("cs", [14336, TT], BF16).ap()
    zs = nc.dram_tensor("zs", [12288, TQ], BF16).ap()
    smtok = nc.dram_tensor("smtok", [TT, 256], F32).ap()
    so = nc.dram_tensor("so", [4096, TQ], BF16).ap()
    go = nc.dram_tensor("go", [4096, TQ], BF16).ap()
    dbg_outs = {}

    def p4_jobs():
        jobs = []
        for f in range(16):
            jobs += [(w_ssm, f * 128, g0, 16, "ssm") for g0 in (0, 16)]
            jobs += [(w_gdn, f * 128, g0, 16, "gdn") for g0 in (0, 16)]
        for f in range(16):
            jobs.append((w_o, f * 128, 0, 16, None))
        for j in range(44):
            jobs.append((w_gu, j * 128, 0, 16, None))
            jobs.append((w_gu, FFN + j * 128, 0, 16, None))
        for f in range(16):
            jobs += [(w_dn, f * 128, 0, 16, None), (w_dn, f * 128, 16, 16, None), (w_dn, f * 128, 32, 12, None)]
        return jobs

    JOBS = p4_jobs()
    wq = nc.dram_tensor("wq", [len(JOBS), 128, 2048], BF16).ap()

    with ExitStack() as top:
        ident = sb(nc, top, "ident", [128, 128], F32, const=True)
        identb = sb(nc, top, "identb", [128, 128], BF16, const=True)
        tri = sb(nc, top, "tri", [128, 128], F32, const=True)
        ones = sb(nc, top, "ones", [128, 128], F32, const=True)
        maskT = sb(nc, top, "maskT", [128, 128], F32, const=True)
        maskLs = sb(nc, top, "maskLs", [128, 128], F32, const=True)
        flag = sb(nc, top, "flagt", [128, 1], F32, const=True)
        ada = sb(nc, top, "ada", [128, 96], F32, const=True)
        A1 = sb(nc, top, "A1", [128, 16], F32, const=True)
        A2 = sb(nc, top, "A2", [128, 16], F32, const=True)
        fnw_row = sb(nc, top, "fnw_row", [128, D], F32, const=True)
        cw = sb(nc, top, "cw", [128, 112, 4], F32, const=True)
        cbias = sb(nc, top, "cbias", [128, 112], F32, const=True)
        smallbias = sb(nc, top, "smallbias", [128, 1], F32, const=True)
        negA = sb(nc, top, "negA", [128, 1], F32, const=True)
        dskip = sb(nc, top, "dskip", [128, 64], F32, const=True)
        ssmnw = sb(nc, top, "ssmnw", [128, 32], F32, const=True)
        gdnnw = sb(nc, top, "gdnnw", [128, 1], F32, const=True)
        hist = sb(nc, top, "hist", [128, 112, 3], F32)

        with ExitStack() as es:
            P = Prog(nc, es, "p0")
            ld = P.slot()
            for buf, src in ((ident, ident_d), (tri, tri_d), (ones, ones_d), (maskT, maskT_d), (maskLs, maskLs_d),
                             (flag, flag_d), (fnw_row, fnw_row_d), (cw, cw_d), (cbias, cb_d), (smallbias, smallbias_d),
                             (dskip, dskip_d), (ssmnw, ssmnw_d), (gdnnw, gdnnw_d)):
                P.dma("sp", buf.ap[:], src, ld, writes=[buf])
            ccol = sb(nc, es, "ccol", [128, 16], F32)
            cact = sb(nc, es, "cact", [128, 16], F32)
            bada = sb(nc, es, "bada", [128, 96], F32)
            nmw = sb(nc, es, "nmw", [128, 16], F32)
            nfw = sb(nc, es, "nfw", [128, 16], F32)
            alog = sb(nc, es, "alog", [128, 1], F32)
            tmp16 = sb(nc, es, "tmp16", [128, 16], F32)
            for buf, src in ((ccol, c_col_d), (bada, b_ada_col_d), (nmw, nmw_d), (nfw, nfw_d), (alog, alog_d)):
                P.dma("sp", buf.ap[:], src, ld, writes=[buf])
            P.op("act", lambda e: e.activation(out=cact.ap[:], in_=ccol.ap[:], func=AF.Silu), reads=[ccol], writes=[cact])
            P.op("act", lambda e: e.activation(out=negA.ap[:], in_=alog.ap[:], func=AF.Exp), reads=[alog], writes=[negA])
            P.op("dve", lambda e: e.tensor_scalar(out=negA.ap[:], in0=negA.ap[:], scalar1=-1.0, scalar2=None, op0=ALU.mult),
                 reads=[negA], writes=[negA])
            P.op("dve", lambda e: e.tensor_copy(out=identb.ap[:], in_=ident.ap[:]), reads=[ident], writes=[identb])
            P.op("dve", lambda e: e.memset(hist.ap[:], 0.0), writes=[hist])
            wa = [sb(nc, es, f"wa{i}", [128, 16, 128], F32) for i in range(3)]
            wsl = [P.slot() for _ in range(3)]
            pada = ps(nc, es, "pada", [128, 96])
            wav = w_ada.rearrange("(kc p) f -> p kc f", p=128)
            if mini:
                P.op("pe", lambda e: e.matmul(pada.ap[:, 0:96], lhsT=ident.ap[:], rhs=ones.ap[:, 0:96], start=True, stop=True),
                     reads=[ident, ones], writes=[pada])
            for ft in range(0 if mini else 96):
                wb = wa[ft % 3]
                P.dma("sp", wb.ap[:, 0:8, :], wav[:, 0:8, ft * 128:(ft + 1) * 128], wsl[ft % 3], writes=[wb])
                P.dma("sp", wb.ap[:, 8:16, :], wav[:, 8:16, ft * 128:(ft + 1) * 128], wsl[ft % 3], writes=[])
                wb.w = ("d", wsl[ft % 3], wsl[ft % 3].count)
                for kc in range(16):
                    P.op("pe", lambda e, wb=wb, kc=kc, ft=ft: e.matmul(pada.ap[:, ft:ft + 1], lhsT=wb.ap[:, kc, :],
                                                                      rhs=cact.ap[:, kc:kc + 1], start=(kc == 0), stop=(kc == 15)),
                         reads=[wb, cact], writes=[pada])
            P.op("dve", lambda e: e.tensor_tensor(out=ada.ap[:], in0=pada.ap[:], in1=bada.ap[:], op=ALU.add),
                 reads=[pada, bada], writes=[ada])
            for (Ax, nw, c0) in ((A1, nmw, 16), (A2, nfw, 64)):
                P.op("dve", lambda e, c0=c0: e.tensor_scalar(out=tmp16.ap[:], in0=ada.ap[:, c0:c0 + 16], scalar1=1.0, scalar2=None,
                                                            op0=ALU.add), reads=[ada], writes=[tmp16])
                P.op("dve", lambda e, Ax=Ax, nw=nw: e.tensor_tensor(out=Ax.ap[:], in0=tmp16.ap[:], in1=nw.ap[:], op=ALU.mult),
                     reads=[tmp16, nw], writes=[Ax])
            P.replay()

        for pas in range(2):
            if upto < 1 + pas:
                continue
            t0 = pas * TP
            NT = 16
            import os as _os
            NT1 = int(_os.environ.get('NT1', '16'))
            CUT = int(_os.environ.get('CUT', '99'))
            with ExitStack() as es:
                P = Prog(nc, es, f"p2{pas}")
                hT = sb(nc, es, "hT", [128, 16, 2048], BF16)
                xt = [sb(nc, es, f"xt{i}", [128, D], F32) for i in range(2)]
                xsl = [P.slot() for _ in range(2)]
                acc = [sb(nc, es, f"acc{i}", [128, 2048], F32) for i in range(2)]
                sq = acc[0]
                st1 = [sb(nc, es, f"st1_{i}", [128, 4], F32) for i in range(2)]
                ptr = [ps(nc, es, f"ptr{i}", [128, 4, 128]) for i in range(4)]
                p1_last = {}
                for ti in range(NT1):
                    xb = xt[ti % 2]
                    s1 = st1[ti % 2]
                    P.dma("sp", xb.ap[:, 0:1024], xin[t0 + ti * 128:t0 + (ti + 1) * 128, 0:1024], xsl[ti % 2], writes=[xb])
                    P.dma("sp", xb.ap[:, 1024:2048], xin[t0 + ti * 128:t0 + (ti + 1) * 128, 1024:2048], xsl[ti % 2])
                    xb.w = ("d", xsl[ti % 2], xsl[ti % 2].count)
                    P.op("act", lambda e, xb=xb, s1=s1: e.activation(out=sq.ap[:], in_=xb.ap[:], func=AF.Square, accum_out=s1.ap[:, 0:1]),
                         reads=[xb], writes=[sq, s1])
                    if CUT < 1:
                        continue
                    P.op("act", lambda e, s1=s1: e.activation(out=s1.ap[:, 1:2], in_=s1.ap[:, 0:1], func=AF.Ln, scale=1.0 / D, bias=1e-6),
                         reads=[s1], writes=[s1])
                    P.op("act", lambda e, s1=s1: e.activation(out=s1.ap[:, 2:3], in_=s1.ap[:, 1:2], func=AF.Exp, scale=-0.5),
                         reads=[s1], writes=[s1])
                    if CUT < 2:
                        continue
                    P.op("dve", lambda e, xb=xb, s1=s1: e.tensor_scalar(out=xb.ap[:], in0=xb.ap[:], scalar1=s1.ap[:, 2:3], scalar2=None,
                                                                       op0=ALU.mult), reads=[xb, s1], writes=[xb])
                    for q in range(4):
                        if CUT < 3:
                            continue
                        pb = ptr[q]
                        for j in range(4):
                            kc = q * 4 + j
                            P.op("pe", lambda e, pb=pb, j=j, kc=kc, xb=xb: e.transpose(pb.ap[:, j, :], xb.ap[:, kc * 128:(kc + 1) * 128], ident.ap[:]),
                                 reads=[xb, ident], writes=[pb])
                        for j in range(4):
                            if CUT < 4:
                                continue
                            kc = q * 4 + j
                            eng = "act" if (q % 2 == 0) else "dve"
                            if eng == "act":
                                P.op("act", lambda e, pb=pb, j=j, kc=kc, ti=ti: e.activation(
                                    out=hT.ap[:, kc, ti * 128:(ti + 1) * 128], in_=pb.ap[:, j, :], func=AF.Identity,
                                    scale=A1.ap[:, kc:kc + 1], bias=ada.ap[:, kc:kc + 1]), reads=[pb, A1, ada])
                                p1_last["act"] = P.streams["act"] and ("e", "act", len(P.streams["act"]) - 1)
                            else:
                                P.op("dve", lambda e, pb=pb, j=j, kc=kc, ti=ti: e.tensor_scalar(
                                    out=hT.ap[:, kc, ti * 128:(ti + 1) * 128], in0=pb.ap[:, j, :], scalar1=A1.ap[:, kc:kc + 1],
                                    scalar2=ada.ap[:, kc:kc + 1], op0=ALU.mult, op1=ALU.add), reads=[pb, A1, ada])
                                p1_last["dve"] = ("e", "dve", len(P.streams["dve"]) - 1)
                tiles = []
                for t in range(48):
                    kd = "hist" if (pas == 0 and t >= 40) else "conv"
                    tiles.append((kd, t, [(OFF_XBC + t * 128, 128, 0)]))
                for t in range(64):
                    kd = "hist" if (pas == 0 and t < 16) else "conv"
                    tiles.append((kd, 48 + t, [(OFF_QKV + t * 128, 128, 0)]))
                tiles.append(("small", 0, [(OFF_BETA, 64, 0), (OFF_DT, 64, 64)]))
                if p1only:
                    tiles = []
                if pas == 1:
                    for t in range(32):
                        tiles.append(("silu", t, [(OFF_Z + t * 128, 128, 0)]))
                    for t in range(32):
                        tiles.append(("silu", 32 + t, [(OFF_GZ + t * 128, 128, 0)]))
                    for t in range(32):
                        tiles.append(("sig", 64 + t, [(OFF_GS + t * 128, 128, 0)]))
                wst = [sb(nc, es, f"wst{i}", [128, 16, 128], F32) for i in range(3)]
                wsl = [P.slot() for _ in range(3)]
                wbf = [sb(nc, es, f"wbf{i}", [128, 16, 128], BF16) for i in range(2)]
                pmm = [ps(nc, es, f"pmm{i}", [128, 512]) for i in range(4)]
                cbuf = [sb(nc, es, f"cbuf{i}", [128, 3 + 2048], F32) for i in range(2)]
                ost = [sb(nc, es, f"ost{i}", [128, 2048], BF16) for i in range(2)]
                osl = [P.slot() for _ in range(2)]
                sm1 = sb(nc, es, "sm1", [128, 2048], F32)
                sm2 = sb(nc, es, "sm2", [128, 2048], F32)
                sm3 = sb(nc, es, "sm3", [128, 2048], F32)
                smt = [sb(nc, es, f"smt{i}", [128, 256], F32) for i in range(2)]
                smsl = [P.slot() for _ in range(2)]
                winv = w_in.rearrange("(kc p) f -> p kc f", p=128)
                nconv = 0
                wi = 0
                cvsl = [P.slot() for _ in range(2)]
                cvb = [sb(nc, es, f"cvb{i}", [128, 16, 128], BF16) for i in range(2)] if (pas == 1 and upto >= 4) else []
                ncv = [0]
                jobq = list(enumerate(JOBS)) if (pas == 1 and upto >= 4) else []

                def convert_job(jid, job, wi):
                    W, f0, g0, n, scale = job
                    ws, wb = wst[wi % 3], cvb[ncv[0] % 2]
                    cslot = cvsl[ncv[0] % 2]
                    ncv[0] += 1
                    v = W.rearrange("(kc p) f -> p kc f", p=128)
                    h_ = n // 2
                    P.dma("sp", ws.ap[:, 0:h_, :], v[:, g0:g0 + h_, f0:f0 + 128], wsl[wi % 3], writes=[ws])
                    P.dma("sp", ws.ap[:, h_:n, :], v[:, g0 + h_:g0 + n, f0:f0 + 128], wsl[wi % 3])
                    ws.w = ("d", wsl[wi % 3], wsl[wi % 3].count)
                    if scale is None:
                        P.op("pool", lambda e: e.tensor_copy(out=wb.ap[:, 0:n, :], in_=ws.ap[:, 0:n, :]), reads=[ws], writes=[wb])
                    elif scale == "ssm":
                        P.op("pool", lambda e: e.tensor_tensor(out=wb.ap[:, 0:n, :], in0=ws.ap[:, 0:n, :],
                                                              in1=ssmnw.ap[:, g0:g0 + n].unsqueeze(2).to_broadcast([128, n, 128]), op=ALU.mult),
                             reads=[ws, ssmnw], writes=[wb])
                    else:
                        P.op("act", lambda e: e.activation(out=wb.ap[:, 0:n, :], in_=ws.ap[:, 0:n, :], func=AF.Copy, scale=gdnnw.ap[:, 0:1]),
                             reads=[ws, gdnnw], writes=[wb])
                    P.dma("pool", wq[jid, :, 0:n * 128], wb.ap[:, 0:n, :].rearrange("p a b -> p (a b)"), cslot, reads=[wb])

                for tix, (kind, tidx, segs) in enumerate(tiles):
                    if jobq:
                        jid, job = jobq.pop(0)
                        convert_job(jid, job, wi)
                        wi += 1
                    ws = wst[wi % 3]
                    first = True
                    for (c0, ncol, f0) in segs:
                        for half in range(2):
                            P.dma("sp", ws.ap[:, half * 8:(half + 1) * 8, f0:f0 + ncol], winv[:, half * 8:(half + 1) * 8, c0:c0 + ncol],
                                  wsl[wi % 3], writes=[ws] if first else [])
                            first = False
                    ws.w = ("d", wsl[wi % 3], wsl[wi % 3].count)
                    wb = wbf[tix % 2]
                    wi += 1
                    P.op("pool", lambda e, wb=wb, ws=ws: e.tensor_copy(out=wb.ap[:], in_=ws.ap[:]), reads=[ws], writes=[wb])
                    if kind in ("conv", "hist"):
                        cbf = cbuf[nconv % 2]
                        ac = acc[nconv % 2]
                        nconv += 1
                        if pas == 0:
                            P.op("pool", lambda e, cbf=cbf: e.memset(cbf.ap[:, 0:3], 0.0), writes=[cbf])
                        else:
                            P.op("pool", lambda e, cbf=cbf, tidx=tidx: e.tensor_scalar(out=cbf.ap[:, 0:3], in0=hist.ap[:, tidx, :], scalar1=flag.ap[:, 0:1],
                                                                                    scalar2=None, op0=ALU.mult), reads=[hist], writes=[cbf])
                    for blk in range(4):
                        if kind == "hist" and blk < 3:
                            continue
                        pb = pmm[(tix * 4 + blk) % 4]
                        for kc in range(16):
                            P.op("pe", lambda e, pb=pb, wb=wb, kc=kc, blk=blk: e.matmul(pb.ap[:], lhsT=wb.ap[:, kc, :], rhs=hT.ap[:, kc, blk * 512:(blk + 1) * 512],
                                                                                  start=(kc == 0), stop=(kc == 15)), reads=[wb], writes=[pb], deps=list(p1_last.values()))
                        sl_ = slice(blk * 512, (blk + 1) * 512)
                        if kind in ("conv", "hist"):
                            P.op("act", lambda e, pb=pb, cbf=cbf, blk=blk: e.activation(out=cbf.ap[:, 3 + blk * 512:3 + (blk + 1) * 512], in_=pb.ap[:], func=AF.Copy),
                                 reads=[pb], writes=[cbf])
                        elif kind == "small":
                            P.op("act", lambda e, pb=pb, sl_=sl_: e.activation(out=sm1.ap[:, sl_], in_=pb.ap[:], func=AF.Identity, bias=smallbias.ap[:, 0:1]),
                                 reads=[pb, smallbias], writes=[sm1])
                        else:
                            o_ = ost[tix % 2]
                            fn_ = AF.Silu if kind == "silu" else AF.Sigmoid
                            P.op("act", lambda e, pb=pb, o_=o_, sl_=sl_, fn_=fn_: e.activation(out=o_.ap[:, sl_], in_=pb.ap[:], func=fn_),
                                 reads=[pb], writes=[o_])
                    if kind == "hist":
                        P.op("pool", lambda e, cbf=cbf, tidx=tidx: e.tensor_copy(out=hist.ap[:, tidx, :], in_=cbf.ap[:, 2048:2051]), reads=[cbf], writes=[hist])
                    elif kind == "conv":
                        P.op("dve", lambda e, cbf=cbf, ac=ac, tidx=tidx: e.tensor_scalar(out=ac.ap[:], in0=cbf.ap[:, 0:2048], scalar1=cw.ap[:, tidx, 0:1],
                                                                                    scalar2=cbias.ap[:, tidx:tidx + 1], op0=ALU.mult, op1=ALU.add),
                             reads=[cbf, cw, cbias], writes=[ac])
                        for k in range(1, 4):
                            P.op("dve", lambda e, cbf=cbf, ac=ac, tidx=tidx, k=k: e.scalar_tensor_tensor(out=ac.ap[:], in0=cbf.ap[:, k:k + 2048], scalar=cw.ap[:, tidx, k:k + 1],
                                                                                                 in1=ac.ap[:], op0=ALU.mult, op1=ALU.add),
                                 reads=[cbf, cw, ac], writes=[ac])
                        if pas == 0:
                            P.op("pool", lambda e, cbf=cbf, tidx=tidx: e.tensor_copy(out=hist.ap[:, tidx, :], in_=cbf.ap[:, 2048:2051]), reads=[cbf], writes=[hist])
                        o_ = ost[tix % 2]
                        P.op("act", lambda e, ac=ac, o_=o_: e.activation(out=o_.ap[:], in_=ac.ap[:], func=AF.Silu), reads=[ac], writes=[o_])
                        P.dma("act", cs[tidx * 128:(tidx + 1) * 128, t0:t0 + 2048], o_.ap[:], osl[tix % 2], reads=[o_])
                    elif kind == "small":
                        P.op("act", lambda e: e.activation(out=sm2.ap[:], in_=sm1.ap[:], func=AF.Exp), reads=[sm1], writes=[sm2])
                        P.op("act", lambda e: e.activation(out=sm2.ap[:], in_=sm2.ap[:], func=AF.Ln, bias=1.0), reads=[sm2], writes=[sm2])
                        P.op("act", lambda e: e.activation(out=sm1.ap[0:32, :], in_=sm1.ap[0:32, :], func=AF.Sigmoid), reads=[sm1, sm2], writes=[sm1])
                        P.op("dve", lambda e: e.tensor_scalar(out=sm3.ap[:], in0=sm2.ap[:], scalar1=negA.ap[:, 0:1], scalar2=None, op0=ALU.mult),
                             reads=[sm2, negA], writes=[sm3])
                        P.op("dve", lambda e: e.tensor_copy(out=sm1.ap[32:64, :], in_=sm3.ap[32:64, :]), reads=[sm3, sm1], writes=[sm1])
                        P.op("dve", lambda e: e.tensor_copy(out=sm1.ap[64:128, :], in_=sm2.ap[64:128, :]), reads=[sm2, sm1], writes=[sm1])
                        for ti in range(NT):
                            pb = ptr[ti % 4]
                            stt = smt[ti % 2]
                            P.op("pe", lambda e, pb=pb, ti=ti: e.transpose(pb.ap[:, 0, :], sm1.ap[:, ti * 128:(ti + 1) * 128], ident.ap[:]), reads=[sm1], writes=[pb])
                            P.op("pe", lambda e, pb=pb, ti=ti: e.transpose(pb.ap[:, 1, :], sm3.ap[:, ti * 128:(ti + 1) * 128], ident.ap[:]), reads=[sm3], writes=[pb])
                            P.op("dve", lambda e, pb=pb, stt=stt: e.tensor_copy(out=stt.ap[:], in_=pb.ap[:, 0:2, :].rearrange("p a b -> p (a b)")), reads=[pb], writes=[stt])
                            P.dma("act", smtok[t0 + ti * 128:t0 + (ti + 1) * 128, :], stt.ap[:], smsl[ti % 2], reads=[stt])
                    else:
                        P.dma("act", zs[tidx * 128:(tidx + 1) * 128, :], o_.ap[:], osl[tix % 2], reads=[o_])
                while jobq:
                    jid, job = jobq.pop(0)
                    convert_job(jid, job, wi)
                    wi += 1
                P.replay()

        if upto >= 3:
          with ExitStack() as es:
            P = Prog(nc, es, "p3")
            KL = 6
            NCH = TT // 128
            C0 = TP // 128
            c_lo = int(_os.environ.get("MIX_CLO", "0"))
            c_hi = int(_os.environ.get("MIX_CHI", str(NCH)))
            do_gdn = int(_os.environ.get("MIX_GDN", "1"))
            do_ssd = int(_os.environ.get("MIX_SSD", "1"))

            def T(name, shape, dt=F32):
                return sb(nc, es, name, shape, dt)

            def mm(ob, oap, lb, lap, rb, rap, start=True, stop=True):
                return P.op("pe", lambda e: e.matmul(oap, lhsT=lap, rhs=rap, start=start, stop=stop), reads=[lb, rb], writes=[ob])

            def tp(ob, oap, ib, iap):
                return P.op("pe", lambda e: e.transpose(oap, iap, ident.ap[:]), reads=[ib, ident], writes=[ob])

            def act(ob, oap, ib, iap, func, rd=(), wr=(), **kw):
                return P.op("act", lambda e: e.activation(out=oap, in_=iap, func=func, **kw), reads=[ib] + list(rd), writes=[ob] + list(wr))

            def cp(eng, ob, oap, ib, iap):
                return P.op(eng, lambda e: e.tensor_copy(out=oap, in_=iap), reads=[ib], writes=[ob])

            def tt(eng, ob, oap, ab, aap, bb, bap, op):
                return P.op(eng, lambda e: e.tensor_tensor(out=oap, in0=aap, in1=bap, op=op), reads=[ab, bb], writes=[ob])

            def tsc(eng, ob, oap, ab, aap, s1, s2, op0, op1=None, rd=()):
                if op1 is None:
                    return P.op(eng, lambda e: e.tensor_scalar(out=oap, in0=aap, scalar1=s1, scalar2=None, op0=op0), reads=[ab] + list(rd), writes=[ob])
                return P.op(eng, lambda e: e.tensor_scalar(out=oap, in0=aap, scalar1=s1, scalar2=s2, op0=op0, op1=op1), reads=[ab] + list(rd), writes=[ob])

            def stt(eng, ob, oap, ab, aap, sc, cb_, cap, op0, op1, rd=()):
                return P.op(eng, lambda e: e.scalar_tensor_tensor(out=oap, in0=aap, scalar=sc, in1=cap, op0=op0, op1=op1),
                            reads=[ab, cb_] + list(rd), writes=[ob])

            csl_s = T("csl_s", [128, 48, 128], BF16)
            csl_g = T("csl_g", [128, 64, 128], BF16)
            csls = [P.slot() for _ in range(2)]
            zsl = [T(f"zsl{i}", [128, 64, 128], BF16) for i in range(1)] * 2
            zsls = [P.slot() for _ in range(2)]
            smb = [T(f"smb{i}", [128, 256]) for i in range(2)]
            smsl = [P.slot() for _ in range(2)]
            Sg_t = T("Sg", [128, 32, 128])
            Ss_t = T("Ss", [128, 64, 64])
            Sg = [Buf(Sg_t.ap[:, h, :]) for h in range(32)]
            Ss = [Buf(Ss_t.ap[:, h, :]) for h in range(64)]
            gcol, ngcol, glast, ktl, eglast = [T(n, [128, 96]) for n in ("gcol", "ngcol", "glast", "ktl", "eglast")]
            bg, nbeta = T("bg", [128, 32]), T("nbeta", [128, 32])
            gost = [T(f"gost{i}", [128, 32, 128], BF16) for i in range(1)] * 2
            sost = [T(f"sost{i}", [128, 32, 128], BF16) for i in range(1)] * 2
            gosl = [P.slot() for _ in range(2)]
            sosl = [P.slot() for _ in range(2)]
            pS = ps(nc, es, "pS", [128, 512])
            pP = ps(nc, es, "pP", [128, 4, 128])
            lanes = []
            for i in range(KL):
                L = {"pA": ps(nc, es, f"pA{i}", [128, 4, 128])}
                for n in ("t1", "Ma", "Mb", "Ba", "Bb", "X", "PT", "qdT", "bkg", "bv", "nwT", "vnew", "kt", "on"):
                    L[n] = T(f"L{i}_{n}", [128, 128])
                L["decs"], L["decT"], L["egrr"] = L["Mb"], L["Bb"], L["nwT"]
                L["st"] = T(f"L{i}_st", [128, 4])
                L["junk"] = L["on"]
                lanes.append(L)
            prep = []
            for i in range(3):
                Pp = {n: T(f"pp{i}_{n}", [128, 128]) for n in ("kTf", "ktok", "kTn", "qtok", "qTn", "vTf0", "vTf1", "vtok0", "vtok1", "zf0", "zf1", "junk")}
                Pp["qTf"] = Pp["kTf"]
                Pp["BTf"], Pp["Btok"], Pp["CTf"], Pp["KQg"] = Pp["kTn"], Pp["ktok"], Pp["qTn"], Pp["qtok"]
                Pp["st"] = T(f"pp{i}_st", [128, 8])
                prep.append(Pp)
            xTf = [T(f"xTf{i}", [128, 128]) for i in range(2)]
            xtk = [T(f"xtk{i}", [128, 128]) for i in range(4)]
            yg = T("yg", [128, 512])
            yz = T("yz", [128, 512])
            zf4 = T("zf4", [128, 4, 128])
            gst = T("gst", [128, 4])
            csv = cs.rearrange("(t p) k -> p t k", p=128)
            zsv = zs.rearrange("(t p) k -> p t k", p=128)
            sov = so.rearrange("(t p) k -> p t k", p=128)
            gov = go.rearrange("(t p) k -> p t k", p=128)

            P.op("dve", lambda e: e.memset(Sg_t.ap[:], 0.0), writes=Sg)
            P.op("pool", lambda e: e.memset(Ss_t.ap[:], 0.0), writes=Ss)

            def run_lanes(gens):
                gens = list(gens)
                while gens:
                    batch, gens = gens[:KL], gens[KL:]
                    live = [g(lanes[i]) for i, g in enumerate(batch)]
                    while live:
                        for g in list(live):
                            try:
                                next(g)
                            except StopIteration:
                                live.remove(g)

            def neumann(L, own):
                pA = pB = L["pA"]
                tp(pA, pA.ap[:, 3, :], L["Ma"], L["Ma"].ap[:])
                act(L["Ba"], L["Ba"].ap[:], pA, pA.ap[:, 3, :], AF.Copy)
                tt("dve", L["X"], L["X"].ap[:], pA, pA.ap[:, 3, :], ident, ident.ap[:], ALU.add)
                yield
                M, B, Mn, Bn = L["Ma"], L["Ba"], L["Mb"], L["Bb"]
                for j in range(1, 7):
                    mm(pB, pB.ap[:, 0, :], B, B.ap[:], M, M.ap[:])
                    if j < 6:
                        mm(pB, pB.ap[:, 1, :], M, M.ap[:], B, B.ap[:])
                    act(Mn, Mn.ap[:], pB, pB.ap[:, 0, :], AF.Copy)
                    if j < 6:
                        cp("dve", Bn, Bn.ap[:], pB, pB.ap[:, 1, :])
                    yield
                    mm(pB, pB.ap[:, 2, :], Mn, Mn.ap[:], L["X"], L["X"].ap[:])
                    tt("dve", L["X"], L["X"].ap[:], pB, pB.ap[:, 2, :], L["X"], L["X"].ap[:], ALU.add)
                    yield
                    M, B, Mn, Bn = Mn, Bn, M, B

            def gdn_unit(c, h, Pp, e01, smc, zslab):
                own = c >= C0
                col = h

                def gen(L):
                    pA = pB = L["pA"]
                    S = Sg[h]
                    vtok = Pp[f"vtok{e01}"]
                    mm(pA, pA.ap[:, 0, :], smc, smc.ap[:, 32 + h:33 + h].to_broadcast([128, 128]), tri, tri.ap[:])
                    mm(pA, pA.ap[:, 1, :], Pp["kTn"], Pp["kTn"].ap[:], Pp["kTn"], Pp["kTn"].ap[:])
                    if own:
                        mm(pA, pA.ap[:, 2, :], Pp["kTn"], Pp["kTn"].ap[:], Pp["qTn"], Pp["qTn"].ap[:])
                    yield
                    stt("dve", L["t1"], L["t1"].ap[:], pA, pA.ap[:, 0, :], -1.0, maskLs, maskLs.ap[:], ALU.mult, ALU.add)
                    act(L["decs"], L["decs"].ap[:], L["t1"], L["t1"].ap[:], AF.Exp, rd=[gcol], bias=gcol.ap[:, col:col + 1])
                    if own:
                        tt("dve", L["junk"], L["junk"].ap[:], pA, pA.ap[:, 0, :], maskT, maskT.ap[:], ALU.add)
                        act(L["decT"], L["decT"].ap[:], L["junk"], L["junk"].ap[:], AF.Exp, rd=[ngcol], bias=ngcol.ap[:, col:col + 1])
                        act(L["egrr"], L["egrr"].ap[:], pA, pA.ap[:, 0, :], AF.Exp)
                    yield
                    stt("dve", L["Ma"], L["Ma"].ap[:], pA, pA.ap[:, 1, :], nbeta.ap[:, h:h + 1], L["decs"], L["decs"].ap[:], ALU.mult, ALU.mult, rd=[nbeta])
                    if own:
                        tt("dve", L["PT"], L["PT"].ap[:], pA, pA.ap[:, 2, :], L["decT"], L["decT"].ap[:], ALU.mult)
                        tt("dve", L["qdT"], L["qdT"].ap[:], Pp["qTn"], Pp["qTn"].ap[:], L["egrr"], L["egrr"].ap[:], ALU.mult)
                    act(L["bkg"], L["bkg"].ap[:], Pp["ktok"], Pp["ktok"].ap[:], AF.Copy, rd=[bg], scale=bg.ap[:, h:h + 1])
                    act(L["bv"], L["bv"].ap[:], vtok, vtok.ap[:], AF.Copy, rd=[smc], scale=smc.ap[:, h:h + 1])
                    act(L["kt"], L["kt"].ap[:], Pp["ktok"], Pp["ktok"].ap[:], AF.Copy, rd=[ktl], scale=ktl.ap[:, col:col + 1])
                    yield
                    yield from neumann(L, own)
                    mm(pA, pA.ap[:, 0, :], L["bkg"], L["bkg"].ap[:], L["X"], L["X"].ap[:])
                    P.op("act", lambda e: e.mul(out=L["nwT"].ap[:], in_=pA.ap[:, 0, :], mul=-1.0), reads=[pA], writes=[L["nwT"]])
                    yield
                    mm(pA, pA.ap[:, 1, :], L["X"], L["X"].ap[:], L["bv"], L["bv"].ap[:], start=True, stop=False)
                    mm(pA, pA.ap[:, 1, :], L["nwT"], L["nwT"].ap[:], S, S.ap, start=False, stop=True)
                    act(L["vnew"], L["vnew"].ap[:], pA, pA.ap[:, 1, :], AF.Copy)
                    yield
                    if own:
                        mm(pB, pB.ap[:, 3, :], L["qdT"], L["qdT"].ap[:], S, S.ap, start=True, stop=False)
                        mm(pB, pB.ap[:, 3, :], L["PT"], L["PT"].ap[:], L["vnew"], L["vnew"].ap[:], start=False, stop=True)
                    mm(pA, pA.ap[:, 2, :], L["kt"], L["kt"].ap[:], L["vnew"], L["vnew"].ap[:])
                    stt("dve", S, S.ap, S, S.ap, eglast.ap[:, col:col + 1], pA, pA.ap[:, 2, :], ALU.mult, ALU.add, rd=[eglast])
                    yield
                    if own:
                        zf = Pp[f"zf{e01}"]
                        st = L["st"]
                        act(L["junk"], L["junk"].ap[:], pB, pB.ap[:, 3, :], AF.Square, wr=[st], accum_out=st.ap[:, 0:1])
                        P.op("act", lambda e: e.activation(out=st.ap[:, 1:2], in_=st.ap[:, 0:1], func=AF.Ln, scale=1.0 / 128, bias=1e-6), reads=[L["junk"]], writes=[st])
                        P.op("act", lambda e: e.activation(out=st.ap[:, 2:3], in_=st.ap[:, 1:2], func=AF.Exp, scale=-0.5), reads=[st], writes=[st])
                        tp(pA, pA.ap[:, 0, :], zf, zf.ap[:])
                        tsc("dve", L["on"], L["on"].ap[:], pB, pB.ap[:, 3, :], st.ap[:, 2:3], None, ALU.mult, rd=[st])
                        yield
                        tt("dve", L["on"], L["on"].ap[:], pA, pA.ap[:, 0, :], L["on"], L["on"].ap[:], ALU.mult)
                        tp(pB, pB.ap[:, 1, :], L["on"], L["on"].ap[:])
                        g_ = gost[c % 2]
                        act(g_, g_.ap[:, h, :], pB, pB.ap[:, 1, :], AF.Copy)
                        yield
                return gen

            def ssd_unit(c, g, j, Pp, xt_, smc):
                own = c >= C0
                hs = 8 * g + j
                col = 32 + hs

                def gen(L):
                    pA = pB = L["pA"]
                    S = Ss[hs]
                    v = L["bv"]
                    act(v, v.ap[:, 0:64], xt_, xt_.ap[:, (hs % 2) * 64:(hs % 2) * 64 + 64], AF.Copy, rd=[smc], scale=smc.ap[:, 64 + hs:65 + hs])
                    act(L["kt"], L["kt"].ap[:], Pp["Btok"], Pp["Btok"].ap[:], AF.Copy, rd=[ktl], scale=ktl.ap[:, col:col + 1])
                    if own:
                        mm(pA, pA.ap[:, 0, :], smc, smc.ap[:, 192 + hs:193 + hs].to_broadcast([128, 128]), tri, tri.ap[:])
                        yield
                        tt("dve", L["junk"], L["junk"].ap[:], pA, pA.ap[:, 0, :], maskT, maskT.ap[:], ALU.add)
                        act(L["decT"], L["decT"].ap[:], L["junk"], L["junk"].ap[:], AF.Exp, rd=[ngcol], bias=ngcol.ap[:, col:col + 1])
                        act(L["egrr"], L["egrr"].ap[:], pA, pA.ap[:, 0, :], AF.Exp)
                        yield
                        tt("dve", L["PT"], L["PT"].ap[:], Pp["KQg"], Pp["KQg"].ap[:], L["decT"], L["decT"].ap[:], ALU.mult)
                        tt("dve", L["qdT"], L["qdT"].ap[:], Pp["CTf"], Pp["CTf"].ap[:], L["egrr"], L["egrr"].ap[:], ALU.mult)
                        yield
                        mm(pB, pB.ap[:, 3, 0:64], L["qdT"], L["qdT"].ap[:], S, S.ap, start=True, stop=False)
                        mm(pB, pB.ap[:, 3, 0:64], L["PT"], L["PT"].ap[:], v, v.ap[:, 0:64], start=False, stop=True)
                    mm(pA, pA.ap[:, 2, 0:64], L["kt"], L["kt"].ap[:], v, v.ap[:, 0:64])
                    stt("dve", S, S.ap, S, S.ap, eglast.ap[:, col:col + 1], pA, pA.ap[:, 2, 0:64], ALU.mult, ALU.add, rd=[eglast])
                    yield
                    if own:
                        stt("dve", yg, yg.ap[:, j * 64:(j + 1) * 64], xt_, xt_.ap[:, (hs % 2) * 64:(hs % 2) * 64 + 64], dskip.ap[:, hs:hs + 1],
                            pB, pB.ap[:, 3, 0:64], ALU.mult, ALU.add, rd=[dskip])
                        yield
                return gen

            for c in range(c_lo, c_hi):
                own = c >= C0
                smc = smb[c % 2]
                tsl = slice(c * 128, (c + 1) * 128)
                for q4 in range(4):
                    P.dma("sp", csl_g.ap[:, q4 * 16:(q4 + 1) * 16, :], csv[:, 48 + q4 * 16:48 + (q4 + 1) * 16, tsl], csls[0], writes=[csl_g] if q4 == 0 else [])
                csl_g.w = ("d", csls[0], csls[0].count)
                P.dma("sp", smc.ap[:], smtok[tsl, :], smsl[c % 2], writes=[smc])
                zslab = zsl[c % 2]
                if own:
                    osl_ = slice((c - C0) * 128, (c - C0 + 1) * 128)
                    for q4 in range(4):
                        P.dma("sp", zslab.ap[:, q4 * 16:(q4 + 1) * 16, :], zsv[:, q4 * 16:(q4 + 1) * 16, osl_], zsls[c % 2], writes=[zslab] if q4 == 0 else [])
                    zslab.w = ("d", zsls[c % 2], zsls[c % 2].count)
                for q4 in range(3):
                    P.dma("sp", csl_s.ap[:, q4 * 16:(q4 + 1) * 16, :], csv[:, q4 * 16:(q4 + 1) * 16, tsl], csls[1], writes=[csl_s] if q4 == 0 else [])
                csl_s.w = ("d", csls[1], csls[1].count)
                if c == C0:
                    P.op("dve", lambda e: e.tensor_scalar(out=Sg_t.ap[:], in0=Sg_t.ap[:], scalar1=flag.ap[:, 0:1], scalar2=None, op0=ALU.mult), reads=Sg + [flag], writes=Sg)
                    P.op("pool", lambda e: e.tensor_scalar(out=Ss_t.ap[:], in0=Ss_t.ap[:], scalar1=flag.ap[:, 0:1], scalar2=None, op0=ALU.mult), reads=Ss + [flag], writes=Ss)
                mm(pS, pS.ap[:, 0:32], tri, tri.ap[:], smc, smc.ap[:, 32:64])
                mm(pS, pS.ap[:, 32:96], tri, tri.ap[:], smc, smc.ap[:, 192:256])
                mm(pS, pS.ap[:, 128:160], ones, ones.ap[:], smc, smc.ap[:, 32:64])
                mm(pS, pS.ap[:, 160:224], ones, ones.ap[:], smc, smc.ap[:, 192:256])
                cp("dve", gcol, gcol.ap[:], pS, pS.ap[:, 0:96])
                cp("dve", glast, glast.ap[:], pS, pS.ap[:, 128:224])
                tsc("dve", ngcol, ngcol.ap[:], gcol, gcol.ap[:], -1.0, None, ALU.mult)
                tt("dve", ktl, ktl.ap[:], glast, glast.ap[:], gcol, gcol.ap[:], ALU.subtract)
                act(ktl, ktl.ap[:], ktl, ktl.ap[:], AF.Exp)
                act(eglast, eglast.ap[:], glast, glast.ap[:], AF.Exp)
                act(bg, bg.ap[:], gcol, gcol.ap[:, 0:32], AF.Exp)
                tt("dve", bg, bg.ap[:], bg, bg.ap[:], smc, smc.ap[:, 0:32], ALU.mult)
                tsc("dve", nbeta, nbeta.ap[:], smc, smc.ap[:, 0:32], -1.0, None, ALU.mult)

                def gdn_prep(hq, Pp):
                    st = Pp["st"]
                    cp("pool", Pp["kTf"], Pp["kTf"].ap[:], csl_g, csl_g.ap[:, 16 + hq, :])
                    tp(pP, pP.ap[:, 0, :], Pp["kTf"], Pp["kTf"].ap[:])
                    act(Pp["junk"], Pp["junk"].ap[:], pP, pP.ap[:, 0, :], AF.Square, wr=[st], accum_out=st.ap[:, 0:1])
                    P.op("act", lambda e: e.activation(out=st.ap[:, 1:2], in_=st.ap[:, 0:1], func=AF.Ln, bias=1e-6), reads=[Pp["junk"]], writes=[st])
                    P.op("act", lambda e: e.activation(out=st.ap[:, 2:3], in_=st.ap[:, 1:2], func=AF.Exp, scale=-0.5), reads=[st], writes=[st])
                    act(Pp["ktok"], Pp["ktok"].ap[:], pP, pP.ap[:, 0, :], AF.Copy, rd=[st], scale=st.ap[:, 2:3])
                    tp(pP, pP.ap[:, 1, :], Pp["ktok"], Pp["ktok"].ap[:])
                    cp("dve", Pp["kTn"], Pp["kTn"].ap[:], pP, pP.ap[:, 1, :])
                    if own:
                        cp("pool", Pp["qTf"], Pp["qTf"].ap[:], csl_g, csl_g.ap[:, hq, :])
                        tp(pP, pP.ap[:, 2, :], Pp["qTf"], Pp["qTf"].ap[:])
                        act(Pp["junk"], Pp["junk"].ap[:], pP, pP.ap[:, 2, :], AF.Square, wr=[st], accum_out=st.ap[:, 4:5])
                        P.op("act", lambda e: e.activation(out=st.ap[:, 5:6], in_=st.ap[:, 4:5], func=AF.Ln, bias=1e-6), reads=[Pp["junk"]], writes=[st])
                        P.op("act", lambda e: e.activation(out=st.ap[:, 6:7], in_=st.ap[:, 5:6], func=AF.Exp, scale=-0.5, bias=-2.4260151319598084), reads=[st], writes=[st])
                        act(Pp["qtok"], Pp["qtok"].ap[:], pP, pP.ap[:, 2, :], AF.Copy, rd=[st], scale=st.ap[:, 6:7])
                        tp(pP, pP.ap[:, 3, :], Pp["qtok"], Pp["qtok"].ap[:])
                        cp("dve", Pp["qTn"], Pp["qTn"].ap[:], pP, pP.ap[:, 3, :])
                    for e01 in range(2):
                        h = 2 * hq + e01
                        vT, vt = Pp[f"vTf{e01}"], Pp[f"vtok{e01}"]
                        cp("pool", vT, vT.ap[:], csl_g, csl_g.ap[:, 32 + h, :])
                        tp(pP, pP.ap[:, e01, :], vT, vT.ap[:])
                        cp("dve", vt, vt.ap[:], pP, pP.ap[:, e01, :])
                        if own:
                            zf = Pp[f"zf{e01}"]
                            cp("pool", zf, zf.ap[:], zslab, zslab.ap[:, 32 + h, :])

                if do_gdn:
                    for hq0 in range(0, 16, 3):
                        gens = []
                        for i3, hq in enumerate(range(hq0, min(16, hq0 + 3))):
                            gdn_prep(hq, prep[i3])
                            gens += [gdn_unit(c, 2 * hq + e01, prep[i3], e01, smc, zslab) for e01 in range(2)]
                        run_lanes(gens)
                    if own:
                        P.dma("act", gov[:, :, (c - C0) * 128:(c - C0 + 1) * 128], gost[c % 2].ap[:], gosl[c % 2], reads=[gost[c % 2]])

                for g in range(8 if do_ssd else 0):
                    Pp = prep[g % 3]
                    cp("pool", Pp["BTf"], Pp["BTf"].ap[:], csl_s, csl_s.ap[:, 32 + g, :])
                    tp(pP, pP.ap[:, 0, :], Pp["BTf"], Pp["BTf"].ap[:])
                    cp("dve", Pp["Btok"], Pp["Btok"].ap[:], pP, pP.ap[:, 0, :])
                    if own:
                        cp("pool", Pp["CTf"], Pp["CTf"].ap[:], csl_s, csl_s.ap[:, 40 + g, :])
                        mm(pP, pP.ap[:, 1, :], Pp["BTf"], Pp["BTf"].ap[:], Pp["CTf"], Pp["CTf"].ap[:])
                        act(Pp["KQg"], Pp["KQg"].ap[:], pP, pP.ap[:, 1, :], AF.Copy)
                    gens = []
                    for j2 in range(4):
                        xT_, xt_ = xTf[j2 % 2], xtk[j2]
                        cp("pool", xT_, xT_.ap[:], csl_s, csl_s.ap[:, 4 * g + j2, :])
                        tp(pP, pP.ap[:, 2 + (j2 % 2), :], xT_, xT_.ap[:])
                        cp("dve", xt_, xt_.ap[:], pP, pP.ap[:, 2 + (j2 % 2), :])
                        gens += [ssd_unit(c, g, 2 * j2 + e01, Pp, xt_, smc) for e01 in range(2)]
                    run_lanes(gens[0:4])
                    run_lanes(gens[4:8])
                    if own:
                        P.op("pool", lambda e, g=g, zslab=zslab: e.tensor_copy(out=zf4.ap[:], in_=zslab.ap[:, 4 * g:4 * g + 4, :]), reads=[zslab], writes=[zf4])
                        for i4 in range(4):
                            tp(pP, pP.ap[:, i4, :], zf4, zf4.ap[:, i4, :])
                        tt("dve", yz, yz.ap[:], pP, pP.ap[:].rearrange("p a b -> p (a b)"), yg, yg.ap[:], ALU.mult)
                        act(yg, yg.ap[:], yz, yz.ap[:], AF.Square, wr=[gst], accum_out=gst.ap[:, 0:1])
                        P.op("act", lambda e: e.activation(out=gst.ap[:, 1:2], in_=gst.ap[:, 0:1], func=AF.Ln, scale=1.0 / 512, bias=1e-6), reads=[yg], writes=[gst])
                        P.op("act", lambda e: e.activation(out=gst.ap[:, 2:3], in_=gst.ap[:, 1:2], func=AF.Exp, scale=-0.5), reads=[gst], writes=[gst])
                        tsc("dve", yz, yz.ap[:], yz, yz.ap[:], gst.ap[:, 2:3], None, ALU.mult, rd=[gst])
                        for i4 in range(4):
                            tp(pP, pP.ap[:, i4, :], yz, yz.ap[:, i4 * 128:(i4 + 1) * 128])
                        s_ = sost[c % 2]
                        act(s_, s_.ap[:, 4 * g:4 * g + 4, :], pP, pP.ap[:], AF.Copy)
                if own and do_ssd:
                    P.dma("act", sov[:, :, (c - C0) * 128:(c - C0 + 1) * 128], sost[c % 2].ap[:], sosl[c % 2], reads=[sost[c % 2]])
            P.replay()

        if upto >= 4:
          with ExitStack() as es:
            P = Prog(nc, es, "p4")
            big = sb(nc, es, "big", [128, 64, 512], BF16)
            xT = sb(nc, es, "xT", [128, 16, 512], F32)
            mT = sb(nc, es, "mT", [128, 16, 512], BF16)
            gt = [sb(nc, es, f"gt{i}", [128, 512], BF16) for i in range(4)]
            gsl = [P.slot() for _ in range(4)]
            tmpa = sb(nc, es, "tmpa", [128, 512], F32)
            tmpb = sb(nc, es, "tmpb", [128, 512], F32)
            rrr = sb(nc, es, "rrr", [128, 512], F32)
            xt4 = [sb(nc, es, f"x4t{i}", [128, D], F32) for i in range(2)]
            xsl = [P.slot() for _ in range(2)]
            st4 = [sb(nc, es, f"st4_{i}", [128, 4], F32) for i in range(2)]
            sq4 = sb(nc, es, "sq4", [128, D], F32)
            ysl = [P.slot() for _ in range(2)]
            bsl = P.slot()
            pm = [ps(nc, es, f"p4m{i}", [128, 512]) for i in range(4)]
            pt = [ps(nc, es, f"p4t{i}", [128, 4, 128]) for i in range(4)]
            wcount = [0]
            pcount = [0]

            NWB = 6
            wbf = [sb(nc, es, f"w4b{i}", [128, 16, 128], BF16) for i in range(NWB)]
            wbsl = [P.slot() for _ in range(NWB)]
            jobctr = [0]

            def gemm(wview, KC, rhs_fn, rhs_bufs, scale=None):
                pb = pm[pcount[0] % 4]
                pcount[0] += 1
                for g0 in range(0, KC, 16):
                    n = min(16, KC - g0)
                    jid = jobctr[0] % len(JOBS)
                    jobctr[0] += 1
                    assert JOBS[jid][2] == g0 and JOBS[jid][3] == n and JOBS[jid][4] == scale, (jid, JOBS[jid][1:], g0, n, scale)
                    i = wcount[0] % NWB
                    wcount[0] += 1
                    wb = wbf[i]
                    P.dma("sp", wb.ap[:, 0:n, :].rearrange("p a b -> p (a b)"), wq[jid, :, 0:n * 128], wbsl[i], writes=[wb])
                    for k in range(n):
                        kc = g0 + k
                        P.op("pe", lambda e, pb=pb, wb=wb, k=k, kc=kc: e.matmul(pb.ap[:], lhsT=wb.ap[:, k, :], rhs=rhs_fn(kc), start=(kc == 0), stop=(kc == KC - 1)),
                             reads=[wb] + rhs_bufs, writes=[pb])
                return pb

            def wv(w, f0):
                v = w.rearrange("(kc p) f -> p kc f", p=128)
                return lambda k0, n: v[:, k0:k0 + n, f0:f0 + 128]

            sov = so.rearrange("(kc p) t -> p kc t", p=128)
            gov = go.rearrange("(kc p) t -> p kc t", p=128)
            for tb in range(4):
                ts_ = slice(tb * 512, (tb + 1) * 512)
                for half in range(4):
                    P.dma("sp", big.ap[:, half * 8:(half + 1) * 8, :], sov[:, half * 8:(half + 1) * 8, ts_], bsl, writes=[big] if half == 0 else [])
                for half in range(4):
                    P.dma("sp", big.ap[:, 32 + half * 8:32 + (half + 1) * 8, :], gov[:, half * 8:(half + 1) * 8, ts_], bsl)
                big.w = ("d", bsl, bsl.count)
                for tl in range(4):
                    ti = tb * 4 + tl
                    xb = xt4[ti % 2]
                    P.dma("sp", xb.ap[:, 0:1024], xin[TP + ti * 128:TP + (ti + 1) * 128, 0:1024], xsl[ti % 2], writes=[xb])
                    P.dma("sp", xb.ap[:, 1024:2048], xin[TP + ti * 128:TP + (ti + 1) * 128, 1024:2048], xsl[ti % 2])
                    xb.w = ("d", xsl[ti % 2], xsl[ti % 2].count)
                    for q in range(4):
                        pb = pt[q]
                        for j in range(4):
                            kc = q * 4 + j
                            P.op("pe", lambda e, pb=pb, j=j, kc=kc, xb=xb: e.transpose(pb.ap[:, j, :], xb.ap[:, kc * 128:(kc + 1) * 128], ident.ap[:]),
                                 reads=[xb, ident], writes=[pb])
                        eng = "act" if q % 2 == 0 else "dve"
                        if eng == "act":
                            P.op("act", lambda e, pb=pb, q=q, tl=tl: e.activation(out=xT.ap[:, q * 4:(q + 1) * 4, tl * 128:(tl + 1) * 128], in_=pb.ap[:], func=AF.Copy),
                                 reads=[pb], writes=[xT])
                        else:
                            P.op("dve", lambda e, pb=pb, q=q, tl=tl: e.tensor_copy(out=xT.ap[:, q * 4:(q + 1) * 4, tl * 128:(tl + 1) * 128], in_=pb.ap[:]),
                                 reads=[pb], writes=[xT])
                for f in range(16):
                    ga, gb = gt[(2 * f) % 4], gt[(2 * f + 1) % 4]
                    P.dma("sp", ga.ap[:], zs[8192 + f * 128:8192 + (f + 1) * 128, ts_], gsl[(2 * f) % 4], writes=[ga])
                    P.dma("sp", gb.ap[:], zs[10240 + f * 128:10240 + (f + 1) * 128, ts_], gsl[(2 * f + 1) % 4], writes=[gb])
                    pa = gemm(wv(w_ssm, f * 128), 32, lambda kc: big.ap[:, kc, :], [big], scale="ssm")
                    P.op("dve", lambda e, pa=pa, ga=ga: e.tensor_tensor(out=tmpa.ap[:], in0=pa.ap[:], in1=ga.ap[:], op=ALU.mult), reads=[pa, ga], writes=[tmpa])
                    pb_ = gemm(wv(w_gdn, f * 128), 32, lambda kc: big.ap[:, 32 + kc, :], [big], scale="gdn")
                    P.op("dve", lambda e, pb_=pb_, gb=gb: e.tensor_tensor(out=tmpb.ap[:], in0=pb_.ap[:], in1=gb.ap[:], op=ALU.mult), reads=[pb_, gb], writes=[tmpb])
                    P.op("pool", lambda e, f=f: e.tensor_tensor(out=mT.ap[:, f, :], in0=tmpa.ap[:], in1=tmpb.ap[:], op=ALU.add), reads=[tmpa, tmpb], writes=[mT])
                for f in range(16):
                    pc = gemm(wv(w_o, f * 128), 16, lambda kc: mT.ap[:, kc, :], [mT])
                    P.op("dve", lambda e, pc=pc, f=f: e.scalar_tensor_tensor(out=xT.ap[:, f, :], in0=pc.ap[:], scalar=ada.ap[:, 32 + f:33 + f], in1=xT.ap[:, f, :],
                                                                           op0=ALU.mult, op1=ALU.add), reads=[pc, xT, ada], writes=[xT])

                def rstd_rows():
                    pr = pm[pcount[0] % 4]
                    pcount[0] += 1
                    for kc in range(16):
                        P.op("act", lambda e, kc=kc: e.activation(out=tmpa.ap[:], in_=xT.ap[:, kc, :], func=AF.Square), reads=[xT], writes=[tmpa])
                        P.op("pe", lambda e, pr=pr, kc=kc: e.matmul(pr.ap[:], lhsT=ones.ap[:], rhs=tmpa.ap[:], start=(kc == 0), stop=(kc == 15)),
                             reads=[tmpa, ones], writes=[pr])
                    P.op("act", lambda e, pr=pr: e.activation(out=rrr.ap[:], in_=pr.ap[:], func=AF.Ln, scale=1.0 / D, bias=1e-6), reads=[pr], writes=[rrr])
                    P.op("act", lambda e: e.activation(out=rrr.ap[:], in_=rrr.ap[:], func=AF.Exp, scale=-0.5), reads=[rrr], writes=[rrr])

                rstd_rows()
                for kc in range(16):
                    P.op("dve", lambda e, kc=kc: e.tensor_tensor(out=tmpb.ap[:], in0=xT.ap[:, kc, :], in1=rrr.ap[:], op=ALU.mult), reads=[xT, rrr], writes=[tmpb])
                    P.op("dve", lambda e, kc=kc: e.tensor_scalar(out=mT.ap[:, kc, :], in0=tmpb.ap[:], scalar1=A2.ap[:, kc:kc + 1], scalar2=ada.ap[:, 48 + kc:49 + kc],
                                                                op0=ALU.mult, op1=ALU.add), reads=[tmpb, A2, ada], writes=[mT])
                for j in range(44):
                    pg = gemm(wv(w_gu, j * 128), 16, lambda kc: mT.ap[:, kc, :], [mT])
                    P.op("act", lambda e, pg=pg: e.activation(out=tmpa.ap[:], in_=pg.ap[:], func=AF.Silu), reads=[pg], writes=[tmpa])
                    pu = gemm(wv(w_gu, FFN + j * 128), 16, lambda kc: mT.ap[:, kc, :], [mT])
                    P.op("dve", lambda e, pu=pu, j=j: e.tensor_tensor(out=big.ap[:, j, :], in0=pu.ap[:], in1=tmpa.ap[:], op=ALU.mult), reads=[pu, tmpa], writes=[big])
                for f in range(16):
                    pd = gemm(wv(w_dn, f * 128), 44, lambda kc: big.ap[:, kc, :], [big])
                    P.op("dve", lambda e, pd=pd, f=f: e.scalar_tensor_tensor(out=xT.ap[:, f, :], in0=pd.ap[:], scalar=ada.ap[:, 80 + f:81 + f], in1=xT.ap[:, f, :],
                                                                           op0=ALU.mult, op1=ALU.add), reads=[pd, xT, ada], writes=[xT])
                for tl in range(4):
                    ti = tb * 4 + tl
                    xb = xt4[ti % 2]
                    s1 = st4[ti % 2]
                    for q in range(4):
                        pb = pt[q]
                        for j in range(4):
                            kc = q * 4 + j
                            P.op("pe", lambda e, pb=pb, j=j, kc=kc, tl=tl: e.transpose(pb.ap[:, j, :], xT.ap[:, kc, tl * 128:(tl + 1) * 128], ident.ap[:]),
                                 reads=[xT, ident], writes=[pb])
                        eng = "act" if q % 2 == 0 else "dve"
                        if eng == "act":
                            P.op("act", lambda e, pb=pb, q=q, xb=xb: e.activation(out=xb.ap[:, q * 512:(q + 1) * 512], in_=pb.ap[:].rearrange("p a b -> p (a b)"), func=AF.Copy),
                                 reads=[pb], writes=[xb])
                        else:
                            P.op("dve", lambda e, pb=pb, q=q, xb=xb: e.tensor_copy(out=xb.ap[:, q * 512:(q + 1) * 512], in_=pb.ap[:].rearrange("p a b -> p (a b)")),
                                 reads=[pb], writes=[xb])
                    P.op("act", lambda e, xb=xb, s1=s1: e.activation(out=sq4.ap[:], in_=xb.ap[:], func=AF.Square, accum_out=s1.ap[:, 0:1]), reads=[xb], writes=[sq4, s1])
                    P.op("act", lambda e, s1=s1: e.activation(out=s1.ap[:, 1:2], in_=s1.ap[:, 0:1], func=AF.Ln, scale=1.0 / D, bias=1e-6), reads=[s1], writes=[s1])
                    P.op("act", lambda e, s1=s1: e.activation(out=s1.ap[:, 2:3], in_=s1.ap[:, 1:2], func=AF.Exp, scale=-0.5), reads=[s1], writes=[s1])
                    P.op("dve", lambda e, xb=xb, s1=s1: e.scalar_tensor_tensor(out=xb.ap[:], in0=xb.ap[:], scalar=s1.ap[:, 2:3], in1=fnw_row.ap[:], op0=ALU.mult, op1=ALU.mult),
                         reads=[xb, s1, fnw_row], writes=[xb])
                    P.dma("sp", y_out[ti * 128:(ti + 1) * 128, :], xb.ap[:], ysl[ti % 2], reads=[xb])
            P.replay()

        if dbg:
            with ExitStack() as es:
                P = Prog(nc, es, "dbg")
                dsl = P.slot()
                d_cs = nc.dram_tensor("dbg_cs", [14336, 256], BF16, kind="ExternalOutput").ap()
                d_sm = nc.dram_tensor("dbg_sm", [TT, 256], F32, kind="ExternalOutput").ap()
                d_zs = nc.dram_tensor("dbg_zs", [12288, 128], BF16, kind="ExternalOutput").ap()
                d_ada = nc.dram_tensor("dbg_ada", [128, 96], F32, kind="ExternalOutput").ap()
                for t in range(14):
                    P.dma("sp", d_cs[t * 1024:(t + 1) * 1024, :], cs[t * 1024:(t + 1) * 1024, 1920:2176], dsl)
                for t in range(4):
                    P.dma("sp", d_sm[t * 1024:(t + 1) * 1024, :], smtok[t * 1024:(t + 1) * 1024, :], dsl)
                for t in range(12):
                    P.dma("sp", d_zs[t * 1024:(t + 1) * 1024, :], zs[t * 1024:(t + 1) * 1024, 0:128], dsl)
                P.dma("sp", d_ada, ada.ap[:], dsl)
                d_so = nc.dram_tensor("dbg_so", [4096, TQ], BF16, kind="ExternalOutput").ap()
                d_go = nc.dram_tensor("dbg_go", [4096, TQ], BF16, kind="ExternalOutput").ap()
                for t in range(4):
                    P.dma("sp", d_so[t * 1024:(t + 1) * 1024, :], so[t * 1024:(t + 1) * 1024, :], dsl)
                    P.dma("sp", d_go[t * 1024:(t + 1) * 1024, :], go[t * 1024:(t + 1) * 1024, :], dsl)
                P.replay()
    return nc


def _layout_inputs(inp):
    f = lambda a: np.ascontiguousarray(a, dtype=np.float32)
    col = lambda v, n: f(np.asarray(v).reshape(n, 128).T)
    x, c = inp["x"], inp["c"]
    cwS = np.asarray(inp["ssm_conv_w"][0])
    cwG = np.asarray(inp["gdn_conv_w"][0])
    cw_all = np.concatenate([cwS, cwG], axis=1)
    cw_l = f(cw_all.reshape(4, 112, 128).transpose(2, 1, 0))
    cb_all = np.concatenate([np.asarray(inp["ssm_conv_b"][0]), np.zeros(8192, np.float32)])
    cb_l = col(cb_all, 112)
    z32 = np.zeros(32, np.float32)
    smallbias = f(np.concatenate([z32, inp["gdn_dt_bias"][0], inp["ssm_dt_bias"][0]]).reshape(128, 1))
    alog = f(np.concatenate([z32, inp["gdn_a_log"][0], inp["ssm_a_log"][0]]).reshape(128, 1))
    idx = np.arange(128)
    common = {
        "w_ada": f(inp["w_ada"][0]), "b_ada_col": col(inp["b_ada"][0], 96),
        "nmw": col(inp["norm_mix_w"][0], 16), "nfw": col(inp["norm_ffn_w"][0], 16),
        "fnw_row": f(np.broadcast_to(np.asarray(inp["final_norm_w"])[None, :], (128, D))),
        "w_in": f(inp["w_in"][0]), "cw": cw_l, "cb": cb_l, "smallbias": smallbias, "alog": alog,
        "dskip_row": f(np.broadcast_to(np.asarray(inp["ssm_d_skip"][0])[None, :], (128, 64))),
        "ssmnw_col": col(inp["ssm_norm_w"][0], 32), "gdnnw_col": f(np.asarray(inp["gdn_norm_w"][0]).reshape(128, 1)),
        "w_ssm": f(inp["w_ssm_proj"][0]), "w_gdn": f(inp["w_gdn_proj"][0]), "w_o": f(inp["w_o"][0]),
        "w_gu": f(inp["w_gate_up"][0]), "w_dn": f(inp["w_down"][0]),
        "ident": np.eye(128, dtype=np.float32),
        "tri": (idx[:, None] <= idx[None, :]).astype(np.float32),
        "ones": np.ones((128, 128), np.float32),
        "maskT": np.where(idx[None, :] >= idx[:, None], 0.0, NEG).astype(np.float32),
        "maskLs": np.where(idx[None, :] < idx[:, None], 0.0, NEG).astype(np.float32),
    }
    maps = []
    for core in range(8):
        b, r = core // 2, core % 2
        xb = np.asarray(x[b])
        own = xb[r * TQ:(r + 1) * TQ]
        pre = xb[0:TP]
        m = dict(common)
        m["xin"] = f(np.concatenate([pre, own], axis=0))
        m["flag"] = np.full((128, 1), float(r), np.float32)
        m["c_col"] = col(c[b], 16)
        maps.append(m)
    return maps


_NC_CACHE = {}


def kernel(**inputs):
    inp = {k: np.asarray(v) for k, v in inputs.items()}
    if "nc" not in _NC_CACHE:
        _NC_CACHE["nc"] = build_program()
    nc = _NC_CACHE["nc"]
    maps = _layout_inputs(inp)
    res = run_bass_kernel_spmd(nc, maps, core_ids=list(range(8)))
    out = np.zeros((4, SEQ, D), np.float32)
    for core in range(8):
        b, r = core // 2, core % 2
        out[b, r * TQ:(r + 1) * TQ] = res.results[core]["y"]
    return out
```
